# Optimizing a Trainium2 kernel written in Bass

```python
import math
import jax, jax.numpy as jnp
from jax import lax
import numpy as np

D_MODEL = 1024
BATCH = 8
SEQ = 2048
DEPTH = 2
DEC_BATCH = 128
DEC_SEQ = 4
PAST_LEN = 2048
PAGE_SIZE = 128

D_CONV = 512
CONV_WIDTH = 31
D_SSM = 512
SSM_GROUP = 16
N_SSM_GROUPS = D_SSM // SSM_GROUP
SSM_STATE = 64
DT_MIN = 0.001
DT_MAX = 0.1
HEAD_DIM = 64
HEADS_PER_GROUP = 4
ATT_PATTERNS = ((128, 1), (512, 4), (2048, 16))
N_ATT_HEADS = HEADS_PER_GROUP * len(ATT_PATTERNS)
D_ATT = N_ATT_HEADS * HEAD_DIM
D_ATT_OUT = HEADS_PER_GROUP * HEAD_DIM
ATT_SCALE = HEAD_DIM ** -0.5
D_FF = -(-8 * D_MODEL // (3 * 256)) * 256
N_IN = 2 * D_CONV + D_SSM + 3 * D_ATT + 3 * D_MODEL
RMS_EPS = 1e-6
LN_EPS = 1e-5
NEG = -1e30

kernel_name = 'hybrid_conv_s5_dilated_attn_decoder_step'


def rmsnorm(x, g):
    xf = x.astype(jnp.float32)
    y = xf * lax.rsqrt(jnp.mean(xf * xf, axis=-1, keepdims=True) + RMS_EPS)
    return (y * g.astype(jnp.float32)).astype(x.dtype)


def layernorm(x, g, b):
    xf = x.astype(jnp.float32)
    xc = xf - jnp.mean(xf, axis=-1, keepdims=True)
    y = xc * lax.rsqrt(jnp.mean(xc * xc, axis=-1, keepdims=True) + LN_EPS)
    return (y * g.astype(jnp.float32) + b.astype(jnp.float32)).astype(x.dtype)


def alibi_slopes():
    return jnp.exp2(-8.0 * jnp.arange(1, N_ATT_HEADS + 1, dtype=jnp.float32) / N_ATT_HEADS)


def conv_branch(a, prev, w_dw, b_dw, ln_g, ln_b, w_proj):
    u = a[..., :D_CONV] * jax.nn.sigmoid(a[..., D_CONV:])
    ucat = jnp.concatenate([prev.astype(u.dtype), u], axis=1)
    y = lax.conv_general_dilated(ucat, w_dw[:, None, :].astype(u.dtype), (1,), 'VALID',
                                 dimension_numbers=('NWC', 'WIO', 'NWC'),
                                 feature_group_count=D_CONV)
    y = jax.nn.silu(layernorm(y + b_dw, ln_g, ln_b))
    return y @ w_proj, ucat[:, ucat.shape[1] - (CONV_WIDTH - 1):]


def _complex_scan_combine(e1, e2):
    a1r, a1i, b1r, b1i = e1
    a2r, a2i, b2r, b2i = e2
    return (a2r * a1r - a2i * a1i, a2r * a1i + a2i * a1r,
            a2r * b1r - a2i * b1i + b2r, a2r * b1i + a2i * b1r + b2i)


def ssm_branch(u, h0_re, h0_im, a_re, a_im, log_dt, b_re, b_im, c_re, c_im, d_skip, w_glu):
    f32 = jnp.float32
    nb, T, _ = u.shape
    uf = u.astype(f32).reshape(nb, T, N_SSM_GROUPS, SSM_GROUP)
    ar, ai = a_re.astype(f32), a_im.astype(f32)
    dt = jnp.exp(log_dt.astype(f32))[:, None]
    mag = jnp.exp(dt * ar)
    abar_r, abar_i = mag * jnp.cos(dt * ai), mag * jnp.sin(dt * ai)
    den = ar * ar + ai * ai
    fr = ((abar_r - 1.0) * ar + abar_i * ai) / den
    fi = (abar_i * ar - (abar_r - 1.0) * ai) / den
    br, bi = b_re.astype(f32), b_im.astype(f32)
    bbr = fr[..., None] * br - fi[..., None] * bi
    bbi = fr[..., None] * bi + fi[..., None] * br
    bur = jnp.einsum('btgc,gpc->btgp', uf, bbr)
    bui = jnp.einsum('btgc,gpc->btgp', uf, bbi)
    shp = bur.shape
    acr, aci, hr, hi = lax.associative_scan(
        _complex_scan_combine,
        (jnp.broadcast_to(abar_r, shp), jnp.broadcast_to(abar_i, shp), bur, bui), axis=1)
    h0r = h0_re.astype(f32)[:, None]
    h0i = h0_im.astype(f32)[:, None]
    hr = hr + acr * h0r - aci * h0i
    hi = hi + acr * h0i + aci * h0r
    y = (jnp.einsum('btgp,gcp->btgc', hr, c_re.astype(f32))
         - jnp.einsum('btgp,gcp->btgc', hi, c_im.astype(f32))
         + d_skip.astype(f32) * uf)
    z = jax.nn.gelu(y.reshape(nb, T, D_SSM)).astype(u.dtype)
    g = z @ w_glu
    out = g[..., :D_MODEL] * jax.nn.sigmoid(g[..., D_MODEL:])
    return out, hr[:, -1], hi[:, -1]


def dilated_attention_prompt(q, k, v, window, dil, slopes):
    f32 = jnp.float32
    nb_, S, H, dh = q.shape
    R = window // dil
    Ls = S // dil
    nblk = -(-Ls // R)
    pad_end = nblk * R - Ls

    def by_residue(t):
        return t.astype(f32).reshape(nb_, Ls, dil, H, dh).transpose(0, 2, 1, 3, 4)

    qb = jnp.pad(by_residue(q), ((0, 0), (0, 0), (0, pad_end), (0, 0), (0, 0)))
    qb = qb.reshape(nb_, dil, nblk, R, H, dh)

    def key_blocks(t):
        ts = jnp.pad(by_residue(t), ((0, 0), (0, 0), (R, pad_end), (0, 0), (0, 0)))
        ts = ts.reshape(nb_, dil, nblk + 1, R, H, dh)
        return jnp.concatenate([ts[:, :, :-1], ts[:, :, 1:]], axis=3)

    kb, vb = key_blocks(k), key_blocks(v)
    s = jnp.einsum('brnqhc,brnkhc->brnhqk', qb, kb) * ATT_SCALE
    qi = jnp.arange(R)[:, None]
    ki = jnp.arange(2 * R)[None, :]
    dist = qi + R - ki
    kpos = jnp.arange(nblk)[:, None, None] * R + ki[None] - R
    valid = (dist >= 0) & (dist <= R) & (kpos >= 0)
    bias = -slopes[:, None, None] * (dil * dist).astype(f32)[None]
    s = jnp.where(valid[None, None, :, None], s + bias[None, None, None], NEG)
    mx = jnp.max(s, axis=-1, keepdims=True)
    p = jnp.exp(s - mx)
    den = jnp.sum(p, axis=-1)
    o = jnp.einsum('brnhqk,brnkhc->brnqhc', p, vb) / den.transpose(0, 1, 2, 4, 3)[..., None]
    lse = (mx[..., 0] + jnp.log(den)).transpose(0, 1, 2, 4, 3)
    o = o.reshape(nb_, dil, nblk * R, H, dh)[:, :, :Ls].transpose(0, 2, 1, 3, 4).reshape(nb_, S, H, dh)
    lse = lse.reshape(nb_, dil, nblk * R, H)[:, :, :Ls].transpose(0, 2, 1, 3).reshape(nb_, S, H)
    return o, lse


def dilated_attention_sample(q, k_new, v_new, k_buf, v_buf, window, dil, slopes):
    f32 = jnp.float32
    L = k_buf.shape[1]
    T = q.shape[1]
    R = window // dil
    kcat = jnp.concatenate([k_buf.astype(f32), k_new.astype(f32)], axis=1)
    vcat = jnp.concatenate([v_buf.astype(f32), v_new.astype(f32)], axis=1)
    j = jnp.arange(T)[:, None]
    kk = jnp.arange(R + 1)[None, :]
    idx = L + j - kk * dil
    valid = idx >= 0
    idx = jnp.maximum(idx, 0)
    kg = kcat[:, idx]
    vg = vcat[:, idx]
    s = jnp.einsum('bthc,btkhc->bhtk', q.astype(f32), kg) * ATT_SCALE
    bias = -slopes[:, None, None] * (kk * dil).astype(f32)[None]
    s = jnp.where(valid[None, None], s + bias[None], NEG)
    mx = jnp.max(s, axis=-1, keepdims=True)
    p = jnp.exp(s - mx)
    den = jnp.sum(p, axis=-1)
    o = jnp.einsum('bhtk,btkhc->bthc', p, vg) / den.transpose(0, 2, 1)[..., None]
    lse = (mx[..., 0] + jnp.log(den)).transpose(0, 2, 1)
    return o, lse


def trunk_layer(x, c, past, p, is_prompt):
    conv_prev, h0_re, h0_im, k_bufs, v_bufs = past
    nb, T, _ = x.shape
    mod = (jax.nn.silu(c) @ p['w_mod'] + p['b_mod'])[:, None, :]
    sh_m, sc_m, g_m, sh_f, sc_f, g_f = jnp.split(mod, 6, axis=-1)
    h = rmsnorm(x, p['g_pre_mix']) * (1.0 + sc_m) + sh_m
    z = h @ p['w_in']
    o1 = 2 * D_CONV
    o2 = o1 + D_SSM
    o3 = o2 + D_ATT
    o4 = o3 + D_ATT
    o5 = o4 + D_ATT
    a_conv, u_ssm = z[..., :o1], z[..., o1:o2]
    q, k, v, gates = z[..., o2:o3], z[..., o3:o4], z[..., o4:o5], z[..., o5:]
    conv_out, conv_state = conv_branch(a_conv, conv_prev, p['conv_w'], p['conv_b'],
                                       p['conv_ln_g'], p['conv_ln_b'], p['w_conv_out'])
    ssm_out, s_re, s_im = ssm_branch(u_ssm, h0_re, h0_im, p['ssm_a_re'], p['ssm_a_im'],
                                     p['ssm_log_dt'], p['ssm_b_re'], p['ssm_b_im'],
                                     p['ssm_c_re'], p['ssm_c_im'], p['ssm_d'], p['w_ssm_glu'])
    hshape = (nb, T, N_ATT_HEADS, HEAD_DIM)
    q, k, v = q.reshape(hshape), k.reshape(hshape), v.reshape(hshape)
    slopes = alibi_slopes()
    outs, lses, kv_new = [], [], []
    for g, (window, dil) in enumerate(ATT_PATTERNS):
        sl = slice(g * HEADS_PER_GROUP, (g + 1) * HEADS_PER_GROUP)
        qg, kg, vg = q[:, :, sl], k[:, :, sl], v[:, :, sl]
        if is_prompt:
            o, lse = dilated_attention_prompt(qg, kg, vg, window, dil, slopes[sl])
            keep = min(window, T)
            kv_new += [kg[:, T - keep:], vg[:, T - keep:]]
        else:
            o, lse = dilated_attention_sample(qg, kg, vg, k_bufs[g], v_bufs[g], window, dil, slopes[sl])
            kv_new += [kg, vg]
        outs.append(o)
        lses.append(lse)
    alpha = jax.nn.softmax(jnp.stack(lses), axis=0)
    att = jnp.sum(alpha[..., None] * jnp.stack(outs), axis=0).reshape(nb, T, D_ATT_OUT).astype(x.dtype)
    att_out = att @ p['w_att']
    g_a, g_b, g_c = jnp.split(gates, 3, axis=-1)
    merged = (jax.nn.sigmoid(g_a) * conv_out + jax.nn.sigmoid(g_b) * ssm_out
              + jax.nn.sigmoid(g_c) * att_out)
    x = x + g_m * rmsnorm(merged @ p['w_out'], p['g_post_mix'])
    hf = rmsnorm(x, p['g_pre_ffn']) * (1.0 + sc_f) + sh_f
    gu = hf @ p['w_ffn_in']
    f = jax.nn.silu(gu[..., :D_FF]) * gu[..., D_FF:]
    x = x + g_f * rmsnorm(f @ p['w_ffn_out'], p['g_post_ffn'])
    return x, (*kv_new, conv_state, s_re.astype(x.dtype), s_im.astype(x.dtype))


def setup_inputs(seed: int = 0) -> dict:
    key = jax.random.key(seed)
    ks = iter(jax.random.split(key, 64))

    def nrm(shape, scale):
        return jax.random.normal(next(ks), shape, jnp.float32) * scale

    inp = {}
    inp['x_prompt'] = nrm((BATCH, SEQ, D_MODEL), 1.0)
    inp['x_sample'] = nrm((DEC_BATCH, DEC_SEQ, D_MODEL), 1.0)
    for g, (window, dil) in enumerate(ATT_PATTERNS):
        L = min(window, PAST_LEN)
        inp['cache_k%d' % g] = nrm((DEPTH, DEC_BATCH, L, HEADS_PER_GROUP, HEAD_DIM), 1.0)
        inp['cache_v%d' % g] = nrm((DEPTH, DEC_BATCH, L, HEADS_PER_GROUP, HEAD_DIM), 1.0)
    inp['state_conv'] = nrm((DEPTH, DEC_BATCH, CONV_WIDTH - 1, D_CONV), 0.5)
    inp['state_ssm_re'] = nrm((DEPTH, DEC_BATCH, N_SSM_GROUPS, SSM_STATE), 0.1)
    inp['state_ssm_im'] = nrm((DEPTH, DEC_BATCH, N_SSM_GROUPS, SSM_STATE), 0.1)
    inp['c_prompt'] = nrm((BATCH, D_MODEL), 1.0)
    inp['c_sample'] = nrm((DEC_BATCH, D_MODEL), 1.0)
    inp['w_mod'] = nrm((DEPTH, D_MODEL, 6 * D_MODEL), 0.5 * D_MODEL ** -0.5)
    inp['b_mod'] = nrm((DEPTH, 6 * D_MODEL), 0.02)
    inp['g_pre_mix'] = 1.0 + nrm((DEPTH, D_MODEL), 0.02)
    inp['g_post_mix'] = 1.0 + nrm((DEPTH, D_MODEL), 0.02)
    inp['g_pre_ffn'] = 1.0 + nrm((DEPTH, D_MODEL), 0.02)
    inp['g_post_ffn'] = 1.0 + nrm((DEPTH, D_MODEL), 0.02)
    inp['w_in'] = nrm((DEPTH, D_MODEL, N_IN), D_MODEL ** -0.5)
    inp['conv_w'] = nrm((DEPTH, CONV_WIDTH, D_CONV), CONV_WIDTH ** -0.5)
    inp['conv_b'] = nrm((DEPTH, D_CONV), 0.02)
    inp['conv_ln_g'] = 1.0 + nrm((DEPTH, D_CONV), 0.02)
    inp['conv_ln_b'] = nrm((DEPTH, D_CONV), 0.02)
    inp['w_conv_out'] = nrm((DEPTH, D_CONV, D_MODEL), D_CONV ** -0.5)
    n_idx = jnp.arange(SSM_STATE, dtype=jnp.float32)
    inp['ssm_a_re'] = -0.5 + nrm((DEPTH, N_SSM_GROUPS, SSM_STATE), 0.01)
    inp['ssm_a_im'] = math.pi * n_idx + nrm((DEPTH, N_SSM_GROUPS, SSM_STATE), 0.01)
    inp['ssm_log_dt'] = jax.random.uniform(next(ks), (DEPTH, N_SSM_GROUPS), jnp.float32,
                                           math.log(DT_MIN), math.log(DT_MAX))
    inp['ssm_b_re'] = nrm((DEPTH, N_SSM_GROUPS, SSM_STATE, SSM_GROUP), (2 * SSM_GROUP) ** -0.5)
    inp['ssm_b_im'] = nrm((DEPTH, N_SSM_GROUPS, SSM_STATE, SSM_GROUP), (2 * SSM_GROUP) ** -0.5)
    inp['ssm_c_re'] = nrm((DEPTH, N_SSM_GROUPS, SSM_GROUP, SSM_STATE), SSM_STATE ** -0.5)
    inp['ssm_c_im'] = nrm((DEPTH, N_SSM_GROUPS, SSM_GROUP, SSM_STATE), SSM_STATE ** -0.5)
    inp['ssm_d'] = nrm((DEPTH, N_SSM_GROUPS, SSM_GROUP), 1.0)
    inp['w_ssm_glu'] = nrm((DEPTH, D_SSM, 2 * D_MODEL), D_SSM ** -0.5)
    inp['w_att'] = nrm((DEPTH, D_ATT_OUT, D_MODEL), D_ATT_OUT ** -0.5)
    inp['w_out'] = nrm((DEPTH, D_MODEL, D_MODEL), D_MODEL ** -0.5)
    inp['w_ffn_in'] = nrm((DEPTH, D_MODEL, 2 * D_FF), D_MODEL ** -0.5)
    inp['w_ffn_out'] = nrm((DEPTH, D_FF, D_MODEL), D_FF ** -0.5)
    return inp


def reference(x_prompt, x_sample, cache_k0, cache_v0, cache_k1, cache_v1, cache_k2, cache_v2,
              state_conv, state_ssm_re, state_ssm_im, c_prompt, c_sample,
              w_mod, b_mod, g_pre_mix, g_post_mix, g_pre_ffn, g_post_ffn, w_in,
              conv_w, conv_b, conv_ln_g, conv_ln_b, w_conv_out,
              ssm_a_re, ssm_a_im, ssm_log_dt, ssm_b_re, ssm_b_im, ssm_c_re, ssm_c_im, ssm_d, w_ssm_glu,
              w_att, w_out, w_ffn_in, w_ffn_out):
    caches_k = (cache_k0, cache_k1, cache_k2)
    caches_v = (cache_v0, cache_v1, cache_v2)
    xp, xs = x_prompt, x_sample
    nbp = xp.shape[0]
    prompt_states, sample_states = [], []
    for l in range(DEPTH):
        p = dict(w_mod=w_mod[l], b_mod=b_mod[l], g_pre_mix=g_pre_mix[l], g_post_mix=g_post_mix[l],
                 g_pre_ffn=g_pre_ffn[l], g_post_ffn=g_post_ffn[l], w_in=w_in[l],
                 conv_w=conv_w[l], conv_b=conv_b[l], conv_ln_g=conv_ln_g[l], conv_ln_b=conv_ln_b[l],
                 w_conv_out=w_conv_out[l], ssm_a_re=ssm_a_re[l], ssm_a_im=ssm_a_im[l],
                 ssm_log_dt=ssm_log_dt[l], ssm_b_re=ssm_b_re[l], ssm_b_im=ssm_b_im[l],
                 ssm_c_re=ssm_c_re[l], ssm_c_im=ssm_c_im[l], ssm_d=ssm_d[l], w_ssm_glu=w_ssm_glu[l],
                 w_att=w_att[l], w_out=w_out[l], w_ffn_in=w_ffn_in[l], w_ffn_out=w_ffn_out[l])
        past_p = (jnp.zeros((nbp, CONV_WIDTH - 1, D_CONV), xp.dtype),
                  jnp.zeros((nbp, N_SSM_GROUPS, SSM_STATE), jnp.float32),
                  jnp.zeros((nbp, N_SSM_GROUPS, SSM_STATE), jnp.float32), None, None)
        xp, st_p = trunk_layer(xp, c_prompt, past_p, p, True)
        past_s = (state_conv[l], state_ssm_re[l], state_ssm_im[l],
                  (caches_k[0][l], caches_k[1][l], caches_k[2][l]),
                  (caches_v[0][l], caches_v[1][l], caches_v[2][l]))
        xs, st_s = trunk_layer(xs, c_sample, past_s, p, False)
        prompt_states.append(st_p)
        sample_states.append(st_s)
    ps = [jnp.stack([s[i] for s in prompt_states]) for i in range(9)]
    ss = [jnp.stack([s[i] for s in sample_states]) for i in range(9)]
    return (xp, xs,
            ps[0], ps[1], ps[2], ps[3], ps[4], ps[5], ps[6], ps[7], ps[8],
            ss[0], ss[1], ss[2], ss[3], ss[4], ss[5], ss[6], ss[7], ss[8])
```

```python
import contextlib
import math
import numpy as np
import concourse.bass as bass
import concourse.mybir as mybir
from concourse.bass_utils import run_bass_kernel_spmd

F32 = mybir.dt.float32
BF16 = mybir.dt.bfloat16
AF = mybir.ActivationFunctionType
ALU = mybir.AluOpType
AX = mybir.AxisListType

D = 1024
NT = 2112
TPR = 2048
NSEQ = 16
DEPTH = 2
D_FF = 2816
N_IN = 6912
O1, O2, O3, O4, O5 = 1024, 1536, 2304, 3072, 3840
PATTERNS = ((128, 1), (512, 4), (2048, 16))
RMS_EPS = 1e-6
LN_EPS = 1e-5
NEG = -1e30
TILES = [(0, 512), (512, 512), (1024, 512), (1536, 512), (2048, 64)]

ENGS = ('pe', 'act', 'dve', 'pool', 'sp')
EPOCH = 20000
NDSEM = 12
SAME_ENGINE_SYNC = {'pe': False, 'act': True, 'dve': True, 'pool': True, 'sp': False}


class _Op:
    __slots__ = ('fn', 'waits', 'inc')

    def __init__(self, fn):
        self.fn = fn
        self.waits = []
        self.inc = None


class Prog:
    def __init__(self, nc):
        self.nc = nc
        self.stack = contextlib.ExitStack()
        self.ops = {e: [] for e in ENGS}
        self.nops = {e: 0 for e in ENGS}
        self.last_w = {}
        self.readers = {}
        self.known = {e: {} for e in ENGS}
        self.sems = {}
        self.dma_rr = {e: 0 for e in ENGS}
        self.dma_cnt = {}
        self.out_tokens = []
        self.phase = 'setup'
        self.pe_phase = []

    def sbuf(self, name, shape, dtype):
        return self.stack.enter_context(self.nc.sbuf_tensor(name, shape, dtype))

    def psum(self, name, shape, dtype):
        return self.stack.enter_context(self.nc.psum_tensor(name, shape, dtype))

    def _sem(self, key):
        if key not in self.sems:
            self.sems[key] = self.stack.enter_context(
                self.nc.semaphore("s_" + "_".join(str(k) for k in key)))
        return self.sems[key]

    def _token_of(self, eng, idx):
        return (('p', eng, idx // EPOCH), idx % EPOCH + 1)

    def _add_dep(self, op, eng, tok, src_eng):
        if tok is None:
            return
        semkey, val = tok
        if src_eng == eng and semkey[0] == 'p' and not SAME_ENGINE_SYNC[eng]:
            return
        if self.known[eng].get(semkey, 0) >= val:
            return
        self.known[eng][semkey] = val
        for i, (k, v) in enumerate(op.waits):
            if k == semkey:
                op.waits[i] = (k, max(v, val))
                return
        op.waits.append((semkey, val))

    @staticmethod
    def _flat(keys):
        out = []
        for k in keys:
            if isinstance(k, (list, tuple)):
                out.extend(Prog._flat(k))
            else:
                out.append(k)
        return out

    def _deps(self, op, eng, reads, writes):
        reads, writes = self._flat(reads), self._flat(writes)
        for r in reads:
            lw = self.last_w.get(r)
            if lw is not None:
                self._add_dep(op, eng, lw[1], lw[0])
        for w in writes:
            lw = self.last_w.get(w)
            if lw is not None:
                self._add_dep(op, eng, lw[1], lw[0])
            for (se, tok) in self.readers.get(w, ()):
                self._add_dep(op, eng, tok, se)

    def _commit(self, eng, tok, reads, writes):
        reads, writes = self._flat(reads), self._flat(writes)
        for r in reads:
            self.readers.setdefault(r, []).append((eng, tok))
        for w in writes:
            self.last_w[w] = (eng, tok)
            self.readers[w] = []

    def op(self, eng, fn, reads=(), writes=(), sync_prev=False):
        o = _Op(fn)
        self._deps(o, eng, reads, writes)
        idx = self.nops[eng]
        if sync_prev and idx > 0:
            self._add_dep(o, eng, self._token_of(eng, idx - 1), None)
        self.nops[eng] += 1
        tok = self._token_of(eng, idx)
        o.inc = (tok[0], 1)
        if eng == 'pe':
            self.pe_phase.append(self.phase)
        self.ops[eng].append(o)
        self._commit(eng, tok, reads, writes)
        return o

    def I(self, eng, method, *args, reads=(), writes=(), sync_prev=False, **kw):
        return self.op(eng, lambda e: getattr(e, method)(*args, **kw), reads=reads, writes=writes, sync_prev=sync_prev)

    def dma(self, q, out_ap, in_ap, reads=(), writes=(), out=False, **kw):
        o = _Op(lambda e: e.dma_start(out=out_ap, in_=in_ap, **kw))
        self._deps(o, q, reads, writes)
        k = self.dma_rr[q]
        self.dma_rr[q] = (k + 1) % NDSEM
        gen = 0
        while self.dma_cnt.get(('d', q, k, gen), 0) >= EPOCH // 16:
            gen += 1
        semkey = ('d', q, k, gen)
        cnt = self.dma_cnt.get(semkey, 0)
        if cnt > 0:
            self._add_dep(o, q, (semkey, 16 * cnt), None)
        self.dma_cnt[semkey] = cnt + 1
        tok = (semkey, 16 * (cnt + 1))
        o.inc = (semkey, 16)
        self.ops[q].append(o)
        self._commit(q, tok, reads, writes)
        if out:
            self.out_tokens.append(tok)
        return o

    def barrier(self):
        toks = []
        for e in ENGS:
            if self.nops[e] > 0:
                toks.append((e, self._token_of(e, self.nops[e] - 1)))
        for semkey, cnt in self.dma_cnt.items():
            toks.append((None, (semkey, 16 * cnt)))
        for e in ENGS:
            o = _Op(None)
            for (se, tok) in toks:
                if se == e:
                    continue
                self._add_dep(o, e, tok, se)
            if o.waits:
                self.ops[e].append(o)
        self.last_w = {}
        self.readers = {}

    def finish(self):
        nc = self.nc
        fin = _Op(None)
        best = {}
        for (k, v) in self.out_tokens:
            best[k] = max(best.get(k, 0), v)
        for k, v in best.items():
            fin.waits.append((k, v))
        self.ops['sp'].append(fin)
        for e in ENGS:
            for o in self.ops[e]:
                for (k, v) in o.waits:
                    self._sem(k)
                if o.inc is not None:
                    self._sem(o.inc[0])
        prog = self

        def emit(eng_name, e):
            for o in prog.ops[eng_name]:
                for (k, v) in o.waits:
                    e.wait_ge(prog.sems[k], v)
                if o.fn is None:
                    continue
                ins = o.fn(e)
                if o.inc is not None:
                    ins.then_inc(prog.sems[o.inc[0]], o.inc[1])

        with nc.Block() as block:
            @block.tensor
            def _(e):
                emit('pe', e)

            @block.scalar
            def _(e):
                emit('act', e)

            @block.vector
            def _(e):
                emit('dve', e)

            @block.gpsimd
            def _(e):
                emit('pool', e)

            @block.sync
            def _(e):
                emit('sp', e)
        self.stack.close()


class _StopMix(Exception):
    pass


class T:
    def __init__(self, h, F):
        self.h = h
        self.F = F

    def v(self, off, dims, p0=0, pn=128):
        return bass.AP(self.h, p0 * self.F + off, [[self.F, pn]] + [list(d) for d in dims])


def vps(t, off, dims, p0, pstep, pn):
    return bass.AP(t.h, p0 * t.F + off, [[pstep * t.F, pn]] + [list(d) for d in dims])


def dap(h, off, dims):
    return bass.AP(h, off, [list(d) for d in dims])


C_IDENT = 0
C_DIST = 128
C_MASK = 384
C_MG2 = 640
C_DELTA = 642
C_DS0 = 658
C_MS0 = 662
C_DS1 = 666
C_DN0 = 667
C_MN0 = 731
C_MN1 = 795
C_QM = 859
C_TOT = 863


def make_consts():
    c = np.zeros((128, C_TOT), np.float32)
    c[:, C_IDENT:C_IDENT + 128] = np.eye(128, dtype=np.float32)
    tk = np.arange(128)[:, None]
    tq = np.arange(128)[None, :]
    d0 = tq - tk
    d1 = 128 + tq - tk
    c[:, C_DIST:C_DIST + 128] = np.where(d0 >= 0, d0, 0)
    c[:, C_MASK:C_MASK + 128] = np.where(d0 >= 0, 0.0, NEG)
    c[:, C_DIST + 128:C_DIST + 256] = np.where(d1 <= 128, d1, 0)
    c[:, C_MASK + 128:C_MASK + 256] = np.where(d1 <= 128, 0.0, NEG)
    p = np.arange(128)
    for g2 in range(2):
        c[:, C_MG2 + g2] = ((p % 32) // 16 == g2)
    for cc in range(16):
        c[:, C_DELTA + cc] = (p % 16 == cc)
    for j in range(4):
        kk = 128 + j - p
        c[:, C_DS0 + j] = np.where(kk <= 128, kk, 0)
        c[:, C_MS0 + j] = np.where(kk <= 128, 0.0, NEG)
    c[:, C_DS1] = 128 - p
    ki = np.arange(64)[:, None]
    qi = np.arange(64)[None, :]
    same = (ki // 4) == (qi // 4)
    dn = (qi % 4) - (ki % 4)
    v0 = same & (dn >= 0)
    c[:64, C_DN0:C_DN0 + 64] = np.where(v0, dn, 0)
    c[:64, C_MN0:C_MN0 + 64] = np.where(v0, 0.0, NEG)
    c[:64, C_MN1:C_MN1 + 64] = np.where(ki == qi, 0.0, NEG)
    for q in range(4):
        c[:, C_QM + q] = (p // 32 == q)
    return c


def make_sel():
    c = np.zeros((128, 2048), np.float32)
    for tpar in range(2):
        for gl in range(8):
            blk = np.zeros((128, 128), np.float32)
            for pp in range(128):
                if (pp % 32) // 16 == tpar:
                    blk[pp, gl * 16 + pp % 16] = 1.0
            o = (tpar * 8 + gl) * 128
            c[:, o:o + 128] = blk
    return c


def alibi_slope(h):
    return 2.0 ** (-8.0 * (h + 1) / 12.0)


IN_SPECS = [
    ("x_p", [2048, 1024]), ("x_s", [64, 1024]), ("c_all", [17, 1024]),
    ("ck0", [2, 16, 128, 256]), ("cv0", [2, 16, 128, 256]),
    ("ck1", [2, 16, 512, 256]), ("cv1", [2, 16, 512, 256]),
    ("ck2", [2, 16, 512, 256]), ("cv2", [2, 16, 512, 256]),
    ("st_conv", [2, 16, 30, 512]), ("st_re", [2, 16, 2048]), ("st_im", [2, 16, 2048]),
    ("w_mod", [2, 1024, 6144]), ("b_mod", [2, 6144]),
    ("g_pre_mix", [2, 1024]), ("g_post_mix", [2, 1024]), ("g_pre_ffn", [2, 1024]), ("g_post_ffn", [2, 1024]),
    ("w_in", [2, 1024, 6912]), ("conv_w", [2, 31, 512]), ("conv_b", [2, 512]),
    ("conv_ln_g", [2, 512]), ("conv_ln_b", [2, 512]), ("w_conv_out", [2, 512, 1024]),
    ("ssm_a_re", [2, 2048]), ("ssm_a_im", [2, 2048]), ("ssm_log_dt", [2, 32]),
    ("ssm_b_re", [2, 32 * 64 * 16]), ("ssm_b_im", [2, 32 * 64 * 16]),
    ("ssm_c_re", [2, 32 * 16 * 64]), ("ssm_c_im", [2, 32 * 16 * 64]), ("ssm_d", [2, 512]),
    ("w_ssm_glu", [2, 512, 2048]), ("w_att", [2, 256, 1024]), ("w_out", [2, 1024, 1024]),
    ("w_ffn_in", [2, 1024, 5632]), ("w_ffn_out", [2, 2816, 1024]),
    ("consts", [128, C_TOT]), ("selc", [128, 2048]),
]
OUT_SPECS = [
    ("y_p", [2048, 1024]), ("y_s", [64, 1024]),
    ("k0_p", [2, 128, 256]), ("v0_p", [2, 128, 256]), ("k1_p", [2, 512, 256]), ("v1_p", [2, 512, 256]),
    ("k2_p", [2, 2048, 256]), ("v2_p", [2, 2048, 256]),
    ("conv_p", [2, 30, 512]), ("ssm_re_p", [2, 2048]), ("ssm_im_p", [2, 2048]),
    ("k0_s", [2, 64, 256]), ("v0_s", [2, 64, 256]), ("k1_s", [2, 64, 256]), ("v1_s", [2, 64, 256]),
    ("k2_s", [2, 64, 256]), ("v2_s", [2, 64, 256]),
    ("conv_s", [2, 16, 30, 512]), ("ssm_re_s", [2, 16, 2048]), ("ssm_im_s", [2, 16, 2048]),
]


def build_program(stop_after=None):
    import os
    KSTOP = os.environ.get('KSTOP', '')
    nc = bass.Bass("TRN2", target_bir_lowering=False)
    P = Prog(nc)
    dr = {}
    for name, shp in IN_SPECS:
        dr[name] = nc.dram_tensor(name, shp, F32, kind="ExternalInput")
    for name, shp in OUT_SPECS:
        dr[name] = nc.dram_tensor(name, shp, F32, kind="ExternalOutput")

    def sb(name, F, dt):
        return T(P.sbuf(name, [128, F], dt), F)

    X = sb("X", 8 * NT, F32)
    H = sb("H", 8 * NT, BF16)
    MG = sb("MG", 8 * NT, BF16)
    NWB = 2
    WB = [sb("WB%d" % i, 4096, BF16) for i in range(NWB)]
    CST = sb("CST", C_TOT, F32)
    IDB = sb("IDB", 128, BF16)
    ONB = sb("ONB", 128, BF16)
    ONF = sb("ONF", 128, F32)
    SELB = sb("SELB", 2048, BF16)
    CT = sb("CT", 8 * 17, BF16)
    VEC = sb("VEC", 96, F32)
    MOD = sb("MOD", 48 * 17, F32)
    AMt = sb("AMt", 8 * 17, F32)
    GMt = sb("GMt", 8 * 17, F32)
    AFt = sb("AFt", 8 * 17, F32)
    GFt = sb("GFt", 8 * 17, F32)
    RS = sb("RS", 512, F32)
    TMPA = sb("TMPA", 512, F32)
    SCRB = 41 * 1024
    SCR_h = P.sbuf("SCR", [128, SCRB // 2], BF16)
    SCR16 = T(SCR_h, SCRB // 2)
    SCR32 = T(SCR_h.bitcast(F32), SCRB // 4)
    PS = []
    PSB = []
    for i in range(8):
        h = P.psum("PS%d" % i, [128, 512], F32)
        PS.append(T(h, 512))
        PSB.append(T(h.bitcast(BF16), 1024))

    IDF = lambda pn=128: CST.v(C_IDENT, [[1, pn]], pn=pn)

    st = {'wrr': 0, 'prr': 0, 'tmp': 0}
    DENSE = (0, 1, 2, 3, 6, 7)

    def tmpbuf(w):
        i = st['tmp'] % 2
        st['tmp'] += 1
        return (TMPA, RS)[i].v(0, [[1, w]]), ('TMPA', 'RS')[i]

    def wload(hname, layer_off, row_stride, col0, ncols, kc, row0=0, pieces=None):
        i = st['wrr']
        st['wrr'] = (i + 1) % NWB
        wb = WB[i]
        if pieces is None and kc >= 16:
            k1 = kc // 2
            for pi_, (ka, kn) in enumerate(((0, k1), (k1, kc - k1))):
                src = dap(dr[hname], layer_off + (row0 + ka * 128) * row_stride + col0, [[row_stride, 128], [128 * row_stride, kn], [1, ncols]])
                wkeys = ['WB%d_0' % i] if pi_ == 0 else ['WB%d_1' % i, 'WB%d_2' % i]
                P.dma('pool', wb.v(ka * ncols, [[ncols, kn], [1, ncols]]), src, reads=[], writes=wkeys)
            return wb, ['WB%d_%d' % (i, x) for x in range(3)]
        if pieces is None:
            pieces = [(col0, ncols)]
        tot = sum(n for _, n in pieces)
        o = 0
        keys = []
        for pi_, (c0_, n_) in enumerate(pieces):
            src = dap(dr[hname], layer_off + row0 * row_stride + c0_, [[row_stride, 128], [128 * row_stride, kc], [1, n_]])
            wkeys = ['WB%d_%d' % (i, pi_)] if pi_ < len(pieces) - 1 else ['WB%d_%d' % (i, x) for x in range(pi_, 3)]
            P.dma('pool', wb.v(o, [[tot, kc], [1, n_]]), src, reads=[], writes=wkeys)
            o += n_
        return wb, ['WB%d_%d' % (i, x) for x in range(3)]

    def wload_multi(items):
        i = st['wrr']
        st['wrr'] = (i + 1) % NWB
        wb = WB[i]
        o = 0
        offs = []
        keys = []
        for pi_, (hname, layer_off, row_stride, c0_, n_, kc) in enumerate(items):
            src = dap(dr[hname], layer_off + c0_, [[row_stride, 128], [128 * row_stride, kc], [1, n_]])
            wkeys = ['WB%d_%d' % (i, pi_)] if pi_ < len(items) - 1 else ['WB%d_%d' % (i, x) for x in range(pi_, 3)]
            P.dma('pool', wb.v(o, [[n_, kc], [1, n_]]), src, reads=[], writes=wkeys)
            offs.append(o)
            o += kc * n_
        assert o <= 4096 and len(items) <= 3
        return wb, ['WB%d_%d' % (i, x) for x in range(3)], offs

    def pbank(pool=(0, 1, 2, 3)):
        i = st['prr']
        st['prr'] = i + 1
        b = pool[i % len(pool)]
        return b, 'PS%d' % b

    def evac(i, out_ap, in_ap, reads, writes):
        if i % 2 == 0:
            P.I('dve', 'tensor_copy', out_ap, in_ap, reads=reads, writes=writes)
        else:
            P.I('act', 'copy', out_ap, in_ap, reads=reads, writes=writes)

    def transpose_rows(src_ap_fn, nrows, dst_fn, rkey, ncol_chunks, bankpool=(4, 5)):
        for c0 in range(0, ncol_chunks, 4):
            ncn = min(4, ncol_chunks - c0)
            b, bk = pbank(bankpool)
            for c in range(ncn):
                P.I('pe', 'transpose', PS[b].v(c * 128, [[1, nrows]]), src_ap_fn(c0 + c), IDF(nrows),
                    reads=[rkey, 'CST'], writes=[bk])
            dst_fn(c0, ncn, b, bk)

    P.dma('sp', CST.v(0, [[1, C_TOT]]), dap(dr['consts'], 0, [[C_TOT, 128], [1, C_TOT]]), writes=['CST'])
    P.dma('pool', SELB.v(0, [[1, 2048]]), dap(dr['selc'], 0, [[2048, 128], [1, 2048]]), writes=['SELB'])
    P.I('dve', 'tensor_copy', IDB.v(0, [[1, 128]]), CST.v(C_IDENT, [[1, 128]]), reads=['CST'], writes=['IDB'])
    P.I('pool', 'memset', ONB.v(0, [[1, 128]]), 1.0, writes=['ONB'])
    P.I('pool', 'memset', ONF.v(0, [[1, 128]]), 1.0, writes=['ONF'])

    for ti in range(17):
        rows = 128 if ti < 16 else 64
        boff = (ti % 2) * 1024
        key = 'XIN%d' % (ti % 2)
        if ti < 16:
            src = dap(dr['x_p'], ti * 128 * 1024, [[1024, 128], [1, 1024]])
        else:
            src = dap(dr['x_s'], 0, [[1024, 64], [1, 1024]])
        P.dma('sp', SCR32.v(boff, [[1, 1024]], pn=rows), src, writes=[key])
        t0 = ti * 128

        def dst(c0, ncn, b, bk, t0=t0, rows=rows):
            evac(c0 // 4, X.v(c0 * NT + t0, [[NT, ncn], [1, rows]]), PS[b].v(0, [[128, ncn], [1, rows]]), [bk], ['X'])
        transpose_rows(lambda c, boff=boff, rows=rows: SCR32.v(boff + c * 128, [[1, 128]], pn=rows), rows, dst, key, 8)

    P.dma('sp', SCR32.v(2048, [[1, 1024]], pn=17), dap(dr['c_all'], 0, [[1024, 17], [1, 1024]]), writes=['CIN'])

    def dst_c(c0, ncn, b, bk):
        P.I('act', 'activation', CT.v(c0 * 17, [[17, ncn], [1, 17]]), PS[b].v(0, [[128, ncn], [1, 17]]), AF.Silu,
            reads=[bk], writes=['CT'])
    transpose_rows(lambda c: SCR32.v(2048 + c * 128, [[1, 128]], pn=17), 17, dst_c, 'CIN', 8)
    P.barrier()

    def rstd_from(src_aps, width, scale, eps, key_r):
        nk = len(src_aps)
        for k in range(nk):
            P.I('act', 'activation', SCR16.v(k * 512, [[1, width]]), src_aps[k], AF.Square, reads=[key_r], writes=['SQ'])
        b, bk = pbank((4, 5))
        for k in range(nk):
            P.I('pe', 'matmul', PS[b].v(0, [[1, width]]), ONB.v(0, [[1, 128]]), SCR16.v(k * 512, [[1, width]]),
                start=(k == 0), stop=(k == nk - 1), reads=['SQ', 'ONB'], writes=[bk])
        P.I('act', 'activation', RS.v(0, [[1, width]]), PS[b].v(0, [[1, width]]), AF.Sqrt, bias=eps, scale=scale,
            reads=[bk], writes=['RS'])
        P.I('dve', 'reciprocal', RS.v(0, [[1, width]]), RS.v(0, [[1, width]]), reads=['RS'], writes=['RS'])

    def pre_norm(At, Bt):
        for (t0, w) in TILES:
            rstd_from([X.v(k * NT + t0, [[1, w]]) for k in range(8)], w, 1.0 / D, RMS_EPS, 'X')
            for k in range(8):
                if t0 < TPR:
                    tp_ = TMPA.v(0, [[1, w]]) if k % 2 == 0 else SCR32.v(2048, [[1, w]])
                    tpk = 'TMPA' if k % 2 == 0 else 'TMPB'
                    P.I('dve', 'scalar_tensor_tensor', tp_, X.v(k * NT + t0, [[1, w]]), At.v(k * 17, [[1, 1]]),
                        RS.v(0, [[1, w]]), op0=ALU.mult, op1=ALU.mult, reads=['X', 'RS', 'MODS'], writes=[tpk])
                    P.I('act', 'activation', H.v(k * NT + t0, [[1, w]]), tp_, AF.Identity,
                        bias=Bt.v(k * 17, [[1, 1]]), scale=1.0, reads=[tpk, 'MODS', 'MOD'], writes=['H'])
                else:
                    P.I('dve', 'tensor_tensor', TMPA.v(0, [[4, 16], [1, 4]]), X.v(k * NT + TPR, [[4, 16], [1, 4]]),
                        At.v(k * 17 + 1, [[1, 16], [0, 4]]), op=ALU.mult, reads=['X', 'MODS'], writes=['TMPA'])
                    P.I('dve', 'tensor_tensor', TMPA.v(0, [[1, 64]]), TMPA.v(0, [[1, 64]]), RS.v(0, [[1, 64]]), op=ALU.mult,
                        reads=['TMPA', 'RS'], writes=['TMPA'])
                    P.I('dve', 'tensor_tensor', H.v(k * NT + TPR, [[4, 16], [1, 4]]), TMPA.v(0, [[4, 16], [1, 4]]),
                        Bt.v(k * 17 + 1, [[1, 16], [0, 4]]), op=ALU.add, reads=['TMPA', 'MODS', 'MOD'], writes=['H'])

    def post_norm_add(MO, t0, w, Gt, tmp2off):
        rstd_from([MO(k, w) for k in range(8)], w, 1.0 / D, RMS_EPS, 'MO')
        for k in range(8):
            if k % 2 == 0:
                tk_ = 'TMPA'
                tv = lambda dims: TMPA.v(0, dims)
            else:
                tk_ = 'TMPB'
                tv = lambda dims: SCR32.v(tmp2off, dims)
            if t0 < TPR:
                P.I('dve', 'scalar_tensor_tensor', tv([[1, w]]), MO(k, w), Gt.v(k * 17, [[1, 1]]), RS.v(0, [[1, w]]),
                    op0=ALU.mult, op1=ALU.mult, reads=['MO', 'RS', 'MODS'], writes=[tk_])
            else:
                P.I('dve', 'tensor_tensor', tv([[4, 16], [1, 4]]), MO(k, None), Gt.v(k * 17 + 1, [[1, 16], [0, 4]]),
                    op=ALU.mult, reads=['MO', 'MODS'], writes=[tk_])
                P.I('dve', 'tensor_tensor', tv([[1, 64]]), tv([[1, 64]]), RS.v(0, [[1, 64]]), op=ALU.mult,
                    reads=[tk_, 'RS'], writes=[tk_])
            P.I('dve', 'tensor_tensor', X.v(k * NT + t0, [[1, w]]), X.v(k * NT + t0, [[1, w]]), tv([[1, w]]), op=ALU.add,
                reads=[tk_, 'XA%d' % k], writes=['XA%d' % k])

    class _Off:
        def __init__(self, t, off):
            self.t, self.off = t, off

        def v(self, off, dims, p0=0, pn=128):
            return self.t.v(self.off + off, dims, p0, pn)

    for l in range(DEPTH):
        P.phase = 'mod'
        rows = [("g_pre_mix", 8), ("g_post_mix", 8), ("g_pre_ffn", 8), ("g_post_ffn", 8),
                ("conv_b", 4), ("conv_ln_g", 4), ("conv_ln_b", 4), ("b_mod", 48)]
        r0 = 0
        vo = {}
        for name, n in rows:
            P.dma('sp', SCR32.v(0, [[1, 128]], p0=r0, pn=n), dap(dr[name], l * n * 128, [[128, n], [1, 128]]), writes=['STG'])
            vo[name] = r0
            r0 += n
        b, bk = pbank((4, 5))
        P.I('pe', 'transpose', PS[b].v(0, [[1, 92]]), SCR32.v(0, [[1, 128]], pn=92), IDF(92), reads=['STG', 'CST'], writes=[bk])
        P.I('dve', 'tensor_copy', VEC.v(0, [[1, 92]]), PS[b].v(0, [[1, 92]]), reads=[bk], writes=['VEC'])

        for blk in range(12):
            wb, wk = wload('w_mod', l * 1024 * 6144, 6144, blk * 512, 512, 8)
            b, bk = pbank((4, 5))
            for m in range(4):
                for k in range(8):
                    P.I('pe', 'matmul', PS[b].v(m * 17, [[1, 17]]), wb.v(k * 512 + m * 128, [[1, 128]]), CT.v(k * 17, [[1, 17]]),
                        start=(k == 0), stop=(k == 7), reads=[wk, 'CT'], writes=[bk])
            for m in range(4):
                j = blk * 4 + m
                P.I('dve', 'tensor_scalar', MOD.v(j * 17, [[1, 17]]), PS[b].v(m * 17, [[1, 17]]), VEC.v(vo['b_mod'] + j, [[1, 1]]),
                    None, op0=ALU.add, reads=[bk, 'VEC'], writes=['MOD'])

        def mk(dst, modj, vname, plus1):
            if plus1:
                P.I('dve', 'tensor_scalar', dst.v(0, [[1, 136]]), MOD.v(modj * 17, [[1, 136]]), 1.0, None, op0=ALU.add,
                    reads=['MOD'], writes=['MODS'])
                P.I('dve', 'tensor_tensor', dst.v(0, [[17, 8], [1, 17]]), dst.v(0, [[17, 8], [1, 17]]),
                    VEC.v(vo[vname], [[1, 8], [0, 17]]), op=ALU.mult, reads=['MODS', 'VEC'], writes=['MODS'])
            else:
                P.I('dve', 'tensor_tensor', dst.v(0, [[17, 8], [1, 17]]), MOD.v(modj * 17, [[17, 8], [1, 17]]),
                    VEC.v(vo[vname], [[1, 8], [0, 17]]), op=ALU.mult, reads=['MOD', 'VEC'], writes=['MODS'])
        mk(AMt, 8, 'g_pre_mix', True)
        mk(GMt, 16, 'g_post_mix', False)
        mk(AFt, 32, 'g_pre_ffn', True)
        mk(GFt, 40, 'g_post_ffn', False)
        BMv = _Off(MOD, 0)
        BFv = _Off(MOD, 24 * 17)
        P.barrier()

        P.phase = 'prenorm'
        pre_norm(AMt, BMv)
        P.barrier()

        P.phase = 'kv'
        kvi = 0
        for g, (win, dil) in enumerate(PATTERNS):
            keep = min(win, TPR)
            for cbase, oname_p, oname_s in ((O3, "k%d_p" % g, "k%d_s" % g), (O4, "v%d_p" % g, "v%d_s" % g)):
                col0 = cbase + g * 256
                wb, wk = wload('w_in', l * 1024 * N_IN, N_IN, col0, 256, 8)
                tiles = [(t0, 128) for t0 in range(TPR - keep, TPR, 128)] + [(TPR, 64)]
                for (t0, rows_) in tiles:
                    b, bk = pbank(DENSE)
                    for k in range(8):
                        P.I('pe', 'matmul', PS[b].v(0, [[1, 256]], pn=rows_), H.v(k * NT + t0, [[1, rows_]]), wb.v(k * 256, [[1, 256]]),
                            start=(k == 0), stop=(k == 7), reads=[wk, 'H'], writes=[bk])
                    so = 4096 + (kvi % 2) * 256
                    skey = 'KVST%d' % (kvi % 2)
                    kvi += 1
                    evac(kvi, SCR32.v(so, [[1, 256]], pn=rows_), PS[b].v(0, [[1, 256]], pn=rows_), [bk], [skey])
                    if t0 < TPR:
                        dst_ = dap(dr[oname_p], l * keep * 256 + (t0 - (TPR - keep)) * 256, [[256, 128], [1, 256]])
                    else:
                        dst_ = dap(dr[oname_s], l * 64 * 256, [[256, 64], [1, 256]])
                    P.dma('sp', dst_, SCR32.v(so, [[1, 256]], pn=rows_), reads=[skey], writes=['OUT' + oname_p], out=True)
        P.barrier()


        def mixer():
            P.phase = 'conv'
            skipmix = False
            UBo, UBW = 4096, 2142
            USo = 12664
            DGo = 14840
            MEANo = 9404
            UFo = 9916
            CWo = 10292
            P.dma('sp', SCR32.v(0, [[1, 512]], pn=31), dap(dr['conv_w'], l * 31 * 512, [[512, 31], [1, 512]]), writes=['STG'])
            b, bk = pbank((4, 5))
            for c in range(4):
                P.I('pe', 'transpose', PS[b].v(c * 32, [[1, 31]]), SCR32.v(c * 128, [[1, 128]], pn=31), IDF(31),
                    reads=['STG', 'CST'], writes=[bk])
            P.I('dve', 'tensor_copy', SCR32.v(CWo, [[31, 4], [1, 31]]), PS[b].v(0, [[32, 4], [1, 31]]), reads=[bk], writes=['CW'])
            if KSTOP == 'A1':
                raise _StopMix()
            P.dma('sp', dap(dr['conv_s'], l * 16 * 15360, [[15360, 16], [1, 13312]]),
                  dap(dr['st_conv'], l * 16 * 15360 + 4 * 512, [[15360, 16], [1, 13312]]), writes=['OUTconvs'], out=True)
            for rt in range(4):
                P.dma('sp', SCR32.v(0, [[1, 512]], pn=120), dap(dr['st_conv'], l * 16 * 15360 + rt * 4 * 15360, [[512, 120], [1, 512]]),
                      writes=['STG'])
                b, bk = pbank((4, 5))
                for c in range(4):
                    P.I('pe', 'transpose', PS[b].v(c * 128, [[1, 120]]), SCR32.v(c * 128, [[1, 128]], pn=120), IDF(120),
                        reads=['STG', 'CST'], writes=[bk])
                P.I('act', 'copy', SCR16.v(USo + rt * 4 * 34, [[544, 4], [34, 4], [1, 30]]), PS[b].v(0, [[128, 4], [30, 4], [1, 30]]),
                    reads=[bk], writes=['US'])
            if KSTOP == 'A2':
                raise _StopMix()
            P.I('pool', 'memset', SCR16.v(UBo, [[UBW, 4], [1, 30]]), 0.0, writes=['UBu0', 'UBu1', 'UBu2', 'UBu3'])
            for c in range(4):
                wb, wk = wload('w_in', l * 1024 * N_IN, N_IN, 0, 0, 8, pieces=[(c * 128, 128), (512 + c * 128, 128)])
                for (t0, w) in TILES:
                    b1, k1 = pbank(DENSE)
                    b2, k2 = pbank(DENSE)
                    for k in range(8):
                        P.I('pe', 'matmul', PS[b1].v(0, [[1, w]]), wb.v(k * 256, [[1, 128]]), H.v(k * NT + t0, [[1, w]]),
                            start=(k == 0), stop=(k == 7), reads=[wk, 'H'], writes=[k1])
                    for k in range(8):
                        P.I('pe', 'matmul', PS[b2].v(0, [[1, w]]), wb.v(k * 256 + 128, [[1, 128]]), H.v(k * NT + t0, [[1, w]]),
                            start=(k == 0), stop=(k == 7), reads=[wk, 'H'], writes=[k2])
                    TG_, tgk = ((TMPA, 'TMPA'), (RS, 'RS'))[st['tmp'] % 2]
                    st['tmp'] += 1
                    P.I('act', 'activation', TG_.v(0, [[1, w]]), PS[b2].v(0, [[1, w]]), AF.Sigmoid, reads=[k2], writes=[tgk])
                    if t0 < TPR:
                        P.I('dve', 'tensor_tensor', SCR16.v(UBo + c * UBW + 30 + t0, [[1, w]]), TG_.v(0, [[1, w]]), PS[b1].v(0, [[1, w]]),
                            op=ALU.mult, reads=[tgk, k1], writes=['UBu%d' % c])
                        if t0 == 1536:
                            P.I('dve', 'tensor_tensor', SCR32.v(UFo + c * 94, [[1, 30]]), TG_.v(482, [[1, 30]]), PS[b1].v(482, [[1, 30]]),
                                op=ALU.mult, reads=[tgk, k1], writes=['UF'])
                    else:
                        P.I('dve', 'tensor_tensor', SCR16.v(USo + c * 544 + 30, [[34, 16], [1, 4]]), TG_.v(0, [[4, 16], [1, 4]]),
                            PS[b1].v(0, [[4, 16], [1, 4]]), op=ALU.mult, reads=[tgk, k1], writes=['US'])
                        P.I('dve', 'tensor_tensor', SCR32.v(UFo + c * 94 + 30, [[1, 64]]), TG_.v(0, [[1, 64]]), PS[b1].v(0, [[1, 64]]),
                            op=ALU.mult, reads=[tgk, k1], writes=['UF'])
            if KSTOP == 'A3':
                raise _StopMix()
            b, bk = pbank((4, 5))
            for c in range(4):
                P.I('pe', 'transpose', PS[b].v(c * 128, [[1, 128]], pn=30), SCR32.v(UFo + c * 94, [[1, 30]]), IDF(128),
                    reads=['UF', 'CST'], writes=[bk])
            P.I('dve', 'tensor_copy', SCR32.v(0, [[1, 512]], pn=30), PS[b].v(0, [[1, 512]], pn=30), reads=[bk], writes=['STG'])
            P.dma('sp', dap(dr['conv_p'], l * 30 * 512, [[512, 30], [1, 512]]), SCR32.v(0, [[1, 512]], pn=30), reads=['STG'],
                  writes=['OUTconvp'], out=True)
            b, bk = pbank((4, 5))
            for c in range(4):
                P.I('pe', 'transpose', PS[b].v(c * 128, [[1, 128]], pn=64), SCR32.v(UFo + c * 94 + 30, [[1, 64]]), IDF(128),
                    reads=['UF', 'CST'], writes=[bk])
            P.I('act', 'copy', SCR32.v(512, [[1, 512]], pn=64), PS[b].v(0, [[1, 512]], pn=64), reads=[bk], writes=['STG2'])
            for j in range(4):
                P.dma('sp', dap(dr['conv_s'], l * 16 * 15360 + (26 + j) * 512, [[15360, 16], [1, 512]]),
                      vps(SCR32, 512, [[1, 512]], j, 4, 16), reads=['STG2'], writes=['OUTconvs'], out=True)
            if KSTOP == 'A4':
                raise _StopMix()
            for c in range(4):
                for k in range(31):
                    P.I('pool' if k % 2 else 'dve', 'tensor_scalar', SCR16.v(DGo + k * 128, [[1, 128]]), IDB.v(0, [[1, 128]]),
                        SCR32.v(CWo + c * 31 + k, [[1, 1]]), None, op0=ALU.mult, reads=['IDB', 'CW'], writes=['DG%d' % k])
                for (t0, w) in reversed(TILES):
                    b, bk = pbank(DENSE)
                    for k in range(31):
                        if t0 < TPR:
                            rhs = SCR16.v(UBo + c * UBW + t0 + k, [[1, w]])
                            out_ = PS[b].v(0, [[1, w]])
                        else:
                            rhs = SCR16.v(USo + c * 544 + k, [[34, 16], [1, 4]])
                            out_ = PS[b].v(0, [[4, 16], [1, 4]])
                        P.I('pe', 'matmul', out_, SCR16.v(DGo + k * 128, [[1, 128]]), rhs, start=(k == 0), stop=(k == 30),
                            reads=['DG%d' % k, 'UBu%d' % c, 'US'], writes=[bk])
                    P.I('act', 'activation', SCR16.v(UBo + c * UBW + 30 + t0, [[1, w]]), PS[b].v(0, [[1, w]]), AF.Identity,
                        bias=VEC.v(vo['conv_b'] + c, [[1, 1]]), scale=1.0, reads=[bk, 'VEC'], writes=['UBy%d' % c])
            if KSTOP == 'A5':
                raise _StopMix()
            for (t0, w) in TILES:
                ys = [SCR16.v(UBo + c * UBW + 30 + t0, [[1, w]]) for c in range(4)]
                for c in range(4):
                    P.I('act', 'activation', SCR16.v(c * 512, [[1, w]]), ys[c], AF.Square, reads=['UBy%d' % c], writes=['YQ'])
                b1, k1 = pbank((4, 5))
                b2, k2 = pbank((4, 5))
                for c in range(4):
                    P.I('pe', 'matmul', PS[b1].v(0, [[1, w]]), ONB.v(0, [[1, 128]]), ys[c], start=(c == 0), stop=(c == 3),
                        reads=['ONB', 'UBy%d' % c], writes=[k1])
                for c in range(4):
                    P.I('pe', 'matmul', PS[b2].v(0, [[1, w]]), ONB.v(0, [[1, 128]]), SCR16.v(c * 512, [[1, w]]), start=(c == 0), stop=(c == 3),
                        reads=['ONB', 'YQ'], writes=[k2])
                MEAN = SCR32.v(MEANo, [[1, w]])
                TA = TMPA.v(0, [[1, w]])
                RSw = RS.v(0, [[1, w]])
                P.I('act', 'activation', MEAN, PS[b1].v(0, [[1, w]]), AF.Identity, scale=1.0 / 512, reads=[k1], writes=['MEAN'])
                P.I('dve', 'tensor_tensor', TA, MEAN, MEAN, op=ALU.mult, reads=['MEAN'], writes=['TMPA'])
                P.I('dve', 'scalar_tensor_tensor', TA, PS[b2].v(0, [[1, w]]), 1.0 / 512, TA, op0=ALU.mult, op1=ALU.subtract,
                    reads=[k2, 'TMPA'], writes=['TMPA'])
                P.I('act', 'activation', RSw, TA, AF.Sqrt, bias=LN_EPS, scale=1.0, reads=['TMPA'], writes=['RS'])
                P.I('dve', 'reciprocal', RSw, RSw, reads=['RS'], writes=['RS'])
                for c in range(4):
                    P.I('dve', 'tensor_tensor', TA, ys[c], MEAN, op=ALU.subtract, reads=['UBy%d' % c, 'MEAN'], writes=['TMPA'])
                    P.I('dve', 'scalar_tensor_tensor', TA, TA, VEC.v(vo['conv_ln_g'] + c, [[1, 1]]), RSw, op0=ALU.mult, op1=ALU.mult,
                        reads=['TMPA', 'RS', 'VEC'], writes=['TMPA'])
                    P.I('act', 'activation', ys[c], TA, AF.Silu, bias=VEC.v(vo['conv_ln_b'] + c, [[1, 1]]), scale=1.0,
                        reads=['TMPA', 'VEC'], writes=['UBy%d' % c])

            if KSTOP == 'A6':
                raise _StopMix()
            def gated_branch(gate_col0, witem_fn, kcb, rhs_fn, rkeys, first):
                for m in range(8):
                    wb, wk, offs = wload_multi([('w_in', l * 1024 * N_IN, N_IN, gate_col0 + m * 128, 128, 8), witem_fn(m)])
                    for (t0, w) in TILES:
                        bg, kg = pbank(DENSE)
                        bb, kb = pbank(DENSE)
                        for k in range(8):
                            P.I('pe', 'matmul', PS[bg].v(0, [[1, w]]), wb.v(offs[0] + k * 128, [[1, 128]]), H.v(k * NT + t0, [[1, w]]),
                                start=(k == 0), stop=(k == 7), reads=[wk, 'H'], writes=[kg])
                        for k in range(kcb):
                            P.I('pe', 'matmul', PS[bb].v(0, [[1, w]]), wb.v(offs[1] + k * 128, [[1, 128]]), rhs_fn(k, t0, w),
                                start=(k == 0), stop=(k == kcb - 1), reads=[wk] + rkeys, writes=[kb])
                        TA, tak = tmpbuf(w)
                        P.I('act', 'activation', TA, PS[bg].v(0, [[1, w]]), AF.Sigmoid, reads=[kg], writes=[tak])
                        if first:
                            P.I('dve', 'tensor_tensor', MG.v(m * NT + t0, [[1, w]]), TA, PS[bb].v(0, [[1, w]]), op=ALU.mult,
                                reads=[tak, kb], writes=['MG'])
                        else:
                            P.I('dve', 'tensor_tensor', TA, TA, PS[bb].v(0, [[1, w]]), op=ALU.mult, reads=[tak, kb], writes=[tak])
                            P.I('dve', 'tensor_tensor', MG.v(m * NT + t0, [[1, w]]), MG.v(m * NT + t0, [[1, w]]), TA, op=ALU.add,
                                reads=[tak, 'MG'], writes=['MG'])

            gated_branch(O5, lambda m: ('w_conv_out', l * 512 * 1024, 1024, m * 128, 128, 4), 4,
                         lambda k, t0, w: SCR16.v(UBo + k * UBW + 30 + t0, [[1, w]]), ['UBy0', 'UBy1', 'UBy2', 'UBy3'], True)
            P.barrier()


            P.phase = 'attn'
            if KSTOP == 'C0':
                raise _StopMix()
            KTo, QTo0, VTo = 0, 4096, 5120
            NUMo, DENo = 3584, 5632
            ATTo = 15360
            BTo, SSo0, PTo0 = 8736, 9248, 19520
            ANo, ADo = 10016, 10080
            loff = l * 1024 * N_IN
            cnt = {'s': 0, 'n': 0, 'q': 0, 'c': 0}

            def attn_prompt(g, hp):
                win, dil = PATTERNS[g]
                nblk = TPR // (dil * 128)
                wb, wk = wload('w_in', loff, N_IN, O3 + g * 256 + hp * 128, 128, 8)
                for ti in range(4):
                    b, bk = pbank((0, 1, 4, 5))
                    for k in range(8):
                        P.I('pe', 'matmul', PS[b].v(0, [[1, 512]]), wb.v(k * 128, [[1, 128]]), H.v(k * NT + ti * 512, [[1, 512]]),
                            start=(k == 0), stop=(k == 7), reads=[wk, 'H'], writes=[bk])
                    evac(ti, SCR16.v(KTo + ti * 512, [[1, 512]]), PS[b].v(0, [[1, 512]]), [bk], ['KT'])
                wb, wk = wload('w_in', loff, N_IN, O4 + g * 256 + hp * 128, 128, 8)
                for t4 in range(4):
                    b, bk = pbank((0, 1, 4, 5))
                    for tt in range(4):
                        ti = t4 * 4 + tt
                        r, blk = ti // nblk, ti % nblk
                        for k in range(8):
                            P.I('pe', 'matmul', PS[b].v(tt * 128, [[1, 128]]), H.v(k * NT + dil * 128 * blk + r, [[dil, 128]]),
                                wb.v(k * 128, [[1, 128]]), start=(k == 0), stop=(k == 7), reads=[wk, 'H'], writes=[bk])
                    evac(t4, SCR16.v(VTo + t4 * 512, [[1, 512]]), PS[b].v(0, [[1, 512]]), [bk], ['VT'])
                for h in range(2):
                    cc = alibi_slope(g * 4 + hp * 2 + h) * dil
                    P.I('dve', 'scalar_tensor_tensor', SCR32.v(BTo + h * 256, [[1, 256]]), CST.v(C_DIST, [[1, 256]]), -cc,
                        CST.v(C_MASK, [[1, 256]]), op0=ALU.mult, op1=ALU.add, reads=['CST'], writes=['BT'])
                wq, wqk = wload('w_in', loff, N_IN, O2 + g * 256 + hp * 128, 128, 8)
                nq = min(4, nblk)
                units = []
                for r in range(dil):
                    for qc in range(nblk // nq):
                        for bi in range(nq):
                            for h in range(2):
                                units.append((r, qc, bi, h))
                ust = {}

                def stage_a(u):
                    r, qc, bi, h = u
                    p0 = qc * nq * 128
                    width = nq * 128
                    if bi == 0 and h == 0:
                        qi = cnt['q'] % 2
                        cnt['q'] += 1
                        QTo = QTo0 + qi * 512
                        b, bk = pbank((0, 1))
                        for k in range(8):
                            P.I('pe', 'matmul', PS[b].v(0, [[1, width]]), wq.v(k * 128, [[1, 128]]),
                                H.v(k * NT + dil * p0 + r, [[dil, width]]), start=(k == 0), stop=(k == 7), reads=[wqk, 'H'], writes=[bk])
                        evac(qi, SCR16.v(QTo, [[1, width]]), PS[b].v(0, [[1, width]]), [bk], ['QT%d' % qi])
                        ust['q'] = (qi, QTo)
                    qi, QTo = ust['q']
                    if h == 0:
                        ni = cnt['n'] % 2
                        cnt['n'] += 1
                        ust['n'] = ni
                    ni = ust['n']
                    i = qc * nq + bi
                    si = cnt['s'] % 2
                    cnt['s'] += 1
                    stb, stk = 4 + si, 'PS%d' % (4 + si)
                    ncols = 256 if i >= 1 else 128
                    P.I('pe', 'matmul', PS[stb].v(0, [[1, 128]]), SCR16.v(KTo + dil * 128 * i + r, [[dil, 128]], p0=h * 64, pn=64),
                        SCR16.v(QTo + bi * 128, [[1, 128]], p0=h * 64, pn=64), start=True, stop=True,
                        reads=['KT', 'QT%d' % qi], writes=[stk])
                    if i >= 1:
                        P.I('pe', 'matmul', PS[stb].v(128, [[1, 128]]),
                            SCR16.v(KTo + dil * 128 * (i - 1) + r, [[dil, 128]], p0=h * 64, pn=64),
                            SCR16.v(QTo + bi * 128, [[1, 128]], p0=h * 64, pn=64), start=True, stop=True,
                            reads=['KT', 'QT%d' % qi], writes=[stk])
                    SSo = SSo0 + si * 256
                    PTo = PTo0 + si * 256
                    P.I('dve', 'scalar_tensor_tensor', SCR32.v(SSo, [[1, ncols]]), PS[stb].v(0, [[1, ncols]]), 0.125,
                        SCR32.v(BTo + h * 256, [[1, ncols]]), op0=ALU.mult, op1=ALU.add, reads=[stk, 'BT'], writes=['SS%d' % si])
                    P.I('act', 'activation', SCR16.v(PTo, [[1, ncols]]), SCR32.v(SSo, [[1, ncols]]), AF.Exp,
                        reads=['SS%d' % si], writes=['PT%d' % si])
                    return (r, i, h, si, ni, PTo)

                def stage_b(r, i, h, si, ni, PTo):
                    tcur = r * nblk + i
                    for (pb, pkey, lfn) in ((2 + ni, 'PS%d' % (2 + ni), lambda t: SCR16.v(VTo + t * 128 + h * 64, [[1, 64]])),
                                            (6 + ni, 'PS%d' % (6 + ni), lambda t: ONB.v(0, [[1, 64]]))):
                        P.I('pe', 'matmul', PS[pb].v(0, [[1, 128]], p0=h * 64, pn=64), lfn(tcur), SCR16.v(PTo, [[1, 128]]),
                            start=True, stop=(i == 0), reads=['VT', 'ONB', 'PT%d' % si], writes=[pkey])
                        if i >= 1:
                            P.I('pe', 'matmul', PS[pb].v(0, [[1, 128]], p0=h * 64, pn=64), lfn(tcur - 1),
                                SCR16.v(PTo + 128, [[1, 128]]), start=False, stop=True,
                                reads=['VT', 'ONB', 'PT%d' % si], writes=[pkey])
                    if h == 1:
                        tok0 = dil * 128 * i + r
                        an = SCR32.v(NUMo + tok0, [[dil, 128]])
                        ad = SCR32.v(DENo + tok0, [[dil, 128]])
                        if g == 0:
                            P.I('dve', 'tensor_copy', an, PS[2 + ni].v(0, [[1, 128]]), reads=['PS%d' % (2 + ni)], writes=['ACC'])
                            P.I('act', 'copy', ad, PS[6 + ni].v(0, [[1, 128]]), reads=['PS%d' % (6 + ni)], writes=['ACCD'])
                        else:
                            P.I('dve', 'tensor_tensor', an, an, PS[2 + ni].v(0, [[1, 128]]), op=ALU.add,
                                reads=['PS%d' % (2 + ni), 'ACC'], writes=['ACC'])
                            P.I('dve', 'tensor_tensor', ad, ad, PS[6 + ni].v(0, [[1, 128]]), op=ALU.add,
                                reads=['PS%d' % (6 + ni), 'ACCD'], writes=['ACCD'])

                cur = stage_a(units[0])
                for ui_ in range(len(units)):
                    nxt = stage_a(units[ui_ + 1]) if ui_ + 1 < len(units) else None
                    stage_b(*cur)
                    cur = nxt

            def attn_sample(g, hp):
                win, dil = PATTERNS[g]
                nt = 1 if g == 0 else 4
                wb, wk, offs = wload_multi([('w_in', loff, N_IN, O2 + g * 256 + hp * 128, 128, 8),
                                            ('w_in', loff, N_IN, O3 + g * 256 + hp * 128, 128, 8),
                                            ('w_in', loff, N_IN, O4 + g * 256 + hp * 128, 128, 8)])
                b, bk = pbank((0, 1, 2, 3))
                for k in range(8):
                    P.I('pe', 'matmul', PS[b].v(0, [[1, 64]]), wb.v(offs[0] + k * 128, [[1, 128]]), H.v(k * NT + TPR, [[1, 64]]),
                        start=(k == 0), stop=(k == 7), reads=[wk, 'H'], writes=[bk])
                for k in range(8):
                    P.I('pe', 'matmul', PS[b].v(64, [[1, 64]]), wb.v(offs[1] + k * 128, [[1, 128]]), H.v(k * NT + TPR, [[1, 64]]),
                        start=(k == 0), stop=(k == 7), reads=[wk, 'H'], writes=[bk])
                for k in range(8):
                    P.I('pe', 'matmul', PS[b].v(128, [[1, 128]], pn=64), H.v(k * NT + TPR, [[1, 64]]), wb.v(offs[2] + k * 128, [[1, 128]]),
                        start=(k == 0), stop=(k == 7), reads=[wk, 'H'], writes=[bk])
                P.I('dve', 'tensor_copy', SCR16.v(0, [[1, 128]]), PS[b].v(0, [[1, 128]]), reads=[bk], writes=['QKS'])
                P.I('dve', 'tensor_copy', SCR16.v(128, [[1, 128]], pn=64), PS[b].v(128, [[1, 128]], pn=64), reads=[bk], writes=['VS'])
                BTS0o, BCo, BTNo, SSNo, SSSo = 3232, 3240, 3264, 3392, 3488
                PTNo, PTSo = 6912, 6400
                for h in range(2):
                    cc = alibi_slope(g * 4 + hp * 2 + h) * dil
                    if g == 0:
                        P.I('dve', 'scalar_tensor_tensor', SCR32.v(BTNo + h * 64, [[1, 64]], pn=64), CST.v(C_DN0, [[1, 64]], pn=64), -cc,
                            CST.v(C_MN0, [[1, 64]], pn=64), op0=ALU.mult, op1=ALU.add, reads=['CST'], writes=['BTN'])
                        P.I('dve', 'scalar_tensor_tensor', SCR32.v(BTS0o + h * 4, [[1, 4]]), CST.v(C_DS0, [[1, 4]]), -cc,
                            CST.v(C_MS0, [[1, 4]]), op0=ALU.mult, op1=ALU.add, reads=['CST'], writes=['BTS'])
                    else:
                        P.I('dve', 'tensor_scalar', SCR32.v(BCo + h, [[1, 1]]), CST.v(C_DS1, [[1, 1]]), -cc, None, op0=ALU.mult,
                            reads=['CST'], writes=['BTS'])
                for h in range(2):
                    P.I('pe', 'matmul', PS[4].v(0, [[1, 64]], pn=64), SCR16.v(64, [[1, 64]], p0=h * 64, pn=64),
                        SCR16.v(0, [[1, 64]], p0=h * 64, pn=64), start=True, stop=True, reads=['QKS'], writes=['PS4'])
                    btn = SCR32.v(BTNo + h * 64, [[1, 64]], pn=64) if g == 0 else CST.v(C_MN1, [[1, 64]], pn=64)
                    P.I('dve', 'scalar_tensor_tensor', SCR32.v(SSNo, [[1, 64]], pn=64), PS[4].v(0, [[1, 64]], pn=64), 0.125, btn,
                        op0=ALU.mult, op1=ALU.add, reads=['PS4', 'BTN', 'CST'], writes=['SSN'])
                    P.I('act', 'activation', SCR16.v(PTNo, [[1, 64]], pn=64), SCR32.v(SSNo, [[1, 64]], pn=64), AF.Exp,
                        reads=['SSN'], writes=['PTN'])
                    P.I('pe', 'matmul', PS[6].v(256, [[1, 64]], p0=h * 64, pn=64), SCR16.v(128 + h * 64, [[1, 64]], pn=64),
                        SCR16.v(PTNo, [[1, 64]], pn=64), start=True, stop=True, reads=['VS', 'PTN'], writes=['PS6'])
                    P.I('pe', 'matmul', PS[7].v(256, [[1, 64]], p0=h * 64, pn=64), ONB.v(0, [[1, 64]], pn=64),
                        SCR16.v(PTNo, [[1, 64]], pn=64), start=True, stop=True, reads=['ONB', 'PTN'], writes=['PS7'])
                L = (128, 512, 512)[g]

                def stage_T(seq):
                    ci = cnt['c'] % 2
                    cnt['c'] += 1
                    Ksto, Vsto = 128 + ci * 512, 1152 + ci * 512
                    KTco, Vco = 4352 + ci * 512, 5376 + ci * 512
                    base = ((l * 16 + seq) * L) * 256 + hp * 128
                    for (nm, sto, key) in (('ck%d' % g, Ksto, 'KST%d' % ci), ('cv%d' % g, Vsto, 'VST%d' % ci)):
                        if g == 0:
                            P.dma('sp', SCR32.v(sto, [[1, 128]]), dap(dr[nm], base, [[256, 128], [1, 128]]), writes=[key])
                        elif g == 1:
                            P.dma('sp', SCR32.v(sto, [[128, 4], [1, 128]]), dap(dr[nm], base, [[4 * 256, 128], [256, 4], [1, 128]]), writes=[key])
                        else:
                            P.dma('sp', SCR32.v(sto, [[128, 4], [1, 128]]), dap(dr[nm], base, [[256, 128], [128 * 256, 4], [1, 128]]), writes=[key])
                    b, bk = pbank((0, 1, 2, 3))
                    for t in range(nt):
                        P.I('pe', 'transpose', PS[b].v(t * 128, [[1, 128]]), SCR32.v(Ksto + t * 128, [[1, 128]]), IDF(128),
                            reads=['KST%d' % ci, 'CST'], writes=[bk])
                    P.I('dve', 'tensor_copy', SCR16.v(KTco, [[1, nt * 128]]), PS[b].v(0, [[1, nt * 128]]), reads=[bk], writes=['KTC%d' % ci])
                    P.I('act', 'copy', SCR16.v(Vco, [[1, nt * 128]]), SCR32.v(Vsto, [[1, nt * 128]]), reads=['VST%d' % ci],
                        writes=['VC%d' % ci])
                    return ci, KTco, Vco

                def stage_Q(seq, h, ci, KTco, Vco):
                    si = cnt['s'] % 2
                    cnt['s'] += 1
                    stb, stk = 4 + si, 'PS%d' % (4 + si)
                    pts = SCR16.v(PTSo + si * 4, [[1, 4]])
                    if g == 0:
                        P.I('pe', 'matmul', PS[stb].v(0, [[1, 4]]), SCR16.v(KTco, [[1, 128]], p0=h * 64, pn=64),
                            SCR16.v(seq * 4, [[1, 4]], p0=h * 64, pn=64), start=True, stop=True, reads=['KTC%d' % ci, 'QKS'], writes=[stk])
                        P.I('dve', 'scalar_tensor_tensor', SCR32.v(SSSo + si * 4, [[1, 4]]), PS[stb].v(0, [[1, 4]]), 0.125,
                            SCR32.v(BTS0o + h * 4, [[1, 4]]), op0=ALU.mult, op1=ALU.add, reads=[stk, 'BTS'], writes=['SSS%d' % si])
                        P.I('act', 'activation', pts, SCR32.v(SSSo + si * 4, [[1, 4]]), AF.Exp, reads=['SSS%d' % si], writes=['PTS%d' % si])
                    else:
                        for j in range(4):
                            P.I('pe', 'matmul', PS[stb].v(j, [[1, 1]]), SCR16.v(KTco + j * 128, [[1, 128]], p0=h * 64, pn=64),
                                SCR16.v(seq * 4 + j, [[1, 1]], p0=h * 64, pn=64), start=True, stop=True,
                                reads=['KTC%d' % ci, 'QKS'], writes=[stk])
                        P.I('act', 'activation', pts, PS[stb].v(0, [[1, 4]]), AF.Exp, bias=SCR32.v(BCo + h, [[1, 1]]), scale=0.125,
                            reads=[stk, 'BTS'], writes=['PTS%d' % si])
                    return si

                def stage_V(seq, h, ci, Vco, si):
                    pts = SCR16.v(PTSo + si * 4, [[1, 4]])
                    if g == 0:
                        P.I('pe', 'matmul', PS[6].v(320 + seq * 4, [[1, 4]], p0=h * 64, pn=64), SCR16.v(Vco + h * 64, [[1, 64]]), pts,
                            start=True, stop=True, reads=['VC%d' % ci, 'PTS%d' % si], writes=['PS6'])
                        P.I('pe', 'matmul', PS[7].v(320 + seq * 4, [[1, 4]], p0=h * 64, pn=64), ONB.v(0, [[1, 64]]), pts,
                            start=True, stop=True, reads=['ONB', 'PTS%d' % si], writes=['PS7'])
                    else:
                        for j in range(4):
                            P.I('pe', 'matmul', PS[6].v(320 + seq * 4 + j, [[1, 1]], p0=h * 64, pn=64),
                                SCR16.v(Vco + j * 128 + h * 64, [[1, 64]]), SCR16.v(PTSo + si * 4 + j, [[1, 1]]),
                                start=True, stop=True, reads=['VC%d' % ci, 'PTS%d' % si], writes=['PS6'])
                            P.I('pe', 'matmul', PS[7].v(320 + seq * 4 + j, [[1, 1]], p0=h * 64, pn=64), ONB.v(0, [[1, 64]]),
                                SCR16.v(PTSo + si * 4 + j, [[1, 1]]), start=True, stop=True, reads=['ONB', 'PTS%d' % si], writes=['PS7'])

                tcur = stage_T(0)
                for seq in range(NSEQ):
                    tnext = stage_T(seq + 1) if seq + 1 < NSEQ else None
                    ci, KTco, Vco = tcur
                    s0 = stage_Q(seq, 0, ci, KTco, Vco)
                    s1 = stage_Q(seq, 1, ci, KTco, Vco)
                    stage_V(seq, 0, ci, Vco, s0)
                    stage_V(seq, 1, ci, Vco, s1)
                    tcur = tnext
                an, ad = SCR32.v(ANo, [[1, 64]]), SCR32.v(ADo, [[1, 64]])
                if g == 0:
                    P.I('dve', 'tensor_copy', an, PS[6].v(256, [[1, 64]]), reads=['PS6'], writes=['ACCS'])
                    P.I('dve', 'tensor_tensor', an, an, PS[6].v(320, [[1, 64]]), op=ALU.add, reads=['PS6', 'ACCS'], writes=['ACCS'])
                    P.I('dve', 'tensor_copy', ad, PS[7].v(256, [[1, 64]]), reads=['PS7'], writes=['ACCSD'])
                    P.I('dve', 'tensor_tensor', ad, ad, PS[7].v(320, [[1, 64]]), op=ALU.add, reads=['PS7', 'ACCSD'], writes=['ACCSD'])
                else:
                    for (a_, pb, k_, ka) in ((an, 6, 'PS6', 'ACCS'), (ad, 7, 'PS7', 'ACCSD')):
                        P.I('dve', 'tensor_tensor', a_, a_, PS[pb].v(256, [[1, 64]]), op=ALU.add, reads=[k_, ka], writes=[ka])
                        P.I('dve', 'tensor_tensor', a_, a_, PS[pb].v(320, [[1, 64]]), op=ALU.add, reads=[k_, ka], writes=[ka])

            for hp in range(2):
                P.phase = 'attn_p'
                for g in range(3):
                    attn_prompt(g, hp)
                P.barrier()
                P.phase = 'attn_s'
                if KSTOP != 'NOS':
                    for g in range(3):
                        attn_sample(g, hp)
                else:
                    P.I('pool', 'memset', SCR32.v(ANo, [[1, 64]]), 0.0, writes=['ACCS'])
                    P.I('pool', 'memset', SCR32.v(ADo, [[1, 64]]), 1.0, writes=['ACCSD'])
                P.I('dve', 'reciprocal', SCR32.v(DENo, [[1, 2048]]), SCR32.v(DENo, [[1, 2048]]), reads=['ACCD'], writes=['ACCD'])
                P.I('dve', 'tensor_tensor', SCR16.v(ATTo, [[1, 2048]]), SCR32.v(NUMo, [[1, 2048]]), SCR32.v(DENo, [[1, 2048]]),
                    op=ALU.mult, reads=['ACC', 'ACCD'], writes=['ATT'])
                P.I('dve', 'reciprocal', SCR32.v(ADo, [[1, 64]]), SCR32.v(ADo, [[1, 64]]), reads=['ACCSD'], writes=['ACCSD'])
                P.I('dve', 'tensor_tensor', SCR16.v(ATTo + 2048, [[1, 64]]), SCR32.v(ANo, [[1, 64]]), SCR32.v(ADo, [[1, 64]]),
                    op=ALU.mult, reads=['ACCS', 'ACCSD'], writes=['ATT'])
                P.phase = 'attn_out'
                gated_branch(O5 + 2048, lambda m, hp=hp: ('w_att', l * 256 * 1024 + hp * 128 * 1024, 1024, m * 128, 128, 1), 1,
                             lambda k, t0, w: SCR16.v(ATTo + t0, [[1, w]]), ['ATT'], False)
                P.barrier()

            P.phase = 'ssm'
            if KSTOP == 'B0':
                raise _StopMix()
            YTo, ESTo, TMSo, CAo = 0, 8448, 10496, 12544
            SHo = 6784
            YGo = 17984
            PPo = 9264
            loff = l * 1024 * N_IN
            for ch in range(4):
                wb, wk = wload('w_in', loff, N_IN, O1 + ch * 128, 128, 8)
                for (t0, w) in TILES:
                    b, bk = pbank(DENSE)
                    for k in range(8):
                        P.I('pe', 'matmul', PS[b].v(0, [[1, w]]), wb.v(k * 128, [[1, 128]]), H.v(k * NT + t0, [[1, w]]),
                            start=(k == 0), stop=(k == 7), reads=[wk, 'H'], writes=[bk])
                    if t0 < TPR:
                        evac(b, SCR16.v(YTo + ch * NT + t0 // 8, [[256, 8], [1, 64]]), PS[b].v(0, [[1, 8], [8, 64]]), [bk], ['YT%d' % ch])
                    else:
                        evac(b, SCR16.v(YTo + ch * NT + t0, [[1, w]]), PS[b].v(0, [[1, w]]), [bk], ['YT%d' % ch])
            P.barrier()

            def slot(k, n=4):
                return SCR32.v(PPo + 4 * k, [[1, n]])
            S_AR, S_AI, S_DT, S_MAG, S_C, S_S, S_ABR, S_ABI, S_FR, S_FI, S_T1, S_T2, S_T3, S_T4 = range(14)
            S_P = 14
            S_W = 32
            S_SQ = 48
            PWo = PPo + 4 * 56
            BSTo = PWo + 128
            CREo = BSTo + 256
            KBo = CREo + 128
            DMo = KBo + 128
            H0o = DMo + 4
            S4o = H0o + 128
            assert S4o + 128 <= SCRB // 4
            dv = lambda *a, **k: P.I('dve', *a, **k)
            KK = ['PP']

            def tt(o, a, b_, op):
                dv('tensor_tensor', o, a, b_, op=op, reads=KK, writes=KK)

            def cmul(or_, oi_, ar_, ai_, br_, bi_, t1, t2):
                tt(t1, ar_, br_, ALU.mult)
                tt(t2, ai_, bi_, ALU.mult)
                tt(or_, t1, t2, ALU.subtract)
                tt(t1, ar_, bi_, ALU.mult)
                tt(t2, ai_, br_, ALU.mult)
                tt(oi_, t1, t2, ALU.add)

            for ch in range(4):
                ukey = 'YT%d' % ch
                P.phase = 'ssm_ld'
                P.dma('sp', RS.v(0, [[1, 128]], pn=4), dap(dr['ssm_a_re'], l * 2048 + ch * 512, [[128, 4], [1, 128]]), writes=['STA'])
                P.dma('sp', RS.v(128, [[1, 128]], pn=4), dap(dr['ssm_a_im'], l * 2048 + ch * 512, [[128, 4], [1, 128]]), writes=['STA2'])
                for ri, nm in enumerate(('ssm_c_re', 'ssm_c_im')):
                    for q in range(4):
                        P.dma('sp', RS.v(256 + ri * 128, [[64, 2], [1, 64]], p0=q * 16, pn=16),
                              dap(dr[nm], l * 32768 + (8 * ch + 2 * q) * 1024, [[64, 16], [1024, 2], [1, 64]]), writes=['STC%d%d' % (ri, q)])
                for ri, nm in enumerate(('st_re', 'st_im')):
                    for q in range(4):
                        P.dma('sp', TMPA.v(ri * 128, [[1, 128]], p0=q * 16, pn=16),
                              dap(dr[nm], l * 16 * 2048 + (4 * ch + q) * 128, [[2048, 16], [1, 128]]), writes=['STH%d%d' % (ri, q)])
                for g2 in range(2):
                    P.dma('sp', SCR32.v(PPo + 4 * S_DT, [[1, 4]], p0=g2 * 64, pn=64),
                          dap(dr['ssm_log_dt'], l * 32 + 8 * ch + g2, [[0, 64], [2, 4]]), writes=['PPDT%d' % g2], allow_slow_non_contiguous=True)
                P.I('pool', 'memset', SCR32.v(BSTo, [[1, 256]]), 0.0, writes=['BST00', 'BST01', 'BST10', 'BST11'])
                for g2 in range(2):
                    for ri, nm in enumerate(('ssm_b_re', 'ssm_b_im')):
                        P.dma('sp', SCR32.v(BSTo + ri * 128 + g2 * 16, [[32, 4], [1, 16]], p0=g2 * 64, pn=64),
                              dap(dr[nm], l * 32768 + (8 * ch + g2) * 1024, [[16, 64], [2048, 4], [1, 16]]), writes=['BST%d%d' % (g2, ri)],
                              allow_slow_non_contiguous=True)
                P.dma('sp', SCR32.v(DMo, [[1, 1]]), dap(dr['ssm_d'], l * 512 + ch * 128, [[1, 128], [1, 1]]), writes=['PPD'],
                      allow_slow_non_contiguous=True)
                tb_, tbk = pbank((5, 6))
                P.I('pe', 'transpose', PS[tb_].v(0, [[1, 4]]), RS.v(0, [[1, 128]], pn=4), IDF(4), reads=['STA', 'CST'], writes=[tbk])
                P.I('pe', 'transpose', PS[tb_].v(4, [[1, 4]]), RS.v(128, [[1, 128]], pn=4), IDF(4), reads=['STA2', 'CST'], writes=[tbk])
                for ri in range(2):
                    P.I('pe', 'transpose', PS[tb_].v(64 + ri * 64, [[1, 64]]), RS.v(256 + ri * 128, [[1, 128]], pn=64), IDF(64),
                        reads=['STC%d%d' % (ri, q) for q in range(4)] + ['CST'], writes=[tbk])
                    P.I('pe', 'transpose', PS[tb_].v(192 + ri * 64, [[1, 64]]), TMPA.v(ri * 128, [[1, 128]], pn=64), IDF(64),
                        reads=['STH%d%d' % (ri, q) for q in range(4)] + ['CST'], writes=[tbk])
                LK = [tbk, 'PPDT0', 'PPDT1', 'PPD', 'BST00', 'BST01', 'BST10', 'BST11']
                dv('tensor_copy', SCR32.v(PPo, [[1, 8]]), PS[tb_].v(0, [[1, 8]]), reads=LK + KK, writes=KK)
                dv('tensor_copy', SCR32.v(CREo, [[1, 64]]), PS[tb_].v(64, [[1, 64]]), reads=LK + KK, writes=KK)
                dv('tensor_scalar', SCR32.v(CREo + 64, [[1, 64]]), PS[tb_].v(128, [[1, 64]]), -1.0, None, op0=ALU.mult, reads=LK + KK, writes=KK)
                for ri in range(2):
                    dv('tensor_copy', SCR32.v(H0o + ri * 16, [[32, 4], [1, 16]]), PS[tb_].v(192 + ri * 64, [[16, 4], [1, 16]]),
                       reads=LK + KK, writes=KK)
                AR, AI, DT, MAG, CC, SS_, ABR, ABI, FR, FI, T1, T2, T3, T4 = [slot(k) for k in range(14)]
                P.I('act', 'activation', DT, DT, AF.Exp, reads=KK, writes=KK)
                tt(T1, DT, AR, ALU.mult)
                P.I('act', 'activation', MAG, T1, AF.Exp, reads=KK, writes=KK)
                tt(T2, DT, AI, ALU.mult)
                P.I('act', 'activation', CC, T2, AF.Sin, bias=math.pi / 2, scale=1.0 / 32, reads=KK, writes=KK)
                P.I('act', 'activation', SS_, T2, AF.Sin, scale=1.0 / 32, reads=KK, writes=KK)
                for _ in range(5):
                    tt(T1, CC, CC, ALU.mult)
                    tt(T2, SS_, SS_, ALU.mult)
                    tt(T3, CC, SS_, ALU.mult)
                    tt(CC, T1, T2, ALU.subtract)
                    dv('tensor_scalar', SS_, T3, 2.0, None, op0=ALU.mult, reads=KK, writes=KK)
                tt(ABR, MAG, CC, ALU.mult)
                tt(ABI, MAG, SS_, ALU.mult)
                tt(T1, AR, AR, ALU.mult)
                tt(T2, AI, AI, ALU.mult)
                tt(T1, T1, T2, ALU.add)
                dv('reciprocal', T1, T1, reads=KK, writes=KK)
                dv('tensor_scalar', T2, ABR, -1.0, None, op0=ALU.add, reads=KK, writes=KK)
                tt(T3, T2, AR, ALU.mult)
                tt(T4, ABI, AI, ALU.mult)
                tt(T3, T3, T4, ALU.add)
                tt(FR, T3, T1, ALU.mult)
                tt(T3, ABI, AR, ALU.mult)
                tt(T4, T2, AI, ALU.mult)
                tt(T3, T3, T4, ALU.subtract)
                tt(FI, T3, T1, ALU.mult)
                Pr = lambda j: slot(S_P + 2 * j)
                Pi = lambda j: slot(S_P + 2 * j + 1)
                Wr = lambda j: slot(S_W + 2 * j)
                Wi = lambda j: slot(S_W + 2 * j + 1)
                P.I('pool', 'memset', Pr(0), 1.0, reads=KK, writes=KK)
                P.I('pool', 'memset', Pi(0), 0.0, reads=KK, writes=KK)
                dv('tensor_copy', Pr(1), ABR, reads=KK, writes=KK)
                dv('tensor_copy', Pi(1), ABI, reads=KK, writes=KK)
                def pv(base_slot, j0, n):
                    return SCR32.v(PPo + 4 * (base_slot + 2 * j0), [[8, n], [1, 4]])

                def bcn(sl_, n):
                    return SCR32.v(PPo + 4 * sl_, [[0, n], [1, 4]])
                cmul(Pr(2), Pi(2), Pr(1), Pi(1), ABR, ABI, T1, T2)
                for (j0, n_, src0) in ((3, 2, 1), (5, 4, 1)):
                    mj = j0 - src0
                    cmul(pv(S_P, j0, n_), pv(S_P + 1, j0, n_), pv(S_P, src0, n_), pv(S_P + 1, src0, n_),
                         bcn(S_P + 2 * mj, n_), bcn(S_P + 2 * mj + 1, n_), TMPA.v(0, [[4, n_], [1, 4]]), TMPA.v(64, [[4, n_], [1, 4]]))
                cmul(pv(S_W, 0, 8), pv(S_W + 1, 0, 8), pv(S_P, 0, 8), pv(S_P + 1, 0, 8), bcn(S_FR, 8), bcn(S_FI, 8),
                     TMPA.v(0, [[4, 8], [1, 4]]), TMPA.v(64, [[4, 8], [1, 4]]))
                SQr = lambda k: slot(S_SQ + 2 * k)
                SQi = lambda k: slot(S_SQ + 2 * k + 1)
                prev_r, prev_i = Pr(8), Pi(8)
                for k in range(7):
                    cmul(SQr(k), SQi(k), prev_r, prev_i, prev_r, prev_i, T1, T2)
                    prev_r, prev_i = SQr(k), SQi(k)
                bc = lambda sl_, n: SCR32.v(PPo + 4 * sl_, [[1, 4], [0, n]])
                S_PW3 = S_SQ + 14
                dv('tensor_copy', slot(S_PW3), Pr(8), reads=KK, writes=KK)
                dv('tensor_copy', slot(S_PW3 + 1), Pi(8), reads=KK, writes=KK)
                dv('tensor_copy', slot(S_PW3 + 2), SQr(0), reads=KK, writes=KK)
                dv('tensor_copy', slot(S_PW3 + 3), SQi(0), reads=KK, writes=KK)
                cmul(slot(S_PW3 + 4), slot(S_PW3 + 5), SQr(0), SQi(0), Pr(8), Pi(8), T1, T2)
                P.phase = 'ssm_E'
                BRE = SCR32.v(BSTo, [[32, 4], [1, 32]])
                BIM = SCR32.v(BSTo + 128, [[32, 4], [1, 32]])
                ER = TMPA.v(0, [[32, 4], [1, 32]])
                EI = TMPA.v(128, [[32, 4], [1, 32]])
                ET = TMPA.v(256, [[32, 4], [1, 32]])
                P.I('dve', 'memset', PS[7].v(0, [[1, 512]]), 0.0, reads=['PS7'], writes=['PS7'])
                for j in range(8):
                    wr_, wi_ = bc(S_W + 2 * j, 32), bc(S_W + 2 * j + 1, 32)
                    dv('tensor_tensor', ER, BRE, wr_, op=ALU.mult, reads=KK + ['TMPA'], writes=['TMPA'])
                    dv('tensor_tensor', ET, BIM, wi_, op=ALU.mult, reads=KK + ['TMPA'], writes=['TMPA'])
                    dv('tensor_tensor', ER, ER, ET, op=ALU.subtract, reads=['TMPA'], writes=['TMPA'])
                    dv('tensor_tensor', EI, BIM, wr_, op=ALU.mult, reads=KK + ['TMPA'], writes=['TMPA'])
                    dv('tensor_tensor', ET, BRE, wi_, op=ALU.mult, reads=KK + ['TMPA'], writes=['TMPA'])
                    dv('tensor_tensor', EI, EI, ET, op=ALU.add, reads=['TMPA'], writes=['TMPA'])
                    b, bk = pbank((5, 6))
                    P.I('pe', 'transpose', PS[b].v(0, [[1, 128]]), TMPA.v(0, [[1, 128]]), IDF(128), reads=['TMPA', 'CST'], writes=[bk])
                    P.I('pe', 'transpose', PS[b].v(128, [[1, 128]]), TMPA.v(128, [[1, 128]]), IDF(128), reads=['TMPA', 'CST'], writes=[bk])
                    P.I('act', 'copy', SCR16.v(ESTo + j * 256, [[1, 256]]), PS[b].v(0, [[1, 256]]), reads=[bk], writes=['EST'])
                    P.I('pe', 'matmul', PS[7].v(j * 64, [[1, 64]]), TMPA.v(0, [[1, 128]]), SCR32.v(CREo, [[1, 64]]), start=True, stop=False,
                        reads=['TMPA'] + KK, writes=['PS7'])
                    P.I('pe', 'matmul', PS[7].v(j * 64, [[1, 64]]), TMPA.v(128, [[1, 128]]), SCR32.v(CREo + 64, [[1, 64]]), start=False, stop=True,
                        reads=['TMPA'] + KK, writes=['PS7'])
                dv('tensor_tensor', TMPA.v(0, [[64, 8], [16, 4], [1, 16]]), PS[7].v(0, [[64, 8], [16, 4], [1, 16]]),
                   CST.v(C_QM, [[0, 8], [1, 4], [0, 16]]), op=ALU.mult, reads=['PS7', 'CST', 'TMPA'], writes=['TMPA'])
                dv('tensor_reduce', SCR32.v(KBo, [[16, 8], [1, 16]]), TMPA.v(0, [[64, 8], [1, 16], [16, 4]]), axis=AX.X, op=ALU.add,
                   reads=['TMPA'] + KK, writes=KK)
                dv('scalar_tensor_tensor', SCR32.v(KBo, [[1, 16]]), CST.v(C_DELTA, [[1, 16]]), SCR32.v(DMo, [[1, 1]]), SCR32.v(KBo, [[1, 16]]),
                   op0=ALU.mult, op1=ALU.add, reads=KK + ['CST'], writes=KK)
                P.I('pool', 'memset', SCR16.v(TMSo, [[1, 2048]]), 0.0, reads=['TMS'], writes=['TMS'])
                for g2s in range(2):
                    for s_ in range(8):
                        n_ = (8 - s_) * 16
                        dv('tensor_scalar', SCR16.v(TMSo + (g2s * 8 + s_) * 128 + s_ * 16, [[1, n_]]), SCR32.v(KBo, [[1, n_]]),
                           CST.v(C_MG2 + g2s, [[1, 1]]), None, op0=ALU.mult, reads=KK + ['CST', 'TMS'], writes=['TMS'])
                CRE4 = SCR32.v(CREo, [[16, 4], [0, 8], [1, 16]])
                NCI4 = SCR32.v(CREo + 64, [[16, 4], [0, 8], [1, 16]])
                PR4 = SCR32.v(PPo + 4 * (S_P + 2), [[1, 4], [8, 8], [0, 16]])
                PI4 = SCR32.v(PPo + 4 * (S_P + 3), [[1, 4], [8, 8], [0, 16]])
                A1 = TMPA.v(0, [[128, 4], [16, 8], [1, 16]])
                A2 = RS.v(0, [[128, 4], [16, 8], [1, 16]])
                CK = KK + ['TMPA', 'RS']
                dv('tensor_tensor', A1, CRE4, PR4, op=ALU.mult, reads=CK, writes=['TMPA'])
                dv('tensor_tensor', A2, NCI4, PI4, op=ALU.mult, reads=CK, writes=['RS'])
                dv('tensor_tensor', SCR16.v(CAo, [[256, 4], [16, 8], [1, 16]]), A1, A2, op=ALU.add, reads=['TMPA', 'RS', 'CA'], writes=['CA'])
                dv('tensor_tensor', A1, NCI4, PR4, op=ALU.mult, reads=CK, writes=['TMPA'])
                dv('tensor_tensor', A2, CRE4, PI4, op=ALU.mult, reads=CK, writes=['RS'])
                dv('tensor_tensor', SCR16.v(CAo + 128, [[256, 4], [16, 8], [1, 16]]), A1, A2, op=ALU.subtract, reads=['TMPA', 'RS', 'CA'], writes=['CA'])
                P.phase = 'ssm_inj'
                P.I('pool', 'memset', SCR32.v(SHo, [[276, 8], [1, 1]]), 0.0, reads=['SH'], writes=['SH'])
                for q in range(4):
                    for ri in range(2):
                        b, bk = pbank((5, 6))
                        for s_ in range(8):
                            P.I('pe', 'matmul', PS[b].v(0, [[1, 256]]), SCR16.v(ESTo + ((7 - s_) * 2 + ri) * 128, [[1, 128]], p0=32 * q, pn=32),
                                SCR16.v(YTo + ch * NT + s_ * 256, [[1, 256]], p0=32 * q, pn=32), start=(s_ == 0), stop=(s_ == 7),
                                tile_position=(32 * q, 0), reads=['EST', ukey], writes=[bk])
                        for s_ in range(4):
                            P.I('pe', 'matmul', PS[b].v(256, [[1, 16]]), SCR16.v(ESTo + ((3 - s_) * 2 + ri) * 128, [[1, 128]], p0=32 * q, pn=32),
                                SCR16.v(YTo + ch * NT + TPR + s_, [[4, 16]], p0=32 * q, pn=32), start=(s_ == 0), stop=(s_ == 3),
                                tile_position=(32 * q, 0), reads=['EST', ukey], writes=[bk])
                        P.I('act', 'copy', SCR32.v(SHo + (q * 2 + ri) * 276 + 1, [[1, 256]]), PS[b].v(0, [[1, 256]]), reads=[bk], writes=['SH'])
                        P.I('act', 'copy', SCR32.v(S4o + (q * 2 + ri) * 16, [[1, 16]]), PS[b].v(256, [[1, 16]]), reads=[bk], writes=KK)
                KS = ['SH']

                def hv(ri, col0, dims):
                    return SCR32.v(SHo + ri * 276 + col0, [[552, 4]] + dims)

                def st2(o, a, b_, op):
                    dv('tensor_tensor', o, a, b_, op=op, reads=KS + KK + ['TMPA', 'RS'], writes=KS)

                def tmp2(o, a, b_, key):
                    dv('tensor_tensor', o, a, b_, op=ALU.mult, reads=KS + KK + [key], writes=[key])
                a8r, a8i = bc(S_P + 16, 64), bc(S_P + 17, 64)
                U1 = TMPA.v(0, [[64, 4], [1, 64]])
                U2 = RS.v(0, [[64, 4], [1, 64]])
                for jj in range(1, 4):
                    hr, hi = hv(0, 1 + jj, [[4, 64]]), hv(1, 1 + jj, [[4, 64]])
                    pr_, pi_ = hv(0, jj, [[4, 64]]), hv(1, jj, [[4, 64]])
                    tmp2(U1, a8r, pr_, 'TMPA')
                    tmp2(U2, a8i, pi_, 'RS')
                    st2(hr, hr, U1, ALU.add)
                    st2(hr, hr, U2, ALU.subtract)
                    tmp2(U1, a8r, pi_, 'TMPA')
                    tmp2(U2, a8i, pr_, 'RS')
                    st2(hi, hi, U1, ALU.add)
                    st2(hi, hi, U2, ALU.add)
                for di, d_ in enumerate((1, 2, 4, 8, 16, 32)):
                    n_ = 64 - d_
                    adr, adi = bc(S_SQ + 2 * (di + 1), n_), bc(S_SQ + 2 * (di + 1) + 1, n_)
                    sr, si_ = hv(0, 4, [[4, n_]]), hv(1, 4, [[4, n_]])
                    dr_, di2 = hv(0, 4 * d_ + 4, [[4, n_]]), hv(1, 4 * d_ + 4, [[4, n_]])
                    q1, q2 = TMPA.v(0, [[64, 4], [1, n_]]), TMPA.v(256, [[64, 4], [1, n_]])
                    q3, q4 = RS.v(0, [[64, 4], [1, n_]]), RS.v(256, [[64, 4], [1, n_]])
                    tmp2(q1, adr, sr, 'TMPA')
                    tmp2(q2, adi, si_, 'TMPA')
                    tmp2(q3, adr, si_, 'RS')
                    tmp2(q4, adi, sr, 'RS')
                    st2(dr_, dr_, q1, ALU.add)
                    st2(dr_, dr_, q2, ALU.subtract)
                    st2(di2, di2, q3, ALU.add)
                    st2(di2, di2, q4, ALU.add)
                for q in range(4):
                    def hq(ri, col0, dims):
                        return SCR32.v(SHo + (q * 2 + ri) * 276 + col0, dims)
                    er = hq(0, 4, [[4, 63], [0, 3]])
                    ei = hq(1, 4, [[4, 63], [0, 3]])
                    hr = hq(0, 5, [[4, 63], [1, 3]])
                    hi = hq(1, 5, [[4, 63], [1, 3]])
                    pwr = SCR32.v(PPo + 4 * S_PW3 + q, [[0, 63], [8, 3]])
                    pwi = SCR32.v(PPo + 4 * (S_PW3 + 1) + q, [[0, 63], [8, 3]])
                    F1 = TMPA.v(0, [[3, 63], [1, 3]])
                    F2 = RS.v(0, [[3, 63], [1, 3]])
                    tmp2(F1, pwr, er, 'TMPA')
                    tmp2(F2, pwi, ei, 'RS')
                    st2(hr, hr, F1, ALU.add)
                    st2(hr, hr, F2, ALU.subtract)
                    tmp2(F1, pwr, ei, 'TMPA')
                    tmp2(F2, pwi, er, 'RS')
                    st2(hi, hi, F1, ALU.add)
                    st2(hi, hi, F2, ALU.add)
                for ri in range(2):
                    dv('tensor_copy', SCR32.v(SHo + ri * 276 + 257, [[552, 4], [1, 16]]), SCR32.v(H0o + ri * 16, [[32, 4], [1, 16]]),
                       reads=KK + KS, writes=KS)
                h0r, h0i = SCR32.v(H0o, [[32, 4], [1, 16]]), SCR32.v(H0o + 16, [[32, 4], [1, 16]])
                s4r, s4i = SCR32.v(S4o, [[32, 4], [1, 16]]), SCR32.v(S4o + 16, [[32, 4], [1, 16]])
                p4r, p4i = bc(S_P + 8, 16), bc(S_P + 9, 16)
                W1 = TMPA.v(0, [[16, 4], [1, 16]])
                for (o_, x1, y1, x2, y2, op2) in ((s4r, p4r, h0r, p4i, h0i, ALU.subtract), (s4i, p4r, h0i, p4i, h0r, ALU.add)):
                    dv('tensor_tensor', W1, x1, y1, op=ALU.mult, reads=KK + ['TMPA'], writes=['TMPA'])
                    dv('tensor_tensor', o_, o_, W1, op=ALU.add, reads=KK + ['TMPA'], writes=KK)
                    dv('tensor_tensor', W1, x2, y2, op=ALU.mult, reads=KK + ['TMPA'], writes=['TMPA'])
                    dv('tensor_tensor', o_, o_, W1, op=op2, reads=KK + ['TMPA'], writes=KK)
                P.phase = 'ssm_so'
                ob, obk = pbank((5, 6))
                for ri in range(2):
                    P.I('pe', 'transpose', PS[ob].v(ri * 128, [[1, 128]], pn=4), SCR32.v(SHo + ri * 276 + 256, [[552, 4]]), IDF(128),
                        reads=KS + ['CST'], writes=[obk])
                    dv('tensor_copy', RS.v(256 + ri * 64, [[16, 4], [1, 16]]), SCR32.v(S4o + ri * 16, [[32, 4], [1, 16]]),
                       reads=KK, writes=['STC%d%d' % (r_, q_) for r_ in range(2) for q_ in range(4)])
                    P.I('pe', 'transpose', PS[ob].v(256 + ri * 128, [[1, 128]], pn=64), RS.v(256 + ri * 64, [[1, 64]]), IDF(128),
                        reads=['STC00', 'CST'], writes=[obk])
                dv('tensor_copy', RS.v(0, [[1, 256]], pn=4), PS[ob].v(0, [[1, 256]], pn=4), reads=[obk, 'RS', 'STA', 'STA2'], writes=['STO1'])
                dv('tensor_copy', TMPA.v(0, [[1, 256]], pn=64), PS[ob].v(256, [[1, 256]], pn=64), reads=[obk, 'TMPA'], writes=['STO2', 'TMPA'])
                for ri, (np_, ns_) in enumerate((('ssm_re_p', 'ssm_re_s'), ('ssm_im_p', 'ssm_im_s'))):
                    P.dma('sp', dap(dr[np_], l * 2048 + ch * 512, [[128, 4], [1, 128]]), RS.v(ri * 128, [[1, 128]], pn=4),
                          reads=['STO1', 'RS'], writes=['OUT' + np_], out=True)
                    for q in range(4):
                        P.dma('sp', dap(dr[ns_], l * 16 * 2048 + (4 * ch + q) * 128, [[2048, 16], [1, 128]]),
                              TMPA.v(ri * 128, [[1, 128]], p0=q * 16, pn=16), reads=['STO2', 'TMPA'], writes=['OUT' + ns_], out=True)
                dv('tensor_copy', SCR16.v(2 * SHo, [[552, 8], [1, 273]]), SCR32.v(SHo, [[276, 8], [1, 273]]), reads=KS, writes=['SHB', 'SH'])
                P.phase = 'ssm_Y'
                for b_ in range(5):
                    P.I('dve', 'memset', PS[b_].v(0, [[1, 512]]), 0.0, reads=['PS%d' % b_], writes=['PS%d' % b_])
                def emit_Y(gl):
                    q, g2 = gl // 2, gl % 2
                    yb, ybk = pbank((5, 6))
                    for s_ in range(8):
                        P.I('pe', 'matmul', PS[yb].v(0, [[1, 256]]), SCR16.v(TMSo + (g2 * 8 + s_) * 128, [[1, 128]], p0=32 * q, pn=32),
                            SCR16.v(YTo + ch * NT + s_ * 256, [[1, 256]], p0=32 * q, pn=32), start=(s_ == 0), stop=False,
                            tile_position=(32 * q, 0), reads=['TMS', ukey], writes=[ybk])
                    for ri in range(2):
                        P.I('pe', 'matmul', PS[yb].v(0, [[1, 256]]), SCR16.v(CAo + (q * 2 + ri) * 128, [[1, 128]], p0=64 * g2, pn=64),
                            SCR16.v(2 * SHo + (q * 2 + ri) * 552, [[1, 256]], p0=64 * g2, pn=64), start=False, stop=(ri == 1),
                            sync_prev=(ri == 0), reads=['CA', 'SHB'], writes=[ybk])
                    for s_ in range(4):
                        P.I('pe', 'matmul', PS[yb].v(256, [[1, 16]]), SCR16.v(TMSo + (g2 * 8 + s_) * 128, [[1, 128]], p0=32 * q, pn=32),
                            SCR16.v(YTo + ch * NT + TPR + s_, [[4, 16]], p0=32 * q, pn=32), start=(s_ == 0), stop=False,
                            tile_position=(32 * q, 0), sync_prev=(s_ == 0), reads=['TMS', ukey], writes=[ybk])
                    for ri in range(2):
                        P.I('pe', 'matmul', PS[yb].v(256, [[1, 16]]), SCR16.v(CAo + (q * 2 + ri) * 128, [[1, 128]], p0=64 * g2, pn=64),
                            SCR16.v(2 * SHo + (q * 2 + ri) * 552 + 257, [[1, 16]], p0=64 * g2, pn=64), start=False, stop=(ri == 1),
                            sync_prev=(ri == 0), reads=['CA', 'SHB'], writes=[ybk])
                    return yb, ybk

                def emit_post(gl, yb, ybk):
                    yg = SCR16.v(YGo + (gl % 2) * 272, [[1, 272]])
                    P.I('act', 'activation', yg, PS[yb].v(0, [[1, 272]]), AF.Gelu_apprx_tanh, reads=[ybk], writes=['YG%d' % (gl % 2)])
                    for t in range(8):
                        sel = SELB.v(((t % 2) * 8 + gl) * 128, [[1, 128]], p0=32 * (t // 2), pn=32)
                        for tb in range(4):
                            P.I('pe', 'matmul', PS[tb].v(t, [[8, 64]]), sel, SCR16.v(YGo + (gl % 2) * 272 + 64 * tb, [[1, 64]], p0=32 * (t // 2), pn=32),
                                start=False, stop=False, skip_group_check=True, tile_position=(32 * (t // 2), 0), sync_prev=(tb == 0 and t % 2 == 0),
                                reads=['SELB', 'YG%d' % (gl % 2)], writes=['PS%d' % tb])
                        if t < 4:
                            P.I('pe', 'matmul', PS[4].v(t, [[4, 16]]), sel, SCR16.v(YGo + (gl % 2) * 272 + 256, [[1, 16]], p0=32 * (t // 2), pn=32),
                                start=False, stop=False, skip_group_check=True, tile_position=(32 * (t // 2), 0),
                                reads=['SELB', 'YG%d' % (gl % 2)], writes=['PS4'])

                ycur = emit_Y(0)
                for gl in range(8):
                    ynext = emit_Y(gl + 1) if gl < 7 else None
                    emit_post(gl, *ycur)
                    ycur = ynext
                for tb in range(4):
                    evac(tb, SCR16.v(YTo + ch * NT + tb * 512, [[1, 512]]), PS[tb].v(0, [[1, 512]]), ['PS%d' % tb], [ukey])
                evac(0, SCR16.v(YTo + ch * NT + TPR, [[1, 64]]), PS[4].v(0, [[1, 64]]), ['PS4'], [ukey])
                P.barrier()
                if KSTOP == 'B1':
                    raise _StopMix()

            P.phase = 'ssm_glu'
            for m in range(8):
                wb, wk, offs = wload_multi([('w_in', loff, N_IN, O5 + 1024 + m * 128, 128, 8),
                                            ('w_ssm_glu', l * 512 * 2048, 2048, m * 128, 128, 4),
                                            ('w_ssm_glu', l * 512 * 2048, 2048, 1024 + m * 128, 128, 4)])
                for (t0, w) in TILES:
                    bg, kg = pbank(DENSE)
                    b1, k1 = pbank(DENSE)
                    b2, k2 = pbank(DENSE)
                    for k in range(8):
                        P.I('pe', 'matmul', PS[bg].v(0, [[1, w]]), wb.v(offs[0] + k * 128, [[1, 128]]), H.v(k * NT + t0, [[1, w]]),
                            start=(k == 0), stop=(k == 7), reads=[wk, 'H'], writes=[kg])
                    for (bb_, kb_, oo) in ((b1, k1, offs[1]), (b2, k2, offs[2])):
                        for k in range(4):
                            P.I('pe', 'matmul', PS[bb_].v(0, [[1, w]]), wb.v(oo + k * 128, [[1, 128]]), SCR16.v(YTo + k * NT + t0, [[1, w]]),
                                start=(k == 0), stop=(k == 3), reads=[wk, 'YT0', 'YT1', 'YT2', 'YT3'], writes=[kb_])
                    TA = TMPA.v(0, [[1, w]])
                    TB = RS.v(0, [[1, w]])
                    P.I('act', 'activation', TA, PS[bg].v(0, [[1, w]]), AF.Sigmoid, reads=[kg], writes=['TMPA'])
                    P.I('act', 'activation', TB, PS[b2].v(0, [[1, w]]), AF.Sigmoid, reads=[k2], writes=['RS'])
                    dv('tensor_tensor', TB, TB, PS[b1].v(0, [[1, w]]), op=ALU.mult, reads=['RS', k1], writes=['RS'])
                    dv('tensor_tensor', TA, TA, TB, op=ALU.mult, reads=['TMPA', 'RS'], writes=['TMPA'])
                    P.I('dve', 'tensor_tensor', MG.v(m * NT + t0, [[1, w]]), MG.v(m * NT + t0, [[1, w]]), TA, op=ALU.add,
                        reads=['TMPA', 'MG'], writes=['MG'])
            P.barrier()
            if KSTOP == 'A7':
                raise _StopMix()
            P.phase = 'wout'
            MOoD = 2048

            def MOapD(k, w):
                if w is None:
                    return SCR32.v(MOoD + k * 512, [[4, 16], [1, 4]])
                return SCR32.v(MOoD + k * 512, [[1, w]])
            wo_ = [wload('w_out', l * 1024 * 1024, 1024, hh * 512, 512, 8) for hh in range(2)]
            for (t0, w) in TILES:
                for m in range(8):
                    wb, wk = wo_[m // 4]
                    b, bk = pbank(DENSE)
                    for k in range(8):
                        P.I('pe', 'matmul', PS[b].v(0, [[1, w]]), wb.v(k * 512 + (m % 4) * 128, [[1, 128]]), MG.v(k * NT + t0, [[1, w]]),
                            start=(k == 0), stop=(k == 7), reads=[wk, 'MG'], writes=[bk])
                    evac(m, SCR32.v(MOoD + m * 512, [[1, w]]), PS[b].v(0, [[1, w]]), [bk], ['MO'])
                if KSTOP != 'D1':
                    post_norm_add(MOapD, t0, w, GMt, 6144)
            P.barrier()
        try:
            mixer()
        except _StopMix:
            P.barrier()

        P.phase = 'ffn'
        pre_norm(AFt, BFv)
        P.barrier()
        FW = 1088
        MOoff = 5856

        def Fap(j, c, w):
            if j < 15:
                return MG.v(j * FW + c, [[1, w]])
            return SCR16.v(4096 + (j - 15) * FW + c, [[1, w]])

        def MOap(k, w):
            if w is None:
                return SCR32.v(MOoff + k * 512, [[4, 16], [1, 4]])
            return SCR32.v(MOoff + k * 512, [[1, w]])

        for half in range(2):
            htiles = TILES[0:2] if half == 0 else TILES[2:5]
            hc0 = htiles[0][0]
            for j in range(22):
                wb, wk = wload('w_ffn_in', l * 1024 * 5632, 5632, 0, 0, 8, pieces=[(j * 128, 128), (D_FF + j * 128, 128)])
                for (t0, w) in htiles:
                    bg, bgk = pbank(DENSE)
                    bu, buk = pbank(DENSE)
                    for k in range(8):
                        P.I('pe', 'matmul', PS[bg].v(0, [[1, w]]), wb.v(k * 256, [[1, 128]]), H.v(k * NT + t0, [[1, w]]),
                            start=(k == 0), stop=(k == 7), reads=[wk, 'H'], writes=[bgk])
                    for k in range(8):
                        P.I('pe', 'matmul', PS[bu].v(0, [[1, w]]), wb.v(k * 256 + 128, [[1, 128]]), H.v(k * NT + t0, [[1, w]]),
                            start=(k == 0), stop=(k == 7), reads=[wk, 'H'], writes=[buk])
                    tq, tqk = tmpbuf(w)
                    P.I('act', 'activation', tq, PS[bg].v(0, [[1, w]]), AF.Silu, reads=[bgk], writes=[tqk])
                    P.I('dve', 'tensor_tensor', Fap(j, t0 - hc0, w), tq, PS[bu].v(0, [[1, w]]), op=ALU.mult,
                        reads=[tqk, buk], writes=['F'])
            for (t0, w) in htiles:
                for m in range(8):
                    wb, wk = wload('w_ffn_out', l * D_FF * 1024, 1024, m * 128, 128, 22)
                    b, bk = pbank(DENSE)
                    for j in range(22):
                        P.I('pe', 'matmul', PS[b].v(0, [[1, w]]), wb.v(j * 128, [[1, 128]]), Fap(j, t0 - hc0, w),
                            start=(j == 0), stop=(j == 21), reads=[wk, 'F'], writes=[bk])
                    evac(m, SCR32.v(MOoff + m * 512, [[1, w]]), PS[b].v(0, [[1, w]]), [bk], ['MO'])
                post_norm_add(MOap, t0, w, GFt, 9952)
            P.barrier()

    P.barrier()
    for ti in range(17):
        rows_ = 128 if ti < 16 else 64
        t0 = ti * 128
        so = (ti % 2) * 1024
        skey = 'YST%d' % (ti % 2)
        for c0 in (0, 4):
            b, bk = pbank((4, 5))
            for c in range(4):
                P.I('pe', 'transpose', PS[b].v(c * 128, [[1, 128]], pn=rows_), X.v((c0 + c) * NT + t0, [[1, rows_]]), IDF(128),
                    reads=['X', 'CST'], writes=[bk])
            evac(c0 // 4, SCR32.v(so + c0 * 128, [[1, 512]], pn=rows_), PS[b].v(0, [[1, 512]], pn=rows_), [bk], [skey + '_%d' % c0])
        if ti < 16:
            dst_ = dap(dr['y_p'], t0 * 1024, [[1024, 128], [1, 1024]])
        else:
            dst_ = dap(dr['y_s'], 0, [[1024, 64], [1, 1024]])
        P.dma('sp', dst_, SCR32.v(so, [[1, 1024]], pn=rows_), reads=[skey + '_0', skey + '_4'], writes=['OUTy'], out=True)

    P.finish()
    return nc


_CACHE = {}


def _get_prog():
    if 'nc' not in _CACHE:
        _CACHE['nc'] = build_program()
        _CACHE['consts'] = (make_consts(), make_sel())
    return _CACHE['nc'], _CACHE['consts']


def kernel(**inp):
    nc, consts = _get_prog()
    f = lambda a: np.ascontiguousarray(np.asarray(a, dtype=np.float32))
    shared = {}
    for name in ("w_mod", "b_mod", "g_pre_mix", "g_post_mix", "g_pre_ffn", "g_post_ffn", "w_in", "conv_w", "conv_b",
                 "conv_ln_g", "conv_ln_b", "w_conv_out", "ssm_log_dt", "w_ssm_glu", "w_att", "w_out", "w_ffn_in", "w_ffn_out"):
        shared[name] = f(inp[name])
    for name in ("ssm_a_re", "ssm_a_im", "ssm_b_re", "ssm_b_im", "ssm_c_re", "ssm_c_im", "ssm_d"):
        shared[name] = f(inp[name]).reshape(2, -1)
    shared["consts"] = consts[0]
    shared["selc"] = consts[1]
    in_maps = []
    for c in range(8):
        s = slice(c * 16, (c + 1) * 16)
        m = dict(shared)
        m["x_p"] = f(inp["x_prompt"][c])
        m["x_s"] = f(inp["x_sample"][s]).reshape(64, 1024)
        m["c_all"] = f(np.concatenate([np.asarray(inp["c_prompt"])[c:c + 1], np.asarray(inp["c_sample"])[s]], axis=0))
        for g in range(2):
            m["ck%d" % g] = f(np.asarray(inp["cache_k%d" % g])[:, s]).reshape(2, 16, -1, 256)
            m["cv%d" % g] = f(np.asarray(inp["cache_v%d" % g])[:, s]).reshape(2, 16, -1, 256)
        for nm, src in (("ck2", "cache_k2"), ("cv2", "cache_v2")):
            a = np.asarray(inp[src])[:, s].reshape(2, 16, 128, 16, 256)[:, :, :, 0:4]
            m[nm] = f(a.transpose(0, 1, 3, 2, 4)).reshape(2, 16, 512, 256)
        m["st_conv"] = f(np.asarray(inp["state_conv"])[:, s])
        m["st_re"] = f(np.asarray(inp["state_ssm_re"])[:, s]).reshape(2, 16, 2048)
        m["st_im"] = f(np.asarray(inp["state_ssm_im"])[:, s]).reshape(2, 16, 2048)
        in_maps.append(m)
    res = run_bass_kernel_spmd(nc, in_maps, core_ids=list(range(8)))
    R = res.results
    cat = lambda name, ax: np.concatenate([np.asarray(r[name]) for r in R], axis=ax)
    stk = lambda name: np.stack([np.asarray(r[name]) for r in R], axis=1)
    outs = []
    outs.append(np.stack([np.asarray(r["y_p"]) for r in R], axis=0).reshape(8, 2048, 1024))
    outs.append(cat("y_s", 0).reshape(128, 4, 1024))
    for g, keep in enumerate((128, 512, 2048)):
        outs.append(stk("k%d_p" % g).reshape(2, 8, keep, 4, 64))
        outs.append(stk("v%d_p" % g).reshape(2, 8, keep, 4, 64))
    outs.append(stk("conv_p").reshape(2, 8, 30, 512))
    outs.append(stk("ssm_re_p").reshape(2, 8, 32, 64))
    outs.append(stk("ssm_im_p").reshape(2, 8, 32, 64))
    for g in range(3):
        outs.append(cat("k%d_s" % g, 1).reshape(2, 128, 4, 4, 64))
        outs.append(cat("v%d_s" % g, 1).reshape(2, 128, 4, 4, 64))
    outs.append(cat("conv_s", 1).reshape(2, 128, 30, 512))
    outs.append(cat("ssm_re_s", 1).reshape(2, 128, 32, 64))
    outs.append(cat("ssm_im_s", 1).reshape(2, 128, 32, 64))
    return tuple(np.ascontiguousarray(o, dtype=np.float32) for o in outs)
```

```python
import contextlib
import math
import numpy as np
import concourse.bass as bass
import concourse.mybir as mybir
from concourse.bass_utils import run_bass_kernel_spmd

F32 = mybir.dt.float32
BF16 = mybir.dt.bfloat16
AF = mybir.ActivationFunctionType
ALU = mybir.AluOpType
AX = mybir.AxisListType

D = 1024
NT = 2112
TPR = 2048
NSEQ = 16
DEPTH = 2
D_FF = 2816
N_IN = 6912
O1, O2, O3, O4, O5 = 1024, 1536, 2304, 3072, 3840
PATTERNS = ((128, 1), (512, 4), (2048, 16))
RMS_EPS = 1e-6
LN_EPS = 1e-5
NEG = -1e30
TILES = [(0, 512), (512, 512), (1024, 512), (1536, 512), (2048, 64)]

ENGS = ('pe', 'act', 'dve', 'pool', 'sp')
EPOCH = 20000
NDSEM = 12
SAME_ENGINE_SYNC = {'pe': False, 'act': True, 'dve': True, 'pool': True, 'sp': False}


class _Op:
    __slots__ = ('fn', 'waits', 'inc')

    def __init__(self, fn):
        self.fn = fn
        self.waits = []
        self.inc = None


class Prog:
    def __init__(self, nc):
        self.nc = nc
        self.stack = contextlib.ExitStack()
        self.ops = {e: [] for e in ENGS}
        self.nops = {e: 0 for e in ENGS}
        self.last_w = {}
        self.readers = {}
        self.known = {e: {} for e in ENGS}
        self.sems = {}
        self.dma_rr = {e: 0 for e in ENGS}
        self.dma_cnt = {}
        self.out_tokens = []
        self.phase = 'setup'
        self.pe_phase = []

    def sbuf(self, name, shape, dtype):
        return self.stack.enter_context(self.nc.sbuf_tensor(name, shape, dtype))

    def psum(self, name, shape, dtype):
        return self.stack.enter_context(self.nc.psum_tensor(name, shape, dtype))

    def _sem(self, key):
        if key not in self.sems:
            self.sems[key] = self.stack.enter_context(
                self.nc.semaphore("s_" + "_".join(str(k) for k in key)))
        return self.sems[key]

    def _token_of(self, eng, idx):
        return (('p', eng, idx // EPOCH), idx % EPOCH + 1)

    def _add_dep(self, op, eng, tok, src_eng):
        if tok is None:
            return
        semkey, val = tok
        if src_eng == eng and semkey[0] == 'p' and not SAME_ENGINE_SYNC[eng]:
            return
        if self.known[eng].get(semkey, 0) >= val:
            return
        self.known[eng][semkey] = val
        for i, (k, v) in enumerate(op.waits):
            if k == semkey:
                op.waits[i] = (k, max(v, val))
                return
        op.waits.append((semkey, val))

    @staticmethod
    def _flat(keys):
        out = []
        for k in keys:
            if isinstance(k, (list, tuple)):
                out.extend(Prog._flat(k))
            else:
                out.append(k)
        return out

    def _deps(self, op, eng, reads, writes):
        reads, writes = self._flat(reads), self._flat(writes)
        for r in reads:
            lw = self.last_w.get(r)
            if lw is not None:
                self._add_dep(op, eng, lw[1], lw[0])
        for w in writes:
            lw = self.last_w.get(w)
            if lw is not None:
                self._add_dep(op, eng, lw[1], lw[0])
            for (se, tok) in self.readers.get(w, ()):
                self._add_dep(op, eng, tok, se)

    def _commit(self, eng, tok, reads, writes):
        reads, writes = self._flat(reads), self._flat(writes)
        for r in reads:
            self.readers.setdefault(r, []).append((eng, tok))
        for w in writes:
            self.last_w[w] = (eng, tok)
            self.readers[w] = []

    def op(self, eng, fn, reads=(), writes=(), sync_prev=False):
        o = _Op(fn)
        self._deps(o, eng, reads, writes)
        idx = self.nops[eng]
        if sync_prev and idx > 0:
            self._add_dep(o, eng, self._token_of(eng, idx - 1), None)
        self.nops[eng] += 1
        tok = self._token_of(eng, idx)
        o.inc = (tok[0], 1)
        if eng == 'pe':
            self.pe_phase.append(self.phase)
        self.ops[eng].append(o)
        self._commit(eng, tok, reads, writes)
        return o

    def I(self, eng, method, *args, reads=(), writes=(), sync_prev=False, **kw):
        return self.op(eng, lambda e: getattr(e, method)(*args, **kw), reads=reads, writes=writes, sync_prev=sync_prev)

    def dma(self, q, out_ap, in_ap, reads=(), writes=(), out=False, **kw):
        o = _Op(lambda e: e.dma_start(out=out_ap, in_=in_ap, **kw))
        self._deps(o, q, reads, writes)
        k = self.dma_rr[q]
        self.dma_rr[q] = (k + 1) % NDSEM
        gen = 0
        while self.dma_cnt.get(('d', q, k, gen), 0) >= EPOCH // 16:
            gen += 1
        semkey = ('d', q, k, gen)
        cnt = self.dma_cnt.get(semkey, 0)
        if cnt > 0:
            self._add_dep(o, q, (semkey, 16 * cnt), None)
        self.dma_cnt[semkey] = cnt + 1
        tok = (semkey, 16 * (cnt + 1))
        o.inc = (semkey, 16)
        self.ops[q].append(o)
        self._commit(q, tok, reads, writes)
        if out:
            self.out_tokens.append(tok)
        return o

    def barrier(self):
        toks = []
        for e in ENGS:
            if self.nops[e] > 0:
                toks.append((e, self._token_of(e, self.nops[e] - 1)))
        for semkey, cnt in self.dma_cnt.items():
            toks.append((None, (semkey, 16 * cnt)))
        for e in ENGS:
            o = _Op(None)
            for (se, tok) in toks:
                if se == e:
                    continue
                self._add_dep(o, e, tok, se)
            if o.waits:
                self.ops[e].append(o)
        self.last_w = {}
        self.readers = {}

    def finish(self):
        nc = self.nc
        fin = _Op(None)
        best = {}
        for (k, v) in self.out_tokens:
            best[k] = max(best.get(k, 0), v)
        for k, v in best.items():
            fin.waits.append((k, v))
        self.ops['sp'].append(fin)
        for e in ENGS:
            for o in self.ops[e]:
                for (k, v) in o.waits:
                    self._sem(k)
                if o.inc is not None:
                    self._sem(o.inc[0])
        prog = self

        def emit(eng_name, e):
            for o in prog.ops[eng_name]:
                for (k, v) in o.waits:
                    e.wait_ge(prog.sems[k], v)
                if o.fn is None:
                    continue
                ins = o.fn(e)
                if o.inc is not None:
                    ins.then_inc(prog.sems[o.inc[0]], o.inc[1])

        with nc.Block() as block:
            @block.tensor
            def _(e):
                emit('pe', e)

            @block.scalar
            def _(e):
                emit('act', e)

            @block.vector
            def _(e):
                emit('dve', e)

            @block.gpsimd
            def _(e):
                emit('pool', e)

            @block.sync
            def _(e):
                emit('sp', e)
        self.stack.close()


class _StopMix(Exception):
    pass


class T:
    def __init__(self, h, F):
        self.h = h
        self.F = F

    def v(self, off, dims, p0=0, pn=128):
        return bass.AP(self.h, p0 * self.F + off, [[self.F, pn]] + [list(d) for d in dims])


def vps(t, off, dims, p0, pstep, pn):
    return bass.AP(t.h, p0 * t.F + off, [[pstep * t.F, pn]] + [list(d) for d in dims])


def dap(h, off, dims):
    return bass.AP(h, off, [list(d) for d in dims])


C_IDENT = 0
C_DIST = 128
C_MASK = 384
C_MG2 = 640
C_DELTA = 642
C_DS0 = 658
C_MS0 = 662
C_DS1 = 666
C_DN0 = 667
C_MN0 = 731
C_MN1 = 795
C_QM = 859
C_TOT = 863


def make_consts():
    c = np.zeros((128, C_TOT), np.float32)
    c[:, C_IDENT:C_IDENT + 128] = np.eye(128, dtype=np.float32)
    tk = np.arange(128)[:, None]
    tq = np.arange(128)[None, :]
    d0 = tq - tk
    d1 = 128 + tq - tk
    c[:, C_DIST:C_DIST + 128] = np.where(d0 >= 0, d0, 0)
    c[:, C_MASK:C_MASK + 128] = np.where(d0 >= 0, 0.0, NEG)
    c[:, C_DIST + 128:C_DIST + 256] = np.where(d1 <= 128, d1, 0)
    c[:, C_MASK + 128:C_MASK + 256] = np.where(d1 <= 128, 0.0, NEG)
    p = np.arange(128)
    for g2 in range(2):
        c[:, C_MG2 + g2] = ((p % 32) // 16 == g2)
    for cc in range(16):
        c[:, C_DELTA + cc] = (p % 16 == cc)
    for j in range(4):
        kk = 128 + j - p
        c[:, C_DS0 + j] = np.where(kk <= 128, kk, 0)
        c[:, C_MS0 + j] = np.where(kk <= 128, 0.0, NEG)
    c[:, C_DS1] = 128 - p
    ki = np.arange(64)[:, None]
    qi = np.arange(64)[None, :]
    same = (ki // 4) == (qi // 4)
    dn = (qi % 4) - (ki % 4)
    v0 = same & (dn >= 0)
    c[:64, C_DN0:C_DN0 + 64] = np.where(v0, dn, 0)
    c[:64, C_MN0:C_MN0 + 64] = np.where(v0, 0.0, NEG)
    c[:64, C_MN1:C_MN1 + 64] = np.where(ki == qi, 0.0, NEG)
    for q in range(4):
        c[:, C_QM + q] = (p // 32 == q)
    return c


def make_sel():
    c = np.zeros((128, 2048), np.float32)
    for tpar in range(2):
        for gl in range(8):
            blk = np.zeros((128, 128), np.float32)
            for pp in range(128):
                if (pp % 32) // 16 == tpar:
                    blk[pp, gl * 16 + pp % 16] = 1.0
            o = (tpar * 8 + gl) * 128
            c[:, o:o + 128] = blk
    return c


def alibi_slope(h):
    return 2.0 ** (-8.0 * (h + 1) / 12.0)


IN_SPECS = [
    ("x_p", [2048, 1024]), ("x_s", [64, 1024]), ("c_all", [17, 1024]),
    ("ck0", [2, 16, 128, 256]), ("cv0", [2, 16, 128, 256]),
    ("ck1", [2, 16, 512, 256]), ("cv1", [2, 16, 512, 256]),
    ("ck2", [2, 16, 512, 256]), ("cv2", [2, 16, 512, 256]),
    ("st_conv", [2, 16, 30, 512]), ("st_re", [2, 16, 2048]), ("st_im", [2, 16, 2048]),
    ("w_mod", [2, 1024, 6144]), ("b_mod", [2, 6144]),
    ("g_pre_mix", [2, 1024]), ("g_post_mix", [2, 1024]), ("g_pre_ffn", [2, 1024]), ("g_post_ffn", [2, 1024]),
    ("w_in", [2, 1024, 6912]), ("conv_w", [2, 31, 512]), ("conv_b", [2, 512]),
    ("conv_ln_g", [2, 512]), ("conv_ln_b", [2, 512]), ("w_conv_out", [2, 512, 1024]),
    ("ssm_a_re", [2, 2048]), ("ssm_a_im", [2, 2048]), ("ssm_log_dt", [2, 32]),
    ("ssm_b_re", [2, 32 * 64 * 16]), ("ssm_b_im", [2, 32 * 64 * 16]),
    ("ssm_c_re", [2, 32 * 16 * 64]), ("ssm_c_im", [2, 32 * 16 * 64]), ("ssm_d", [2, 512]),
    ("w_ssm_glu", [2, 512, 2048]), ("w_att", [2, 256, 1024]), ("w_out", [2, 1024, 1024]),
    ("w_ffn_in", [2, 1024, 5632]), ("w_ffn_out", [2, 2816, 1024]),
    ("consts", [128, C_TOT]), ("selc", [128, 2048]),
]
OUT_SPECS = [
    ("y_p", [2048, 1024]), ("y_s", [64, 1024]),
    ("k0_p", [2, 128, 256]), ("v0_p", [2, 128, 256]), ("k1_p", [2, 512, 256]), ("v1_p", [2, 512, 256]),
    ("k2_p", [2, 2048, 256]), ("v2_p", [2, 2048, 256]),
    ("conv_p", [2, 30, 512]), ("ssm_re_p", [2, 2048]), ("ssm_im_p", [2, 2048]),
    ("k0_s", [2, 64, 256]), ("v0_s", [2, 64, 256]), ("k1_s", [2, 64, 256]), ("v1_s", [2, 64, 256]),
    ("k2_s", [2, 64, 256]), ("v2_s", [2, 64, 256]),
    ("conv_s", [2, 16, 30, 512]), ("ssm_re_s", [2, 16, 2048]), ("ssm_im_s", [2, 16, 2048]),
]


def build_program(stop_after=None):
    import os
    KSTOP = os.environ.get('KSTOP', '')
    nc = bass.Bass("TRN2", target_bir_lowering=False)
    P = Prog(nc)
    dr = {}
    for name, shp in IN_SPECS:
        dr[name] = nc.dram_tensor(name, shp, F32, kind="ExternalInput")
    for name, shp in OUT_SPECS:
        dr[name] = nc.dram_tensor(name, shp, F32, kind="ExternalOutput")

    def sb(name, F, dt):
        return T(P.sbuf(name, [128, F], dt), F)

    X = sb("X", 8 * NT, F32)
    H = sb("H", 8 * NT, BF16)
    MG = sb("MG", 8 * NT, BF16)
    NWB = 2
    WB = [sb("WB%d" % i, 4096, BF16) for i in range(NWB)]
    CST = sb("CST", C_TOT, F32)
    IDB = sb("IDB", 128, BF16)
    ONB = sb("ONB", 128, BF16)
    ONF = sb("ONF", 128, F32)
    SELB = sb("SELB", 2048, BF16)
    CT = sb("CT", 8 * 17, BF16)
    VEC = sb("VEC", 96, F32)
    MOD = sb("MOD", 48 * 17, F32)
    AMt = sb("AMt", 8 * 17, F32)
    GMt = sb("GMt", 8 * 17, F32)
    AFt = sb("AFt", 8 * 17, F32)
    GFt = sb("GFt", 8 * 17, F32)
    RS = sb("RS", 512, F32)
    TMPA = sb("TMPA", 512, F32)
    SCRB = 41 * 1024
    SCR_h = P.sbuf("SCR", [128, SCRB // 2], BF16)
    SCR16 = T(SCR_h, SCRB // 2)
    SCR32 = T(SCR_h.bitcast(F32), SCRB // 4)
    PS = []
    PSB = []
    for i in range(8):
        h = P.psum("PS%d" % i, [128, 512], F32)
        PS.append(T(h, 512))
        PSB.append(T(h.bitcast(BF16), 1024))

    IDF = lambda pn=128: CST.v(C_IDENT, [[1, pn]], pn=pn)

    st = {'wrr': 0, 'prr': 0, 'tmp': 0}
    DENSE = (0, 1, 2, 3, 6, 7)

    def tmpbuf(w):
        i = st['tmp'] % 2
        st['tmp'] += 1
        return (TMPA, RS)[i].v(0, [[1, w]]), ('TMPA', 'RS')[i]

    def wload(hname, layer_off, row_stride, col0, ncols, kc, row0=0, pieces=None):
        i = st['wrr']
        st['wrr'] = (i + 1) % NWB
        wb = WB[i]
        if pieces is None and kc >= 16:
            k1 = kc // 2
            for pi_, (ka, kn) in enumerate(((0, k1), (k1, kc - k1))):
                src = dap(dr[hname], layer_off + (row0 + ka * 128) * row_stride + col0, [[row_stride, 128], [128 * row_stride, kn], [1, ncols]])
                wkeys = ['WB%d_0' % i] if pi_ == 0 else ['WB%d_1' % i, 'WB%d_2' % i]
                P.dma('pool', wb.v(ka * ncols, [[ncols, kn], [1, ncols]]), src, reads=[], writes=wkeys)
            return wb, ['WB%d_%d' % (i, x) for x in range(3)]
        if pieces is None:
            pieces = [(col0, ncols)]
        tot = sum(n for _, n in pieces)
        o = 0
        keys = []
        for pi_, (c0_, n_) in enumerate(pieces):
            src = dap(dr[hname], layer_off + row0 * row_stride + c0_, [[row_stride, 128], [128 * row_stride, kc], [1, n_]])
            wkeys = ['WB%d_%d' % (i, pi_)] if pi_ < len(pieces) - 1 else ['WB%d_%d' % (i, x) for x in range(pi_, 3)]
            P.dma('pool', wb.v(o, [[tot, kc], [1, n_]]), src, reads=[], writes=wkeys)
            o += n_
        return wb, ['WB%d_%d' % (i, x) for x in range(3)]

    def wload_multi(items):
        i = st['wrr']
        st['wrr'] = (i + 1) % NWB
        wb = WB[i]
        o = 0
        offs = []
        keys = []
        for pi_, (hname, layer_off, row_stride, c0_, n_, kc) in enumerate(items):
            src = dap(dr[hname], layer_off + c0_, [[row_stride, 128], [128 * row_stride, kc], [1, n_]])
            wkeys = ['WB%d_%d' % (i, pi_)] if pi_ < len(items) - 1 else ['WB%d_%d' % (i, x) for x in range(pi_, 3)]
            P.dma('pool', wb.v(o, [[n_, kc], [1, n_]]), src, reads=[], writes=wkeys)
            offs.append(o)
            o += kc * n_
        assert o <= 4096 and len(items) <= 3
        return wb, ['WB%d_%d' % (i, x) for x in range(3)], offs

    def pbank(pool=(0, 1, 2, 3)):
        i = st['prr']
        st['prr'] = i + 1
        b = pool[i % len(pool)]
        return b, 'PS%d' % b

    def evac(i, out_ap, in_ap, reads, writes):
        if i % 2 == 0:
            P.I('dve', 'tensor_copy', out_ap, in_ap, reads=reads, writes=writes)
        else:
            P.I('act', 'copy', out_ap, in_ap, reads=reads, writes=writes)

    def transpose_rows(src_ap_fn, nrows, dst_fn, rkey, ncol_chunks, bankpool=(4, 5)):
        for c0 in range(0, ncol_chunks, 4):
            ncn = min(4, ncol_chunks - c0)
            b, bk = pbank(bankpool)
            for c in range(ncn):
                P.I('pe', 'transpose', PS[b].v(c * 128, [[1, nrows]]), src_ap_fn(c0 + c), IDF(nrows),
                    reads=[rkey, 'CST'], writes=[bk])
            dst_fn(c0, ncn, b, bk)

    P.dma('sp', CST.v(0, [[1, C_TOT]]), dap(dr['consts'], 0, [[C_TOT, 128], [1, C_TOT]]), writes=['CST'])
    P.dma('pool', SELB.v(0, [[1, 2048]]), dap(dr['selc'], 0, [[2048, 128], [1, 2048]]), writes=['SELB'])
    P.I('dve', 'tensor_copy', IDB.v(0, [[1, 128]]), CST.v(C_IDENT, [[1, 128]]), reads=['CST'], writes=['IDB'])
    P.I('pool', 'memset', ONB.v(0, [[1, 128]]), 1.0, writes=['ONB'])
    P.I('pool', 'memset', ONF.v(0, [[1, 128]]), 1.0, writes=['ONF'])

    for ti in range(17):
        rows = 128 if ti < 16 else 64
        boff = (ti % 2) * 1024
        key = 'XIN%d' % (ti % 2)
        if ti < 16:
            src = dap(dr['x_p'], ti * 128 * 1024, [[1024, 128], [1, 1024]])
        else:
            src = dap(dr['x_s'], 0, [[1024, 64], [1, 1024]])
        P.dma('sp', SCR32.v(boff, [[1, 1024]], pn=rows), src, writes=[key])
        t0 = ti * 128

        def dst(c0, ncn, b, bk, t0=t0, rows=rows):
            evac(c0 // 4, X.v(c0 * NT + t0, [[NT, ncn], [1, rows]]), PS[b].v(0, [[128, ncn], [1, rows]]), [bk], ['X'])
        transpose_rows(lambda c, boff=boff, rows=rows: SCR32.v(boff + c * 128, [[1, 128]], pn=rows), rows, dst, key, 8)

    P.dma('sp', SCR32.v(2048, [[1, 1024]], pn=17), dap(dr['c_all'], 0, [[1024, 17], [1, 1024]]), writes=['CIN'])

    def dst_c(c0, ncn, b, bk):
        P.I('act', 'activation', CT.v(c0 * 17, [[17, ncn], [1, 17]]), PS[b].v(0, [[128, ncn], [1, 17]]), AF.Silu,
            reads=[bk], writes=['CT'])
    transpose_rows(lambda c: SCR32.v(2048 + c * 128, [[1, 128]], pn=17), 17, dst_c, 'CIN', 8)
    P.barrier()

    def rstd_from(src_aps, width, scale, eps, key_r):
        nk = len(src_aps)
        for k in range(nk):
            P.I('act', 'activation', SCR16.v(k * 512, [[1, width]]), src_aps[k], AF.Square, reads=[key_r], writes=['SQ'])
        b, bk = pbank((4, 5))
        for k in range(nk):
            P.I('pe', 'matmul', PS[b].v(0, [[1, width]]), ONB.v(0, [[1, 128]]), SCR16.v(k * 512, [[1, width]]),
                start=(k == 0), stop=(k == nk - 1), reads=['SQ', 'ONB'], writes=[bk])
        P.I('act', 'activation', RS.v(0, [[1, width]]), PS[b].v(0, [[1, width]]), AF.Sqrt, bias=eps, scale=scale,
            reads=[bk], writes=['RS'])
        P.I('dve', 'reciprocal', RS.v(0, [[1, width]]), RS.v(0, [[1, width]]), reads=['RS'], writes=['RS'])

    def pre_norm(At, Bt):
        for (t0, w) in TILES:
            rstd_from([X.v(k * NT + t0, [[1, w]]) for k in range(8)], w, 1.0 / D, RMS_EPS, 'X')
            for k in range(8):
                if t0 < TPR:
                    tp_ = TMPA.v(0, [[1, w]]) if k % 2 == 0 else SCR32.v(2048, [[1, w]])
                    tpk = 'TMPA' if k % 2 == 0 else 'TMPB'
                    P.I('dve', 'scalar_tensor_tensor', tp_, X.v(k * NT + t0, [[1, w]]), At.v(k * 17, [[1, 1]]),
                        RS.v(0, [[1, w]]), op0=ALU.mult, op1=ALU.mult, reads=['X', 'RS', 'MODS'], writes=[tpk])
                    P.I('act', 'activation', H.v(k * NT + t0, [[1, w]]), tp_, AF.Identity,
                        bias=Bt.v(k * 17, [[1, 1]]), scale=1.0, reads=[tpk, 'MODS', 'MOD'], writes=['H'])
                else:
                    P.I('dve', 'tensor_tensor', TMPA.v(0, [[4, 16], [1, 4]]), X.v(k * NT + TPR, [[4, 16], [1, 4]]),
                        At.v(k * 17 + 1, [[1, 16], [0, 4]]), op=ALU.mult, reads=['X', 'MODS'], writes=['TMPA'])
                    P.I('dve', 'tensor_tensor', TMPA.v(0, [[1, 64]]), TMPA.v(0, [[1, 64]]), RS.v(0, [[1, 64]]), op=ALU.mult,
                        reads=['TMPA', 'RS'], writes=['TMPA'])
                    P.I('dve', 'tensor_tensor', H.v(k * NT + TPR, [[4, 16], [1, 4]]), TMPA.v(0, [[4, 16], [1, 4]]),
                        Bt.v(k * 17 + 1, [[1, 16], [0, 4]]), op=ALU.add, reads=['TMPA', 'MODS', 'MOD'], writes=['H'])

    def post_norm_add(MO, t0, w, Gt, tmp2off):
        rstd_from([MO(k, w) for k in range(8)], w, 1.0 / D, RMS_EPS, 'MO')
        for k in range(8):
            if k % 2 == 0:
                tk_ = 'TMPA'
                tv = lambda dims: TMPA.v(0, dims)
            else:
                tk_ = 'TMPB'
                tv = lambda dims: SCR32.v(tmp2off, dims)
            if t0 < TPR:
                P.I('dve', 'scalar_tensor_tensor', tv([[1, w]]), MO(k, w), Gt.v(k * 17, [[1, 1]]), RS.v(0, [[1, w]]),
                    op0=ALU.mult, op1=ALU.mult, reads=['MO', 'RS', 'MODS'], writes=[tk_])
            else:
                P.I('dve', 'tensor_tensor', tv([[4, 16], [1, 4]]), MO(k, None), Gt.v(k * 17 + 1, [[1, 16], [0, 4]]),
                    op=ALU.mult, reads=['MO', 'MODS'], writes=[tk_])
                P.I('dve', 'tensor_tensor', tv([[1, 64]]), tv([[1, 64]]), RS.v(0, [[1, 64]]), op=ALU.mult,
                    reads=[tk_, 'RS'], writes=[tk_])
            P.I('dve', 'tensor_tensor', X.v(k * NT + t0, [[1, w]]), X.v(k * NT + t0, [[1, w]]), tv([[1, w]]), op=ALU.add,
                reads=[tk_, 'XA%d' % k], writes=['XA%d' % k])

    class _Off:
        def __init__(self, t, off):
            self.t, self.off = t, off

        def v(self, off, dims, p0=0, pn=128):
            return self.t.v(self.off + off, dims, p0, pn)

    for l in range(DEPTH):
        P.phase = 'mod'
        rows = [("g_pre_mix", 8), ("g_post_mix", 8), ("g_pre_ffn", 8), ("g_post_ffn", 8),
                ("conv_b", 4), ("conv_ln_g", 4), ("conv_ln_b", 4), ("b_mod", 48)]
        r0 = 0
        vo = {}
        for name, n in rows:
            P.dma('sp', SCR32.v(0, [[1, 128]], p0=r0, pn=n), dap(dr[name], l * n * 128, [[128, n], [1, 128]]), writes=['STG'])
            vo[name] = r0
            r0 += n
        b, bk = pbank((4, 5))
        P.I('pe', 'transpose', PS[b].v(0, [[1, 92]]), SCR32.v(0, [[1, 128]], pn=92), IDF(92), reads=['STG', 'CST'], writes=[bk])
        P.I('dve', 'tensor_copy', VEC.v(0, [[1, 92]]), PS[b].v(0, [[1, 92]]), reads=[bk], writes=['VEC'])

        for blk in range(12):
            wb, wk = wload('w_mod', l * 1024 * 6144, 6144, blk * 512, 512, 8)
            b, bk = pbank((4, 5))
            for m in range(4):
                for k in range(8):
                    P.I('pe', 'matmul', PS[b].v(m * 17, [[1, 17]]), wb.v(k * 512 + m * 128, [[1, 128]]), CT.v(k * 17, [[1, 17]]),
                        start=(k == 0), stop=(k == 7), reads=[wk, 'CT'], writes=[bk])
            for m in range(4):
                j = blk * 4 + m
                P.I('dve', 'tensor_scalar', MOD.v(j * 17, [[1, 17]]), PS[b].v(m * 17, [[1, 17]]), VEC.v(vo['b_mod'] + j, [[1, 1]]),
                    None, op0=ALU.add, reads=[bk, 'VEC'], writes=['MOD'])

        def mk(dst, modj, vname, plus1):
            if plus1:
                P.I('dve', 'tensor_scalar', dst.v(0, [[1, 136]]), MOD.v(modj * 17, [[1, 136]]), 1.0, None, op0=ALU.add,
                    reads=['MOD'], writes=['MODS'])
                P.I('dve', 'tensor_tensor', dst.v(0, [[17, 8], [1, 17]]), dst.v(0, [[17, 8], [1, 17]]),
                    VEC.v(vo[vname], [[1, 8], [0, 17]]), op=ALU.mult, reads=['MODS', 'VEC'], writes=['MODS'])
            else:
                P.I('dve', 'tensor_tensor', dst.v(0, [[17, 8], [1, 17]]), MOD.v(modj * 17, [[17, 8], [1, 17]]),
                    VEC.v(vo[vname], [[1, 8], [0, 17]]), op=ALU.mult, reads=['MOD', 'VEC'], writes=['MODS'])
        mk(AMt, 8, 'g_pre_mix', True)
        mk(GMt, 16, 'g_post_mix', False)
        mk(AFt, 32, 'g_pre_ffn', True)
        mk(GFt, 40, 'g_post_ffn', False)
        BMv = _Off(MOD, 0)
        BFv = _Off(MOD, 24 * 17)
        P.barrier()

        P.phase = 'prenorm'
        pre_norm(AMt, BMv)
        P.barrier()

        P.phase = 'kv'
        kvi = 0
        for g, (win, dil) in enumerate(PATTERNS):
            keep = min(win, TPR)
            for cbase, oname_p, oname_s in ((O3, "k%d_p" % g, "k%d_s" % g), (O4, "v%d_p" % g, "v%d_s" % g)):
                col0 = cbase + g * 256
                wb, wk = wload('w_in', l * 1024 * N_IN, N_IN, col0, 256, 8)
                tiles = [(t0, 128) for t0 in range(TPR - keep, TPR, 128)] + [(TPR, 64)]
                for (t0, rows_) in tiles:
                    b, bk = pbank(DENSE)
                    for k in range(8):
                        P.I('pe', 'matmul', PS[b].v(0, [[1, 256]], pn=rows_), H.v(k * NT + t0, [[1, rows_]]), wb.v(k * 256, [[1, 256]]),
                            start=(k == 0), stop=(k == 7), reads=[wk, 'H'], writes=[bk])
                    so = 4096 + (kvi % 2) * 256
                    skey = 'KVST%d' % (kvi % 2)
                    kvi += 1
                    evac(kvi, SCR32.v(so, [[1, 256]], pn=rows_), PS[b].v(0, [[1, 256]], pn=rows_), [bk], [skey])
                    if t0 < TPR:
                        dst_ = dap(dr[oname_p], l * keep * 256 + (t0 - (TPR - keep)) * 256, [[256, 128], [1, 256]])
                    else:
                        dst_ = dap(dr[oname_s], l * 64 * 256, [[256, 64], [1, 256]])
                    P.dma('sp', dst_, SCR32.v(so, [[1, 256]], pn=rows_), reads=[skey], writes=['OUT' + oname_p], out=True)
        P.barrier()


        def mixer():
            P.phase = 'conv'
            skipmix = False
            UBo, UBW = 4096, 2142
            USo = 12664
            DGo = 14840
            MEANo = 9404
            UFo = 9916
            CWo = 10292
            P.dma('sp', SCR32.v(0, [[1, 512]], pn=31), dap(dr['conv_w'], l * 31 * 512, [[512, 31], [1, 512]]), writes=['STG'])
            b, bk = pbank((4, 5))
            for c in range(4):
                P.I('pe', 'transpose', PS[b].v(c * 32, [[1, 31]]), SCR32.v(c * 128, [[1, 128]], pn=31), IDF(31),
                    reads=['STG', 'CST'], writes=[bk])
            P.I('dve', 'tensor_copy', SCR32.v(CWo, [[31, 4], [1, 31]]), PS[b].v(0, [[32, 4], [1, 31]]), reads=[bk], writes=['CW'])
            if KSTOP == 'A1':
                raise _StopMix()
            P.dma('sp', dap(dr['conv_s'], l * 16 * 15360, [[15360, 16], [1, 13312]]),
                  dap(dr['st_conv'], l * 16 * 15360 + 4 * 512, [[15360, 16], [1, 13312]]), writes=['OUTconvs'], out=True)
            for rt in range(4):
                P.dma('sp', SCR32.v(0, [[1, 512]], pn=120), dap(dr['st_conv'], l * 16 * 15360 + rt * 4 * 15360, [[512, 120], [1, 512]]),
                      writes=['STG'])
                b, bk = pbank((4, 5))
                for c in range(4):
                    P.I('pe', 'transpose', PS[b].v(c * 128, [[1, 120]]), SCR32.v(c * 128, [[1, 128]], pn=120), IDF(120),
                        reads=['STG', 'CST'], writes=[bk])
                P.I('act', 'copy', SCR16.v(USo + rt * 4 * 34, [[544, 4], [34, 4], [1, 30]]), PS[b].v(0, [[128, 4], [30, 4], [1, 30]]),
                    reads=[bk], writes=['US'])
            if KSTOP == 'A2':
                raise _StopMix()
            P.I('pool', 'memset', SCR16.v(UBo, [[UBW, 4], [1, 30]]), 0.0, writes=['UBu0', 'UBu1', 'UBu2', 'UBu3'])
            for c in range(4):
                wb, wk = wload('w_in', l * 1024 * N_IN, N_IN, 0, 0, 8, pieces=[(c * 128, 128), (512 + c * 128, 128)])
                for (t0, w) in TILES:
                    b1, k1 = pbank(DENSE)
                    b2, k2 = pbank(DENSE)
                    for k in range(8):
                        P.I('pe', 'matmul', PS[b1].v(0, [[1, w]]), wb.v(k * 256, [[1, 128]]), H.v(k * NT + t0, [[1, w]]),
                            start=(k == 0), stop=(k == 7), reads=[wk, 'H'], writes=[k1])
                    for k in range(8):
                        P.I('pe', 'matmul', PS[b2].v(0, [[1, w]]), wb.v(k * 256 + 128, [[1, 128]]), H.v(k * NT + t0, [[1, w]]),
                            start=(k == 0), stop=(k == 7), reads=[wk, 'H'], writes=[k2])
                    TG_, tgk = ((TMPA, 'TMPA'), (RS, 'RS'))[st['tmp'] % 2]
                    st['tmp'] += 1
                    P.I('act', 'activation', TG_.v(0, [[1, w]]), PS[b2].v(0, [[1, w]]), AF.Sigmoid, reads=[k2], writes=[tgk])
                    if t0 < TPR:
                        P.I('dve', 'tensor_tensor', SCR16.v(UBo + c * UBW + 30 + t0, [[1, w]]), TG_.v(0, [[1, w]]), PS[b1].v(0, [[1, w]]),
                            op=ALU.mult, reads=[tgk, k1], writes=['UBu%d' % c])
                        if t0 == 1536:
                            P.I('dve', 'tensor_tensor', SCR32.v(UFo + c * 94, [[1, 30]]), TG_.v(482, [[1, 30]]), PS[b1].v(482, [[1, 30]]),
                                op=ALU.mult, reads=[tgk, k1], writes=['UF'])
                    else:
                        P.I('dve', 'tensor_tensor', SCR16.v(USo + c * 544 + 30, [[34, 16], [1, 4]]), TG_.v(0, [[4, 16], [1, 4]]),
                            PS[b1].v(0, [[4, 16], [1, 4]]), op=ALU.mult, reads=[tgk, k1], writes=['US'])
                        P.I('dve', 'tensor_tensor', SCR32.v(UFo + c * 94 + 30, [[1, 64]]), TG_.v(0, [[1, 64]]), PS[b1].v(0, [[1, 64]]),
                            op=ALU.mult, reads=[tgk, k1], writes=['UF'])
            if KSTOP == 'A3':
                raise _StopMix()
            b, bk = pbank((4, 5))
            for c in range(4):
                P.I('pe', 'transpose', PS[b].v(c * 128, [[1, 128]], pn=30), SCR32.v(UFo + c * 94, [[1, 30]]), IDF(128),
                    reads=['UF', 'CST'], writes=[bk])
            P.I('dve', 'tensor_copy', SCR32.v(0, [[1, 512]], pn=30), PS[b].v(0, [[1, 512]], pn=30), reads=[bk], writes=['STG'])
            P.dma('sp', dap(dr['conv_p'], l * 30 * 512, [[512, 30], [1, 512]]), SCR32.v(0, [[1, 512]], pn=30), reads=['STG'],
                  writes=['OUTconvp'], out=True)
            b, bk = pbank((4, 5))
            for c in range(4):
                P.I('pe', 'transpose', PS[b].v(c * 128, [[1, 128]], pn=64), SCR32.v(UFo + c * 94 + 30, [[1, 64]]), IDF(128),
                    reads=['UF', 'CST'], writes=[bk])
            P.I('act', 'copy', SCR32.v(512, [[1, 512]], pn=64), PS[b].v(0, [[1, 512]], pn=64), reads=[bk], writes=['STG2'])
            for j in range(4):
                P.dma('sp', dap(dr['conv_s'], l * 16 * 15360 + (26 + j) * 512, [[15360, 16], [1, 512]]),
                      vps(SCR32, 512, [[1, 512]], j, 4, 16), reads=['STG2'], writes=['OUTconvs'], out=True)
            if KSTOP == 'A4':
                raise _StopMix()
            for c in range(4):
                for k in range(31):
                    P.I('dve', 'tensor_scalar', SCR16.v(DGo + k * 128, [[1, 128]]), IDB.v(0, [[1, 128]]),
                        SCR32.v(CWo + c * 31 + k, [[1, 1]]), None, op0=ALU.mult, reads=['IDB', 'CW'], writes=['DG%d' % k])
                for (t0, w) in reversed(TILES):
                    b, bk = pbank(DENSE)
                    for k in range(31):
                        if t0 < TPR:
                            rhs = SCR16.v(UBo + c * UBW + t0 + k, [[1, w]])
                            out_ = PS[b].v(0, [[1, w]])
                        else:
                            rhs = SCR16.v(USo + c * 544 + k, [[34, 16], [1, 4]])
                            out_ = PS[b].v(0, [[4, 16], [1, 4]])
                        P.I('pe', 'matmul', out_, SCR16.v(DGo + k * 128, [[1, 128]]), rhs, start=(k == 0), stop=(k == 30),
                            reads=['DG%d' % k, 'UBu%d' % c, 'US'], writes=[bk])
                    P.I('act', 'activation', SCR16.v(UBo + c * UBW + 30 + t0, [[1, w]]), PS[b].v(0, [[1, w]]), AF.Identity,
                        bias=VEC.v(vo['conv_b'] + c, [[1, 1]]), scale=1.0, reads=[bk, 'VEC'], writes=['UBy%d' % c])
            if KSTOP == 'A5':
                raise _StopMix()
            for (t0, w) in TILES:
                ys = [SCR16.v(UBo + c * UBW + 30 + t0, [[1, w]]) for c in range(4)]
                for c in range(4):
                    P.I('act', 'activation', SCR16.v(c * 512, [[1, w]]), ys[c], AF.Square, reads=['UBy%d' % c], writes=['YQ'])
                b1, k1 = pbank((4, 5))
                b2, k2 = pbank((4, 5))
                for c in range(4):
                    P.I('pe', 'matmul', PS[b1].v(0, [[1, w]]), ONB.v(0, [[1, 128]]), ys[c], start=(c == 0), stop=(c == 3),
                        reads=['ONB', 'UBy%d' % c], writes=[k1])
                for c in range(4):
                    P.I('pe', 'matmul', PS[b2].v(0, [[1, w]]), ONB.v(0, [[1, 128]]), SCR16.v(c * 512, [[1, w]]), start=(c == 0), stop=(c == 3),
                        reads=['ONB', 'YQ'], writes=[k2])
                MEAN = SCR32.v(MEANo, [[1, w]])
                TA = TMPA.v(0, [[1, w]])
                RSw = RS.v(0, [[1, w]])
                P.I('act', 'activation', MEAN, PS[b1].v(0, [[1, w]]), AF.Identity, scale=1.0 / 512, reads=[k1], writes=['MEAN'])
                P.I('dve', 'tensor_tensor', TA, MEAN, MEAN, op=ALU.mult, reads=['MEAN'], writes=['TMPA'])
                P.I('dve', 'scalar_tensor_tensor', TA, PS[b2].v(0, [[1, w]]), 1.0 / 512, TA, op0=ALU.mult, op1=ALU.subtract,
                    reads=[k2, 'TMPA'], writes=['TMPA'])
                P.I('act', 'activation', RSw, TA, AF.Sqrt, bias=LN_EPS, scale=1.0, reads=['TMPA'], writes=['RS'])
                P.I('dve', 'reciprocal', RSw, RSw, reads=['RS'], writes=['RS'])
                for c in range(4):
                    P.I('dve', 'tensor_tensor', TA, ys[c], MEAN, op=ALU.subtract, reads=['UBy%d' % c, 'MEAN'], writes=['TMPA'])
                    P.I('dve', 'scalar_tensor_tensor', TA, TA, VEC.v(vo['conv_ln_g'] + c, [[1, 1]]), RSw, op0=ALU.mult, op1=ALU.mult,
                        reads=['TMPA', 'RS', 'VEC'], writes=['TMPA'])
                    P.I('act', 'activation', ys[c], TA, AF.Silu, bias=VEC.v(vo['conv_ln_b'] + c, [[1, 1]]), scale=1.0,
                        reads=['TMPA', 'VEC'], writes=['UBy%d' % c])

            if KSTOP == 'A6':
                raise _StopMix()
            def gated_branch(gate_col0, witem_fn, kcb, rhs_fn, rkeys, first):
                for m in range(8):
                    wb, wk, offs = wload_multi([('w_in', l * 1024 * N_IN, N_IN, gate_col0 + m * 128, 128, 8), witem_fn(m)])
                    for (t0, w) in TILES:
                        bg, kg = pbank(DENSE)
                        bb, kb = pbank(DENSE)
                        for k in range(8):
                            P.I('pe', 'matmul', PS[bg].v(0, [[1, w]]), wb.v(offs[0] + k * 128, [[1, 128]]), H.v(k * NT + t0, [[1, w]]),
                                start=(k == 0), stop=(k == 7), reads=[wk, 'H'], writes=[kg])
                        for k in range(kcb):
                            P.I('pe', 'matmul', PS[bb].v(0, [[1, w]]), wb.v(offs[1] + k * 128, [[1, 128]]), rhs_fn(k, t0, w),
                                start=(k == 0), stop=(k == kcb - 1), reads=[wk] + rkeys, writes=[kb])
                        TA, tak = tmpbuf(w)
                        P.I('act', 'activation', TA, PS[bg].v(0, [[1, w]]), AF.Sigmoid, reads=[kg], writes=[tak])
                        if first:
                            P.I('dve', 'tensor_tensor', MG.v(m * NT + t0, [[1, w]]), TA, PS[bb].v(0, [[1, w]]), op=ALU.mult,
                                reads=[tak, kb], writes=['MG'])
                        else:
                            P.I('dve', 'tensor_tensor', TA, TA, PS[bb].v(0, [[1, w]]), op=ALU.mult, reads=[tak, kb], writes=[tak])
                            P.I('dve', 'tensor_tensor', MG.v(m * NT + t0, [[1, w]]), MG.v(m * NT + t0, [[1, w]]), TA, op=ALU.add,
                                reads=[tak, 'MG'], writes=['MG'])

            gated_branch(O5, lambda m: ('w_conv_out', l * 512 * 1024, 1024, m * 128, 128, 4), 4,
                         lambda k, t0, w: SCR16.v(UBo + k * UBW + 30 + t0, [[1, w]]), ['UBy0', 'UBy1', 'UBy2', 'UBy3'], True)
            P.barrier()


            P.phase = 'attn'
            if KSTOP == 'C0':
                raise _StopMix()
            KTo, QTo0, VTo = 0, 4096, 5120
            NUMo, DENo = 3584, 5632
            ATTo = 15360
            BTo, SSo0, PTo0 = 8736, 9248, 19520
            ANo, ADo = 10016, 10080
            loff = l * 1024 * N_IN
            cnt = {'s': 0, 'n': 0, 'q': 0, 'c': 0}

            def attn_prompt(g, hp):
                win, dil = PATTERNS[g]
                nblk = TPR // (dil * 128)
                wb, wk = wload('w_in', loff, N_IN, O3 + g * 256 + hp * 128, 128, 8)
                for ti in range(4):
                    b, bk = pbank((0, 1))
                    for k in range(8):
                        P.I('pe', 'matmul', PS[b].v(0, [[1, 512]]), wb.v(k * 128, [[1, 128]]), H.v(k * NT + ti * 512, [[1, 512]]),
                            start=(k == 0), stop=(k == 7), reads=[wk, 'H'], writes=[bk])
                    evac(ti, SCR16.v(KTo + ti * 512, [[1, 512]]), PS[b].v(0, [[1, 512]]), [bk], ['KT'])
                wb, wk = wload('w_in', loff, N_IN, O4 + g * 256 + hp * 128, 128, 8)
                for t4 in range(4):
                    b, bk = pbank((0, 1))
                    for tt in range(4):
                        ti = t4 * 4 + tt
                        r, blk = ti // nblk, ti % nblk
                        for k in range(8):
                            P.I('pe', 'matmul', PS[b].v(tt * 128, [[1, 128]]), H.v(k * NT + dil * 128 * blk + r, [[dil, 128]]),
                                wb.v(k * 128, [[1, 128]]), start=(k == 0), stop=(k == 7), reads=[wk, 'H'], writes=[bk])
                    evac(t4, SCR16.v(VTo + t4 * 512, [[1, 512]]), PS[b].v(0, [[1, 512]]), [bk], ['VT'])
                for h in range(2):
                    cc = alibi_slope(g * 4 + hp * 2 + h) * dil
                    P.I('dve', 'scalar_tensor_tensor', SCR32.v(BTo + h * 256, [[1, 256]]), CST.v(C_DIST, [[1, 256]]), -cc,
                        CST.v(C_MASK, [[1, 256]]), op0=ALU.mult, op1=ALU.add, reads=['CST'], writes=['BT'])
                wq, wqk = wload('w_in', loff, N_IN, O2 + g * 256 + hp * 128, 128, 8)
                nq = min(4, nblk)
                units = []
                for r in range(dil):
                    for qc in range(nblk // nq):
                        for bi in range(nq):
                            for h in range(2):
                                units.append((r, qc, bi, h))
                ust = {}

                def stage_a(u):
                    r, qc, bi, h = u
                    p0 = qc * nq * 128
                    width = nq * 128
                    if bi == 0 and h == 0:
                        qi = cnt['q'] % 2
                        cnt['q'] += 1
                        QTo = QTo0 + qi * 512
                        b, bk = pbank((0, 1))
                        for k in range(8):
                            P.I('pe', 'matmul', PS[b].v(0, [[1, width]]), wq.v(k * 128, [[1, 128]]),
                                H.v(k * NT + dil * p0 + r, [[dil, width]]), start=(k == 0), stop=(k == 7), reads=[wqk, 'H'], writes=[bk])
                        evac(qi, SCR16.v(QTo, [[1, width]]), PS[b].v(0, [[1, width]]), [bk], ['QT%d' % qi])
                        ust['q'] = (qi, QTo)
                    qi, QTo = ust['q']
                    if h == 0:
                        ni = cnt['n'] % 2
                        cnt['n'] += 1
                        ust['n'] = ni
                    ni = ust['n']
                    i = qc * nq + bi
                    si = cnt['s'] % 2
                    cnt['s'] += 1
                    stb, stk = 4 + si, 'PS%d' % (4 + si)
                    ncols = 256 if i >= 1 else 128
                    P.I('pe', 'matmul', PS[stb].v(0, [[1, 128]]), SCR16.v(KTo + dil * 128 * i + r, [[dil, 128]], p0=h * 64, pn=64),
                        SCR16.v(QTo + bi * 128, [[1, 128]], p0=h * 64, pn=64), start=True, stop=True,
                        reads=['KT', 'QT%d' % qi], writes=[stk])
                    if i >= 1:
                        P.I('pe', 'matmul', PS[stb].v(128, [[1, 128]]),
                            SCR16.v(KTo + dil * 128 * (i - 1) + r, [[dil, 128]], p0=h * 64, pn=64),
                            SCR16.v(QTo + bi * 128, [[1, 128]], p0=h * 64, pn=64), start=True, stop=True,
                            reads=['KT', 'QT%d' % qi], writes=[stk])
                    SSo = SSo0 + si * 256
                    PTo = PTo0 + si * 256
                    P.I('dve', 'scalar_tensor_tensor', SCR32.v(SSo, [[1, ncols]]), PS[stb].v(0, [[1, ncols]]), 0.125,
                        SCR32.v(BTo + h * 256, [[1, ncols]]), op0=ALU.mult, op1=ALU.add, reads=[stk, 'BT'], writes=['SS%d' % si])
                    P.I('act', 'activation', SCR16.v(PTo, [[1, ncols]]), SCR32.v(SSo, [[1, ncols]]), AF.Exp,
                        reads=['SS%d' % si], writes=['PT%d' % si])
                    return (r, i, h, si, ni, PTo)

                def stage_b(r, i, h, si, ni, PTo):
                    tcur = r * nblk + i
                    for (pb, pkey, lfn) in ((2 + ni, 'PS%d' % (2 + ni), lambda t: SCR16.v(VTo + t * 128 + h * 64, [[1, 64]])),
                                            (6 + ni, 'PS%d' % (6 + ni), lambda t: ONB.v(0, [[1, 64]]))):
                        P.I('pe', 'matmul', PS[pb].v(0, [[1, 128]], p0=h * 64, pn=64), lfn(tcur), SCR16.v(PTo, [[1, 128]]),
                            start=True, stop=(i == 0), reads=['VT', 'ONB', 'PT%d' % si], writes=[pkey])
                        if i >= 1:
                            P.I('pe', 'matmul', PS[pb].v(0, [[1, 128]], p0=h * 64, pn=64), lfn(tcur - 1),
                                SCR16.v(PTo + 128, [[1, 128]]), start=False, stop=True,
                                reads=['VT', 'ONB', 'PT%d' % si], writes=[pkey])
                    if h == 1:
                        tok0 = dil * 128 * i + r
                        an = SCR32.v(NUMo + tok0, [[dil, 128]])
                        ad = SCR32.v(DENo + tok0, [[dil, 128]])
                        if g == 0:
                            P.I('dve', 'tensor_copy', an, PS[2 + ni].v(0, [[1, 128]]), reads=['PS%d' % (2 + ni)], writes=['ACC'])
                            P.I('act', 'copy', ad, PS[6 + ni].v(0, [[1, 128]]), reads=['PS%d' % (6 + ni)], writes=['ACCD'])
                        else:
                            P.I('dve', 'tensor_tensor', an, an, PS[2 + ni].v(0, [[1, 128]]), op=ALU.add,
                                reads=['PS%d' % (2 + ni), 'ACC'], writes=['ACC'])
                            P.I('dve', 'tensor_tensor', ad, ad, PS[6 + ni].v(0, [[1, 128]]), op=ALU.add,
                                reads=['PS%d' % (6 + ni), 'ACCD'], writes=['ACCD'])

                cur = stage_a(units[0])
                for ui_ in range(len(units)):
                    nxt = stage_a(units[ui_ + 1]) if ui_ + 1 < len(units) else None
                    stage_b(*cur)
                    cur = nxt

            def attn_sample(g, hp):
                win, dil = PATTERNS[g]
                nt = 1 if g == 0 else 4
                wb, wk, offs = wload_multi([('w_in', loff, N_IN, O2 + g * 256 + hp * 128, 128, 8),
                                            ('w_in', loff, N_IN, O3 + g * 256 + hp * 128, 128, 8),
                                            ('w_in', loff, N_IN, O4 + g * 256 + hp * 128, 128, 8)])
                b, bk = pbank((0, 1, 2, 3))
                for k in range(8):
                    P.I('pe', 'matmul', PS[b].v(0, [[1, 64]]), wb.v(offs[0] + k * 128, [[1, 128]]), H.v(k * NT + TPR, [[1, 64]]),
                        start=(k == 0), stop=(k == 7), reads=[wk, 'H'], writes=[bk])
                for k in range(8):
                    P.I('pe', 'matmul', PS[b].v(64, [[1, 64]]), wb.v(offs[1] + k * 128, [[1, 128]]), H.v(k * NT + TPR, [[1, 64]]),
                        start=(k == 0), stop=(k == 7), reads=[wk, 'H'], writes=[bk])
                for k in range(8):
                    P.I('pe', 'matmul', PS[b].v(128, [[1, 128]], pn=64), H.v(k * NT + TPR, [[1, 64]]), wb.v(offs[2] + k * 128, [[1, 128]]),
                        start=(k == 0), stop=(k == 7), reads=[wk, 'H'], writes=[bk])
                P.I('dve', 'tensor_copy', SCR16.v(0, [[1, 128]]), PS[b].v(0, [[1, 128]]), reads=[bk], writes=['QKS'])
                P.I('dve', 'tensor_copy', SCR16.v(128, [[1, 128]], pn=64), PS[b].v(128, [[1, 128]], pn=64), reads=[bk], writes=['VS'])
                BTS0o, BCo, BTNo, SSNo, SSSo = 3232, 3240, 3264, 3392, 3488
                PTNo, PTSo = 6912, 6400
                for h in range(2):
                    cc = alibi_slope(g * 4 + hp * 2 + h) * dil
                    if g == 0:
                        P.I('dve', 'scalar_tensor_tensor', SCR32.v(BTNo + h * 64, [[1, 64]], pn=64), CST.v(C_DN0, [[1, 64]], pn=64), -cc,
                            CST.v(C_MN0, [[1, 64]], pn=64), op0=ALU.mult, op1=ALU.add, reads=['CST'], writes=['BTN'])
                        P.I('dve', 'scalar_tensor_tensor', SCR32.v(BTS0o + h * 4, [[1, 4]]), CST.v(C_DS0, [[1, 4]]), -cc,
                            CST.v(C_MS0, [[1, 4]]), op0=ALU.mult, op1=ALU.add, reads=['CST'], writes=['BTS'])
                    else:
                        P.I('dve', 'tensor_scalar', SCR32.v(BCo + h, [[1, 1]]), CST.v(C_DS1, [[1, 1]]), -cc, None, op0=ALU.mult,
                            reads=['CST'], writes=['BTS'])
                for h in range(2):
                    P.I('pe', 'matmul', PS[4].v(0, [[1, 64]], pn=64), SCR16.v(64, [[1, 64]], p0=h * 64, pn=64),
                        SCR16.v(0, [[1, 64]], p0=h * 64, pn=64), start=True, stop=True, reads=['QKS'], writes=['PS4'])
                    btn = SCR32.v(BTNo + h * 64, [[1, 64]], pn=64) if g == 0 else CST.v(C_MN1, [[1, 64]], pn=64)
                    P.I('dve', 'scalar_tensor_tensor', SCR32.v(SSNo, [[1, 64]], pn=64), PS[4].v(0, [[1, 64]], pn=64), 0.125, btn,
                        op0=ALU.mult, op1=ALU.add, reads=['PS4', 'BTN', 'CST'], writes=['SSN'])
                    P.I('act', 'activation', SCR16.v(PTNo, [[1, 64]], pn=64), SCR32.v(SSNo, [[1, 64]], pn=64), AF.Exp,
                        reads=['SSN'], writes=['PTN'])
                    P.I('pe', 'matmul', PS[6].v(256, [[1, 64]], p0=h * 64, pn=64), SCR16.v(128 + h * 64, [[1, 64]], pn=64),
                        SCR16.v(PTNo, [[1, 64]], pn=64), start=True, stop=True, reads=['VS', 'PTN'], writes=['PS6'])
                    P.I('pe', 'matmul', PS[7].v(256, [[1, 64]], p0=h * 64, pn=64), ONB.v(0, [[1, 64]], pn=64),
                        SCR16.v(PTNo, [[1, 64]], pn=64), start=True, stop=True, reads=['ONB', 'PTN'], writes=['PS7'])
                L = (128, 512, 512)[g]

                def stage_T(seq):
                    ci = cnt['c'] % 2
                    cnt['c'] += 1
                    Ksto, Vsto = 128 + ci * 512, 1152 + ci * 512
                    KTco, Vco = 4352 + ci * 512, 5376 + ci * 512
                    base = ((l * 16 + seq) * L) * 256 + hp * 128
                    for (nm, sto, key) in (('ck%d' % g, Ksto, 'KST%d' % ci), ('cv%d' % g, Vsto, 'VST%d' % ci)):
                        if g == 0:
                            P.dma('sp', SCR32.v(sto, [[1, 128]]), dap(dr[nm], base, [[256, 128], [1, 128]]), writes=[key])
                        elif g == 1:
                            P.dma('sp', SCR32.v(sto, [[128, 4], [1, 128]]), dap(dr[nm], base, [[4 * 256, 128], [256, 4], [1, 128]]), writes=[key])
                        else:
                            P.dma('sp', SCR32.v(sto, [[128, 4], [1, 128]]), dap(dr[nm], base, [[256, 128], [128 * 256, 4], [1, 128]]), writes=[key])
                    b, bk = pbank((0, 1, 2, 3))
                    for t in range(nt):
                        P.I('pe', 'transpose', PS[b].v(t * 128, [[1, 128]]), SCR32.v(Ksto + t * 128, [[1, 128]]), IDF(128),
                            reads=['KST%d' % ci, 'CST'], writes=[bk])
                    P.I('dve', 'tensor_copy', SCR16.v(KTco, [[1, nt * 128]]), PS[b].v(0, [[1, nt * 128]]), reads=[bk], writes=['KTC%d' % ci])
                    P.I('act', 'copy', SCR16.v(Vco, [[1, nt * 128]]), SCR32.v(Vsto, [[1, nt * 128]]), reads=['VST%d' % ci],
                        writes=['VC%d' % ci])
                    return ci, KTco, Vco

                def stage_Q(seq, h, ci, KTco, Vco):
                    si = cnt['s'] % 2
                    cnt['s'] += 1
                    stb, stk = 4 + si, 'PS%d' % (4 + si)
                    pts = SCR16.v(PTSo + si * 4, [[1, 4]])
                    if g == 0:
                        P.I('pe', 'matmul', PS[stb].v(0, [[1, 4]]), SCR16.v(KTco, [[1, 128]], p0=h * 64, pn=64),
                            SCR16.v(seq * 4, [[1, 4]], p0=h * 64, pn=64), start=True, stop=True, reads=['KTC%d' % ci, 'QKS'], writes=[stk])
                        P.I('dve', 'scalar_tensor_tensor', SCR32.v(SSSo + si * 4, [[1, 4]]), PS[stb].v(0, [[1, 4]]), 0.125,
                            SCR32.v(BTS0o + h * 4, [[1, 4]]), op0=ALU.mult, op1=ALU.add, reads=[stk, 'BTS'], writes=['SSS%d' % si])
                        P.I('act', 'activation', pts, SCR32.v(SSSo + si * 4, [[1, 4]]), AF.Exp, reads=['SSS%d' % si], writes=['PTS%d' % si])
                    else:
                        for j in range(4):
                            P.I('pe', 'matmul', PS[stb].v(j, [[1, 1]]), SCR16.v(KTco + j * 128, [[1, 128]], p0=h * 64, pn=64),
                                SCR16.v(seq * 4 + j, [[1, 1]], p0=h * 64, pn=64), start=True, stop=True,
                                reads=['KTC%d' % ci, 'QKS'], writes=[stk])
                        P.I('act', 'activation', pts, PS[stb].v(0, [[1, 4]]), AF.Exp, bias=SCR32.v(BCo + h, [[1, 1]]), scale=0.125,
                            reads=[stk, 'BTS'], writes=['PTS%d' % si])
                    return si

                def stage_V(seq, h, ci, Vco, si):
                    pts = SCR16.v(PTSo + si * 4, [[1, 4]])
                    if g == 0:
                        P.I('pe', 'matmul', PS[6].v(320 + seq * 4, [[1, 4]], p0=h * 64, pn=64), SCR16.v(Vco + h * 64, [[1, 64]]), pts,
                            start=True, stop=True, reads=['VC%d' % ci, 'PTS%d' % si], writes=['PS6'])
                        P.I('pe', 'matmul', PS[7].v(320 + seq * 4, [[1, 4]], p0=h * 64, pn=64), ONB.v(0, [[1, 64]]), pts,
                            start=True, stop=True, reads=['ONB', 'PTS%d' % si], writes=['PS7'])
                    else:
                        for j in range(4):
                            P.I('pe', 'matmul', PS[6].v(320 + seq * 4 + j, [[1, 1]], p0=h * 64, pn=64),
                                SCR16.v(Vco + j * 128 + h * 64, [[1, 64]]), SCR16.v(PTSo + si * 4 + j, [[1, 1]]),
                                start=True, stop=True, reads=['VC%d' % ci, 'PTS%d' % si], writes=['PS6'])
                            P.I('pe', 'matmul', PS[7].v(320 + seq * 4 + j, [[1, 1]], p0=h * 64, pn=64), ONB.v(0, [[1, 64]]),
                                SCR16.v(PTSo + si * 4 + j, [[1, 1]]), start=True, stop=True, reads=['ONB', 'PTS%d' % si], writes=['PS7'])

                tcur = stage_T(0)
                for seq in range(NSEQ):
                    tnext = stage_T(seq + 1) if seq + 1 < NSEQ else None
                    ci, KTco, Vco = tcur
                    s0 = stage_Q(seq, 0, ci, KTco, Vco)
                    s1 = stage_Q(seq, 1, ci, KTco, Vco)
                    stage_V(seq, 0, ci, Vco, s0)
                    stage_V(seq, 1, ci, Vco, s1)
                    tcur = tnext
                an, ad = SCR32.v(ANo, [[1, 64]]), SCR32.v(ADo, [[1, 64]])
                if g == 0:
                    P.I('dve', 'tensor_copy', an, PS[6].v(256, [[1, 64]]), reads=['PS6'], writes=['ACCS'])
                    P.I('dve', 'tensor_tensor', an, an, PS[6].v(320, [[1, 64]]), op=ALU.add, reads=['PS6', 'ACCS'], writes=['ACCS'])
                    P.I('dve', 'tensor_copy', ad, PS[7].v(256, [[1, 64]]), reads=['PS7'], writes=['ACCSD'])
                    P.I('dve', 'tensor_tensor', ad, ad, PS[7].v(320, [[1, 64]]), op=ALU.add, reads=['PS7', 'ACCSD'], writes=['ACCSD'])
                else:
                    for (a_, pb, k_, ka) in ((an, 6, 'PS6', 'ACCS'), (ad, 7, 'PS7', 'ACCSD')):
                        P.I('dve', 'tensor_tensor', a_, a_, PS[pb].v(256, [[1, 64]]), op=ALU.add, reads=[k_, ka], writes=[ka])
                        P.I('dve', 'tensor_tensor', a_, a_, PS[pb].v(320, [[1, 64]]), op=ALU.add, reads=[k_, ka], writes=[ka])

            for hp in range(2):
                P.phase = 'attn_p'
                for g in range(3):
                    attn_prompt(g, hp)
                P.barrier()
                P.phase = 'attn_s'
                if KSTOP != 'NOS':
                    for g in range(3):
                        attn_sample(g, hp)
                else:
                    P.I('pool', 'memset', SCR32.v(ANo, [[1, 64]]), 0.0, writes=['ACCS'])
                    P.I('pool', 'memset', SCR32.v(ADo, [[1, 64]]), 1.0, writes=['ACCSD'])
                P.I('dve', 'reciprocal', SCR32.v(DENo, [[1, 2048]]), SCR32.v(DENo, [[1, 2048]]), reads=['ACCD'], writes=['ACCD'])
                P.I('dve', 'tensor_tensor', SCR16.v(ATTo, [[1, 2048]]), SCR32.v(NUMo, [[1, 2048]]), SCR32.v(DENo, [[1, 2048]]),
                    op=ALU.mult, reads=['ACC', 'ACCD'], writes=['ATT'])
                P.I('dve', 'reciprocal', SCR32.v(ADo, [[1, 64]]), SCR32.v(ADo, [[1, 64]]), reads=['ACCSD'], writes=['ACCSD'])
                P.I('dve', 'tensor_tensor', SCR16.v(ATTo + 2048, [[1, 64]]), SCR32.v(ANo, [[1, 64]]), SCR32.v(ADo, [[1, 64]]),
                    op=ALU.mult, reads=['ACCS', 'ACCSD'], writes=['ATT'])
                P.phase = 'attn_out'
                gated_branch(O5 + 2048, lambda m, hp=hp: ('w_att', l * 256 * 1024 + hp * 128 * 1024, 1024, m * 128, 128, 1), 1,
                             lambda k, t0, w: SCR16.v(ATTo + t0, [[1, w]]), ['ATT'], False)
                P.barrier()

            P.phase = 'ssm'
            if KSTOP == 'B0':
                raise _StopMix()
            YTo, ESTo, TMSo, CAo = 0, 8448, 10496, 12544
            SHo = 6784
            YGo = 17984
            PPo = 9264
            loff = l * 1024 * N_IN
            for ch in range(4):
                wb, wk = wload('w_in', loff, N_IN, O1 + ch * 128, 128, 8)
                for (t0, w) in TILES:
                    b, bk = pbank(DENSE)
                    for k in range(8):
                        P.I('pe', 'matmul', PS[b].v(0, [[1, w]]), wb.v(k * 128, [[1, 128]]), H.v(k * NT + t0, [[1, w]]),
                            start=(k == 0), stop=(k == 7), reads=[wk, 'H'], writes=[bk])
                    if t0 < TPR:
                        evac(b, SCR16.v(YTo + ch * NT + t0 // 8, [[256, 8], [1, 64]]), PS[b].v(0, [[1, 8], [8, 64]]), [bk], ['YT%d' % ch])
                    else:
                        evac(b, SCR16.v(YTo + ch * NT + t0, [[1, w]]), PS[b].v(0, [[1, w]]), [bk], ['YT%d' % ch])
            P.barrier()

            def slot(k, n=4):
                return SCR32.v(PPo + 4 * k, [[1, n]])
            S_AR, S_AI, S_DT, S_MAG, S_C, S_S, S_ABR, S_ABI, S_FR, S_FI, S_T1, S_T2, S_T3, S_T4 = range(14)
            S_P = 14
            S_W = 32
            S_SQ = 48
            PWo = PPo + 4 * 56
            BSTo = PWo + 128
            CREo = BSTo + 256
            KBo = CREo + 128
            DMo = KBo + 128
            H0o = DMo + 4
            S4o = H0o + 128
            assert S4o + 128 <= SCRB // 4
            dv = lambda *a, **k: P.I('dve', *a, **k)
            KK = ['PP']

            def tt(o, a, b_, op):
                dv('tensor_tensor', o, a, b_, op=op, reads=KK, writes=KK)

            def cmul(or_, oi_, ar_, ai_, br_, bi_, t1, t2):
                tt(t1, ar_, br_, ALU.mult)
                tt(t2, ai_, bi_, ALU.mult)
                tt(or_, t1, t2, ALU.subtract)
                tt(t1, ar_, bi_, ALU.mult)
                tt(t2, ai_, br_, ALU.mult)
                tt(oi_, t1, t2, ALU.add)

            for ch in range(4):
                ukey = 'YT%d' % ch
                P.phase = 'ssm_ld'
                P.dma('sp', RS.v(0, [[1, 128]], pn=4), dap(dr['ssm_a_re'], l * 2048 + ch * 512, [[128, 4], [1, 128]]), writes=['STA'])
                P.dma('sp', RS.v(128, [[1, 128]], pn=4), dap(dr['ssm_a_im'], l * 2048 + ch * 512, [[128, 4], [1, 128]]), writes=['STA2'])
                for ri, nm in enumerate(('ssm_c_re', 'ssm_c_im')):
                    for q in range(4):
                        P.dma('sp', RS.v(256 + ri * 128, [[64, 2], [1, 64]], p0=q * 16, pn=16),
                              dap(dr[nm], l * 32768 + (8 * ch + 2 * q) * 1024, [[64, 16], [1024, 2], [1, 64]]), writes=['STC%d%d' % (ri, q)])
                for ri, nm in enumerate(('st_re', 'st_im')):
                    for q in range(4):
                        P.dma('sp', TMPA.v(ri * 128, [[1, 128]], p0=q * 16, pn=16),
                              dap(dr[nm], l * 16 * 2048 + (4 * ch + q) * 128, [[2048, 16], [1, 128]]), writes=['STH%d%d' % (ri, q)])
                for g2 in range(2):
                    P.dma('sp', SCR32.v(PPo + 4 * S_DT, [[1, 4]], p0=g2 * 64, pn=64),
                          dap(dr['ssm_log_dt'], l * 32 + 8 * ch + g2, [[0, 64], [2, 4]]), writes=['PPDT%d' % g2], allow_slow_non_contiguous=True)
                P.I('pool', 'memset', SCR32.v(BSTo, [[1, 256]]), 0.0, writes=['BST00', 'BST01', 'BST10', 'BST11'])
                for g2 in range(2):
                    for ri, nm in enumerate(('ssm_b_re', 'ssm_b_im')):
                        P.dma('sp', SCR32.v(BSTo + ri * 128 + g2 * 16, [[32, 4], [1, 16]], p0=g2 * 64, pn=64),
                              dap(dr[nm], l * 32768 + (8 * ch + g2) * 1024, [[16, 64], [2048, 4], [1, 16]]), writes=['BST%d%d' % (g2, ri)],
                              allow_slow_non_contiguous=True)
                P.dma('sp', SCR32.v(DMo, [[1, 1]]), dap(dr['ssm_d'], l * 512 + ch * 128, [[1, 128], [1, 1]]), writes=['PPD'],
                      allow_slow_non_contiguous=True)
                tb_, tbk = pbank((5, 6))
                P.I('pe', 'transpose', PS[tb_].v(0, [[1, 4]]), RS.v(0, [[1, 128]], pn=4), IDF(4), reads=['STA', 'CST'], writes=[tbk])
                P.I('pe', 'transpose', PS[tb_].v(4, [[1, 4]]), RS.v(128, [[1, 128]], pn=4), IDF(4), reads=['STA2', 'CST'], writes=[tbk])
                for ri in range(2):
                    P.I('pe', 'transpose', PS[tb_].v(64 + ri * 64, [[1, 64]]), RS.v(256 + ri * 128, [[1, 128]], pn=64), IDF(64),
                        reads=['STC%d%d' % (ri, q) for q in range(4)] + ['CST'], writes=[tbk])
                    P.I('pe', 'transpose', PS[tb_].v(192 + ri * 64, [[1, 64]]), TMPA.v(ri * 128, [[1, 128]], pn=64), IDF(64),
                        reads=['STH%d%d' % (ri, q) for q in range(4)] + ['CST'], writes=[tbk])
                LK = [tbk, 'PPDT0', 'PPDT1', 'PPD', 'BST00', 'BST01', 'BST10', 'BST11']
                dv('tensor_copy', SCR32.v(PPo, [[1, 8]]), PS[tb_].v(0, [[1, 8]]), reads=LK + KK, writes=KK)
                dv('tensor_copy', SCR32.v(CREo, [[1, 64]]), PS[tb_].v(64, [[1, 64]]), reads=LK + KK, writes=KK)
                dv('tensor_scalar', SCR32.v(CREo + 64, [[1, 64]]), PS[tb_].v(128, [[1, 64]]), -1.0, None, op0=ALU.mult, reads=LK + KK, writes=KK)
                for ri in range(2):
                    dv('tensor_copy', SCR32.v(H0o + ri * 16, [[32, 4], [1, 16]]), PS[tb_].v(192 + ri * 64, [[16, 4], [1, 16]]),
                       reads=LK + KK, writes=KK)
                AR, AI, DT, MAG, CC, SS_, ABR, ABI, FR, FI, T1, T2, T3, T4 = [slot(k) for k in range(14)]
                P.I('act', 'activation', DT, DT, AF.Exp, reads=KK, writes=KK)
                tt(T1, DT, AR, ALU.mult)
                P.I('act', 'activation', MAG, T1, AF.Exp, reads=KK, writes=KK)
                tt(T2, DT, AI, ALU.mult)
                P.I('act', 'activation', CC, T2, AF.Sin, bias=math.pi / 2, scale=1.0 / 32, reads=KK, writes=KK)
                P.I('act', 'activation', SS_, T2, AF.Sin, scale=1.0 / 32, reads=KK, writes=KK)
                for _ in range(5):
                    tt(T1, CC, CC, ALU.mult)
                    tt(T2, SS_, SS_, ALU.mult)
                    tt(T3, CC, SS_, ALU.mult)
                    tt(CC, T1, T2, ALU.subtract)
                    dv('tensor_scalar', SS_, T3, 2.0, None, op0=ALU.mult, reads=KK, writes=KK)
                tt(ABR, MAG, CC, ALU.mult)
                tt(ABI, MAG, SS_, ALU.mult)
                tt(T1, AR, AR, ALU.mult)
                tt(T2, AI, AI, ALU.mult)
                tt(T1, T1, T2, ALU.add)
                dv('reciprocal', T1, T1, reads=KK, writes=KK)
                dv('tensor_scalar', T2, ABR, -1.0, None, op0=ALU.add, reads=KK, writes=KK)
                tt(T3, T2, AR, ALU.mult)
                tt(T4, ABI, AI, ALU.mult)
                tt(T3, T3, T4, ALU.add)
                tt(FR, T3, T1, ALU.mult)
                tt(T3, ABI, AR, ALU.mult)
                tt(T4, T2, AI, ALU.mult)
                tt(T3, T3, T4, ALU.subtract)
                tt(FI, T3, T1, ALU.mult)
                Pr = lambda j: slot(S_P + 2 * j)
                Pi = lambda j: slot(S_P + 2 * j + 1)
                Wr = lambda j: slot(S_W + 2 * j)
                Wi = lambda j: slot(S_W + 2 * j + 1)
                P.I('pool', 'memset', Pr(0), 1.0, reads=KK, writes=KK)
                P.I('pool', 'memset', Pi(0), 0.0, reads=KK, writes=KK)
                dv('tensor_copy', Pr(1), ABR, reads=KK, writes=KK)
                dv('tensor_copy', Pi(1), ABI, reads=KK, writes=KK)
                def pv(base_slot, j0, n):
                    return SCR32.v(PPo + 4 * (base_slot + 2 * j0), [[8, n], [1, 4]])

                def bcn(sl_, n):
                    return SCR32.v(PPo + 4 * sl_, [[0, n], [1, 4]])
                cmul(Pr(2), Pi(2), Pr(1), Pi(1), ABR, ABI, T1, T2)
                for (j0, n_, src0) in ((3, 2, 1), (5, 4, 1)):
                    mj = j0 - src0
                    cmul(pv(S_P, j0, n_), pv(S_P + 1, j0, n_), pv(S_P, src0, n_), pv(S_P + 1, src0, n_),
                         bcn(S_P + 2 * mj, n_), bcn(S_P + 2 * mj + 1, n_), TMPA.v(0, [[4, n_], [1, 4]]), TMPA.v(64, [[4, n_], [1, 4]]))
                cmul(pv(S_W, 0, 8), pv(S_W + 1, 0, 8), pv(S_P, 0, 8), pv(S_P + 1, 0, 8), bcn(S_FR, 8), bcn(S_FI, 8),
                     TMPA.v(0, [[4, 8], [1, 4]]), TMPA.v(64, [[4, 8], [1, 4]]))
                SQr = lambda k: slot(S_SQ + 2 * k)
                SQi = lambda k: slot(S_SQ + 2 * k + 1)
                prev_r, prev_i = Pr(8), Pi(8)
                for k in range(7):
                    cmul(SQr(k), SQi(k), prev_r, prev_i, prev_r, prev_i, T1, T2)
                    prev_r, prev_i = SQr(k), SQi(k)
                bc = lambda sl_, n: SCR32.v(PPo + 4 * sl_, [[1, 4], [0, n]])
                S_PW3 = S_SQ + 14
                dv('tensor_copy', slot(S_PW3), Pr(8), reads=KK, writes=KK)
                dv('tensor_copy', slot(S_PW3 + 1), Pi(8), reads=KK, writes=KK)
                dv('tensor_copy', slot(S_PW3 + 2), SQr(0), reads=KK, writes=KK)
                dv('tensor_copy', slot(S_PW3 + 3), SQi(0), reads=KK, writes=KK)
                cmul(slot(S_PW3 + 4), slot(S_PW3 + 5), SQr(0), SQi(0), Pr(8), Pi(8), T1, T2)
                P.phase = 'ssm_E'
                BRE = SCR32.v(BSTo, [[32, 4], [1, 32]])
                BIM = SCR32.v(BSTo + 128, [[32, 4], [1, 32]])
                ER = TMPA.v(0, [[32, 4], [1, 32]])
                EI = TMPA.v(128, [[32, 4], [1, 32]])
                ET = TMPA.v(256, [[32, 4], [1, 32]])
                P.I('dve', 'memset', PS[7].v(0, [[1, 512]]), 0.0, reads=['PS7'], writes=['PS7'])
                for j in range(8):
                    wr_, wi_ = bc(S_W + 2 * j, 32), bc(S_W + 2 * j + 1, 32)
                    dv('tensor_tensor', ER, BRE, wr_, op=ALU.mult, reads=KK + ['TMPA'], writes=['TMPA'])
                    dv('tensor_tensor', ET, BIM, wi_, op=ALU.mult, reads=KK + ['TMPA'], writes=['TMPA'])
                    dv('tensor_tensor', ER, ER, ET, op=ALU.subtract, reads=['TMPA'], writes=['TMPA'])
                    dv('tensor_tensor', EI, BIM, wr_, op=ALU.mult, reads=KK + ['TMPA'], writes=['TMPA'])
                    dv('tensor_tensor', ET, BRE, wi_, op=ALU.mult, reads=KK + ['TMPA'], writes=['TMPA'])
                    dv('tensor_tensor', EI, EI, ET, op=ALU.add, reads=['TMPA'], writes=['TMPA'])
                    b, bk = pbank((5, 6))
                    P.I('pe', 'transpose', PS[b].v(0, [[1, 128]]), TMPA.v(0, [[1, 128]]), IDF(128), reads=['TMPA', 'CST'], writes=[bk])
                    P.I('pe', 'transpose', PS[b].v(128, [[1, 128]]), TMPA.v(128, [[1, 128]]), IDF(128), reads=['TMPA', 'CST'], writes=[bk])
                    P.I('act', 'copy', SCR16.v(ESTo + j * 256, [[1, 256]]), PS[b].v(0, [[1, 256]]), reads=[bk], writes=['EST'])
                    P.I('pe', 'matmul', PS[7].v(j * 64, [[1, 64]]), TMPA.v(0, [[1, 128]]), SCR32.v(CREo, [[1, 64]]), start=True, stop=False,
                        reads=['TMPA'] + KK, writes=['PS7'])
                    P.I('pe', 'matmul', PS[7].v(j * 64, [[1, 64]]), TMPA.v(128, [[1, 128]]), SCR32.v(CREo + 64, [[1, 64]]), start=False, stop=True,
                        reads=['TMPA'] + KK, writes=['PS7'])
                dv('tensor_tensor', TMPA.v(0, [[64, 8], [16, 4], [1, 16]]), PS[7].v(0, [[64, 8], [16, 4], [1, 16]]),
                   CST.v(C_QM, [[0, 8], [1, 4], [0, 16]]), op=ALU.mult, reads=['PS7', 'CST', 'TMPA'], writes=['TMPA'])
                dv('tensor_reduce', SCR32.v(KBo, [[16, 8], [1, 16]]), TMPA.v(0, [[64, 8], [1, 16], [16, 4]]), axis=AX.X, op=ALU.add,
                   reads=['TMPA'] + KK, writes=KK)
                dv('scalar_tensor_tensor', SCR32.v(KBo, [[1, 16]]), CST.v(C_DELTA, [[1, 16]]), SCR32.v(DMo, [[1, 1]]), SCR32.v(KBo, [[1, 16]]),
                   op0=ALU.mult, op1=ALU.add, reads=KK + ['CST'], writes=KK)
                P.I('pool', 'memset', SCR16.v(TMSo, [[1, 2048]]), 0.0, reads=['TMS'], writes=['TMS'])
                for g2s in range(2):
                    for s_ in range(8):
                        n_ = (8 - s_) * 16
                        dv('tensor_scalar', SCR16.v(TMSo + (g2s * 8 + s_) * 128 + s_ * 16, [[1, n_]]), SCR32.v(KBo, [[1, n_]]),
                           CST.v(C_MG2 + g2s, [[1, 1]]), None, op0=ALU.mult, reads=KK + ['CST', 'TMS'], writes=['TMS'])
                CRE4 = SCR32.v(CREo, [[16, 4], [0, 8], [1, 16]])
                NCI4 = SCR32.v(CREo + 64, [[16, 4], [0, 8], [1, 16]])
                PR4 = SCR32.v(PPo + 4 * (S_P + 2), [[1, 4], [8, 8], [0, 16]])
                PI4 = SCR32.v(PPo + 4 * (S_P + 3), [[1, 4], [8, 8], [0, 16]])
                A1 = TMPA.v(0, [[128, 4], [16, 8], [1, 16]])
                A2 = RS.v(0, [[128, 4], [16, 8], [1, 16]])
                CK = KK + ['TMPA', 'RS']
                dv('tensor_tensor', A1, CRE4, PR4, op=ALU.mult, reads=CK, writes=['TMPA'])
                dv('tensor_tensor', A2, NCI4, PI4, op=ALU.mult, reads=CK, writes=['RS'])
                dv('tensor_tensor', SCR16.v(CAo, [[256, 4], [16, 8], [1, 16]]), A1, A2, op=ALU.add, reads=['TMPA', 'RS', 'CA'], writes=['CA'])
                dv('tensor_tensor', A1, NCI4, PR4, op=ALU.mult, reads=CK, writes=['TMPA'])
                dv('tensor_tensor', A2, CRE4, PI4, op=ALU.mult, reads=CK, writes=['RS'])
                dv('tensor_tensor', SCR16.v(CAo + 128, [[256, 4], [16, 8], [1, 16]]), A1, A2, op=ALU.subtract, reads=['TMPA', 'RS', 'CA'], writes=['CA'])
                P.phase = 'ssm_inj'
                P.I('pool', 'memset', SCR32.v(SHo, [[276, 8], [1, 1]]), 0.0, reads=['SH'], writes=['SH'])
                for q in range(4):
                    for ri in range(2):
                        b, bk = pbank((5, 6))
                        for s_ in range(8):
                            P.I('pe', 'matmul', PS[b].v(0, [[1, 256]]), SCR16.v(ESTo + ((7 - s_) * 2 + ri) * 128, [[1, 128]], p0=32 * q, pn=32),
                                SCR16.v(YTo + ch * NT + s_ * 256, [[1, 256]], p0=32 * q, pn=32), start=(s_ == 0), stop=(s_ == 7),
                                tile_position=(32 * q, 0), reads=['EST', ukey], writes=[bk])
                        for s_ in range(4):
                            P.I('pe', 'matmul', PS[b].v(256, [[1, 16]]), SCR16.v(ESTo + ((3 - s_) * 2 + ri) * 128, [[1, 128]], p0=32 * q, pn=32),
                                SCR16.v(YTo + ch * NT + TPR + s_, [[4, 16]], p0=32 * q, pn=32), start=(s_ == 0), stop=(s_ == 3),
                                tile_position=(32 * q, 0), reads=['EST', ukey], writes=[bk])
                        P.I('act', 'copy', SCR32.v(SHo + (q * 2 + ri) * 276 + 1, [[1, 256]]), PS[b].v(0, [[1, 256]]), reads=[bk], writes=['SH'])
                        P.I('act', 'copy', SCR32.v(S4o + (q * 2 + ri) * 16, [[1, 16]]), PS[b].v(256, [[1, 16]]), reads=[bk], writes=KK)
                KS = ['SH']

                def hv(ri, col0, dims):
                    return SCR32.v(SHo + ri * 276 + col0, [[552, 4]] + dims)

                def st2(o, a, b_, op):
                    dv('tensor_tensor', o, a, b_, op=op, reads=KS + KK + ['TMPA', 'RS'], writes=KS)

                def tmp2(o, a, b_, key):
                    dv('tensor_tensor', o, a, b_, op=ALU.mult, reads=KS + KK + [key], writes=[key])
                a8r, a8i = bc(S_P + 16, 64), bc(S_P + 17, 64)
                U1 = TMPA.v(0, [[64, 4], [1, 64]])
                U2 = RS.v(0, [[64, 4], [1, 64]])
                for jj in range(1, 4):
                    hr, hi = hv(0, 1 + jj, [[4, 64]]), hv(1, 1 + jj, [[4, 64]])
                    pr_, pi_ = hv(0, jj, [[4, 64]]), hv(1, jj, [[4, 64]])
                    tmp2(U1, a8r, pr_, 'TMPA')
                    tmp2(U2, a8i, pi_, 'RS')
                    st2(hr, hr, U1, ALU.add)
                    st2(hr, hr, U2, ALU.subtract)
                    tmp2(U1, a8r, pi_, 'TMPA')
                    tmp2(U2, a8i, pr_, 'RS')
                    st2(hi, hi, U1, ALU.add)
                    st2(hi, hi, U2, ALU.add)
                for di, d_ in enumerate((1, 2, 4, 8, 16, 32)):
                    n_ = 64 - d_
                    adr, adi = bc(S_SQ + 2 * (di + 1), n_), bc(S_SQ + 2 * (di + 1) + 1, n_)
                    sr, si_ = hv(0, 4, [[4, n_]]), hv(1, 4, [[4, n_]])
                    dr_, di2 = hv(0, 4 * d_ + 4, [[4, n_]]), hv(1, 4 * d_ + 4, [[4, n_]])
                    q1, q2 = TMPA.v(0, [[64, 4], [1, n_]]), TMPA.v(256, [[64, 4], [1, n_]])
                    q3, q4 = RS.v(0, [[64, 4], [1, n_]]), RS.v(256, [[64, 4], [1, n_]])
                    tmp2(q1, adr, sr, 'TMPA')
                    tmp2(q2, adi, si_, 'TMPA')
                    tmp2(q3, adr, si_, 'RS')
                    tmp2(q4, adi, sr, 'RS')
                    st2(dr_, dr_, q1, ALU.add)
                    st2(dr_, dr_, q2, ALU.subtract)
                    st2(di2, di2, q3, ALU.add)
                    st2(di2, di2, q4, ALU.add)
                for q in range(4):
                    def hq(ri, col0, dims):
                        return SCR32.v(SHo + (q * 2 + ri) * 276 + col0, dims)
                    er = hq(0, 4, [[4, 63], [0, 3]])
                    ei = hq(1, 4, [[4, 63], [0, 3]])
                    hr = hq(0, 5, [[4, 63], [1, 3]])
                    hi = hq(1, 5, [[4, 63], [1, 3]])
                    pwr = SCR32.v(PPo + 4 * S_PW3 + q, [[0, 63], [8, 3]])
                    pwi = SCR32.v(PPo + 4 * (S_PW3 + 1) + q, [[0, 63], [8, 3]])
                    F1 = TMPA.v(0, [[3, 63], [1, 3]])
                    F2 = RS.v(0, [[3, 63], [1, 3]])
                    tmp2(F1, pwr, er, 'TMPA')
                    tmp2(F2, pwi, ei, 'RS')
                    st2(hr, hr, F1, ALU.add)
                    st2(hr, hr, F2, ALU.subtract)
                    tmp2(F1, pwr, ei, 'TMPA')
                    tmp2(F2, pwi, er, 'RS')
                    st2(hi, hi, F1, ALU.add)
                    st2(hi, hi, F2, ALU.add)
                for ri in range(2):
                    dv('tensor_copy', SCR32.v(SHo + ri * 276 + 257, [[552, 4], [1, 16]]), SCR32.v(H0o + ri * 16, [[32, 4], [1, 16]]),
                       reads=KK + KS, writes=KS)
                h0r, h0i = SCR32.v(H0o, [[32, 4], [1, 16]]), SCR32.v(H0o + 16, [[32, 4], [1, 16]])
                s4r, s4i = SCR32.v(S4o, [[32, 4], [1, 16]]), SCR32.v(S4o + 16, [[32, 4], [1, 16]])
                p4r, p4i = bc(S_P + 8, 16), bc(S_P + 9, 16)
                W1 = TMPA.v(0, [[16, 4], [1, 16]])
                for (o_, x1, y1, x2, y2, op2) in ((s4r, p4r, h0r, p4i, h0i, ALU.subtract), (s4i, p4r, h0i, p4i, h0r, ALU.add)):
                    dv('tensor_tensor', W1, x1, y1, op=ALU.mult, reads=KK + ['TMPA'], writes=['TMPA'])
                    dv('tensor_tensor', o_, o_, W1, op=ALU.add, reads=KK + ['TMPA'], writes=KK)
                    dv('tensor_tensor', W1, x2, y2, op=ALU.mult, reads=KK + ['TMPA'], writes=['TMPA'])
                    dv('tensor_tensor', o_, o_, W1, op=op2, reads=KK + ['TMPA'], writes=KK)
                P.phase = 'ssm_so'
                ob, obk = pbank((5, 6))
                for ri in range(2):
                    P.I('pe', 'transpose', PS[ob].v(ri * 128, [[1, 128]], pn=4), SCR32.v(SHo + ri * 276 + 256, [[552, 4]]), IDF(128),
                        reads=KS + ['CST'], writes=[obk])
                    dv('tensor_copy', RS.v(256 + ri * 64, [[16, 4], [1, 16]]), SCR32.v(S4o + ri * 16, [[32, 4], [1, 16]]),
                       reads=KK, writes=['STC%d%d' % (r_, q_) for r_ in range(2) for q_ in range(4)])
                    P.I('pe', 'transpose', PS[ob].v(256 + ri * 128, [[1, 128]], pn=64), RS.v(256 + ri * 64, [[1, 64]]), IDF(128),
                        reads=['STC00', 'CST'], writes=[obk])
                dv('tensor_copy', RS.v(0, [[1, 256]], pn=4), PS[ob].v(0, [[1, 256]], pn=4), reads=[obk, 'RS', 'STA', 'STA2'], writes=['STO1'])
                dv('tensor_copy', TMPA.v(0, [[1, 256]], pn=64), PS[ob].v(256, [[1, 256]], pn=64), reads=[obk, 'TMPA'], writes=['STO2', 'TMPA'])
                for ri, (np_, ns_) in enumerate((('ssm_re_p', 'ssm_re_s'), ('ssm_im_p', 'ssm_im_s'))):
                    P.dma('sp', dap(dr[np_], l * 2048 + ch * 512, [[128, 4], [1, 128]]), RS.v(ri * 128, [[1, 128]], pn=4),
                          reads=['STO1', 'RS'], writes=['OUT' + np_], out=True)
                    for q in range(4):
                        P.dma('sp', dap(dr[ns_], l * 16 * 2048 + (4 * ch + q) * 128, [[2048, 16], [1, 128]]),
                              TMPA.v(ri * 128, [[1, 128]], p0=q * 16, pn=16), reads=['STO2', 'TMPA'], writes=['OUT' + ns_], out=True)
                dv('tensor_copy', SCR16.v(2 * SHo, [[552, 8], [1, 273]]), SCR32.v(SHo, [[276, 8], [1, 273]]), reads=KS, writes=['SHB', 'SH'])
                P.phase = 'ssm_Y'
                for b_ in range(5):
                    P.I('dve', 'memset', PS[b_].v(0, [[1, 512]]), 0.0, reads=['PS%d' % b_], writes=['PS%d' % b_])
                def emit_Y(gl):
                    q, g2 = gl // 2, gl % 2
                    yb, ybk = pbank((5, 6))
                    for s_ in range(8):
                        P.I('pe', 'matmul', PS[yb].v(0, [[1, 256]]), SCR16.v(TMSo + (g2 * 8 + s_) * 128, [[1, 128]], p0=32 * q, pn=32),
                            SCR16.v(YTo + ch * NT + s_ * 256, [[1, 256]], p0=32 * q, pn=32), start=(s_ == 0), stop=False,
                            tile_position=(32 * q, 0), reads=['TMS', ukey], writes=[ybk])
                    for ri in range(2):
                        P.I('pe', 'matmul', PS[yb].v(0, [[1, 256]]), SCR16.v(CAo + (q * 2 + ri) * 128, [[1, 128]], p0=64 * g2, pn=64),
                            SCR16.v(2 * SHo + (q * 2 + ri) * 552, [[1, 256]], p0=64 * g2, pn=64), start=False, stop=(ri == 1),
                            sync_prev=(ri == 0), reads=['CA', 'SHB'], writes=[ybk])
                    for s_ in range(4):
                        P.I('pe', 'matmul', PS[yb].v(256, [[1, 16]]), SCR16.v(TMSo + (g2 * 8 + s_) * 128, [[1, 128]], p0=32 * q, pn=32),
                            SCR16.v(YTo + ch * NT + TPR + s_, [[4, 16]], p0=32 * q, pn=32), start=(s_ == 0), stop=False,
                            tile_position=(32 * q, 0), sync_prev=(s_ == 0), reads=['TMS', ukey], writes=[ybk])
                    for ri in range(2):
                        P.I('pe', 'matmul', PS[yb].v(256, [[1, 16]]), SCR16.v(CAo + (q * 2 + ri) * 128, [[1, 128]], p0=64 * g2, pn=64),
                            SCR16.v(2 * SHo + (q * 2 + ri) * 552 + 257, [[1, 16]], p0=64 * g2, pn=64), start=False, stop=(ri == 1),
                            sync_prev=(ri == 0), reads=['CA', 'SHB'], writes=[ybk])
                    return yb, ybk

                def emit_post(gl, yb, ybk):
                    yg = SCR16.v(YGo + (gl % 2) * 272, [[1, 272]])
                    P.I('act', 'activation', yg, PS[yb].v(0, [[1, 272]]), AF.Gelu_apprx_tanh, reads=[ybk], writes=['YG%d' % (gl % 2)])
                    for t in range(8):
                        sel = SELB.v(((t % 2) * 8 + gl) * 128, [[1, 128]], p0=32 * (t // 2), pn=32)
                        for tb in range(4):
                            P.I('pe', 'matmul', PS[tb].v(t, [[8, 64]]), sel, SCR16.v(YGo + (gl % 2) * 272 + 64 * tb, [[1, 64]], p0=32 * (t // 2), pn=32),
                                start=False, stop=False, skip_group_check=True, tile_position=(32 * (t // 2), 0), sync_prev=(tb == 0 and t % 2 == 0),
                                reads=['SELB', 'YG%d' % (gl % 2)], writes=['PS%d' % tb])
                        if t < 4:
                            P.I('pe', 'matmul', PS[4].v(t, [[4, 16]]), sel, SCR16.v(YGo + (gl % 2) * 272 + 256, [[1, 16]], p0=32 * (t // 2), pn=32),
                                start=False, stop=False, skip_group_check=True, tile_position=(32 * (t // 2), 0),
                                reads=['SELB', 'YG%d' % (gl % 2)], writes=['PS4'])

                ycur = emit_Y(0)
                for gl in range(8):
                    ynext = emit_Y(gl + 1) if gl < 7 else None
                    emit_post(gl, *ycur)
                    ycur = ynext
                for tb in range(4):
                    evac(tb, SCR16.v(YTo + ch * NT + tb * 512, [[1, 512]]), PS[tb].v(0, [[1, 512]]), ['PS%d' % tb], [ukey])
                evac(0, SCR16.v(YTo + ch * NT + TPR, [[1, 64]]), PS[4].v(0, [[1, 64]]), ['PS4'], [ukey])
                P.barrier()
                if KSTOP == 'B1':
                    raise _StopMix()

            P.phase = 'ssm_glu'
            for m in range(8):
                wb, wk, offs = wload_multi([('w_in', loff, N_IN, O5 + 1024 + m * 128, 128, 8),
                                            ('w_ssm_glu', l * 512 * 2048, 2048, m * 128, 128, 4),
                                            ('w_ssm_glu', l * 512 * 2048, 2048, 1024 + m * 128, 128, 4)])
                for (t0, w) in TILES:
                    bg, kg = pbank(DENSE)
                    b1, k1 = pbank(DENSE)
                    b2, k2 = pbank(DENSE)
                    for k in range(8):
                        P.I('pe', 'matmul', PS[bg].v(0, [[1, w]]), wb.v(offs[0] + k * 128, [[1, 128]]), H.v(k * NT + t0, [[1, w]]),
                            start=(k == 0), stop=(k == 7), reads=[wk, 'H'], writes=[kg])
                    for (bb_, kb_, oo) in ((b1, k1, offs[1]), (b2, k2, offs[2])):
                        for k in range(4):
                            P.I('pe', 'matmul', PS[bb_].v(0, [[1, w]]), wb.v(oo + k * 128, [[1, 128]]), SCR16.v(YTo + k * NT + t0, [[1, w]]),
                                start=(k == 0), stop=(k == 3), reads=[wk, 'YT0', 'YT1', 'YT2', 'YT3'], writes=[kb_])
                    TA = TMPA.v(0, [[1, w]])
                    TB = RS.v(0, [[1, w]])
                    P.I('act', 'activation', TA, PS[bg].v(0, [[1, w]]), AF.Sigmoid, reads=[kg], writes=['TMPA'])
                    P.I('act', 'activation', TB, PS[b2].v(0, [[1, w]]), AF.Sigmoid, reads=[k2], writes=['RS'])
                    dv('tensor_tensor', TB, TB, PS[b1].v(0, [[1, w]]), op=ALU.mult, reads=['RS', k1], writes=['RS'])
                    dv('tensor_tensor', TA, TA, TB, op=ALU.mult, reads=['TMPA', 'RS'], writes=['TMPA'])
                    P.I('dve', 'tensor_tensor', MG.v(m * NT + t0, [[1, w]]), MG.v(m * NT + t0, [[1, w]]), TA, op=ALU.add,
                        reads=['TMPA', 'MG'], writes=['MG'])
            P.barrier()
            if KSTOP == 'A7':
                raise _StopMix()
            P.phase = 'wout'
            MOoD = 2048

            def MOapD(k, w):
                if w is None:
                    return SCR32.v(MOoD + k * 512, [[4, 16], [1, 4]])
                return SCR32.v(MOoD + k * 512, [[1, w]])
            wo_ = [wload('w_out', l * 1024 * 1024, 1024, hh * 512, 512, 8) for hh in range(2)]
            for (t0, w) in TILES:
                for m in range(8):
                    wb, wk = wo_[m // 4]
                    b, bk = pbank(DENSE)
                    for k in range(8):
                        P.I('pe', 'matmul', PS[b].v(0, [[1, w]]), wb.v(k * 512 + (m % 4) * 128, [[1, 128]]), MG.v(k * NT + t0, [[1, w]]),
                            start=(k == 0), stop=(k == 7), reads=[wk, 'MG'], writes=[bk])
                    evac(m, SCR32.v(MOoD + m * 512, [[1, w]]), PS[b].v(0, [[1, w]]), [bk], ['MO'])
                if KSTOP != 'D1':
                    post_norm_add(MOapD, t0, w, GMt, 6144)
            P.barrier()
        try:
            mixer()
        except _StopMix:
            P.barrier()

        P.phase = 'ffn'
        pre_norm(AFt, BFv)
        P.barrier()
        FW = 1088
        MOoff = 5856

        def Fap(j, c, w):
            if j < 15:
                return MG.v(j * FW + c, [[1, w]])
            return SCR16.v(4096 + (j - 15) * FW + c, [[1, w]])

        def MOap(k, w):
            if w is None:
                return SCR32.v(MOoff + k * 512, [[4, 16], [1, 4]])
            return SCR32.v(MOoff + k * 512, [[1, w]])

        for half in range(2):
            htiles = TILES[0:2] if half == 0 else TILES[2:5]
            hc0 = htiles[0][0]
            for j in range(22):
                wb, wk = wload('w_ffn_in', l * 1024 * 5632, 5632, 0, 0, 8, pieces=[(j * 128, 128), (D_FF + j * 128, 128)])
                for (t0, w) in htiles:
                    bg, bgk = pbank(DENSE)
                    bu, buk = pbank(DENSE)
                    for k in range(8):
                        P.I('pe', 'matmul', PS[bg].v(0, [[1, w]]), wb.v(k * 256, [[1, 128]]), H.v(k * NT + t0, [[1, w]]),
                            start=(k == 0), stop=(k == 7), reads=[wk, 'H'], writes=[bgk])
                    for k in range(8):
                        P.I('pe', 'matmul', PS[bu].v(0, [[1, w]]), wb.v(k * 256 + 128, [[1, 128]]), H.v(k * NT + t0, [[1, w]]),
                            start=(k == 0), stop=(k == 7), reads=[wk, 'H'], writes=[buk])
                    tq, tqk = tmpbuf(w)
                    P.I('act', 'activation', tq, PS[bg].v(0, [[1, w]]), AF.Silu, reads=[bgk], writes=[tqk])
                    P.I('dve', 'tensor_tensor', Fap(j, t0 - hc0, w), tq, PS[bu].v(0, [[1, w]]), op=ALU.mult,
                        reads=[tqk, buk], writes=['F'])
            for (t0, w) in htiles:
                for m in range(8):
                    wb, wk = wload('w_ffn_out', l * D_FF * 1024, 1024, m * 128, 128, 22)
                    b, bk = pbank(DENSE)
                    for j in range(22):
                        P.I('pe', 'matmul', PS[b].v(0, [[1, w]]), wb.v(j * 128, [[1, 128]]), Fap(j, t0 - hc0, w),
                            start=(j == 0), stop=(j == 21), reads=[wk, 'F'], writes=[bk])
                    evac(m, SCR32.v(MOoff + m * 512, [[1, w]]), PS[b].v(0, [[1, w]]), [bk], ['MO'])
                post_norm_add(MOap, t0, w, GFt, 9952)
            P.barrier()

    P.barrier()
    for ti in range(17):
        rows_ = 128 if ti < 16 else 64
        t0 = ti * 128
        so = (ti % 2) * 1024
        skey = 'YST%d' % (ti % 2)
        for c0 in (0, 4):
            b, bk = pbank((4, 5))
            for c in range(4):
                P.I('pe', 'transpose', PS[b].v(c * 128, [[1, 128]], pn=rows_), X.v((c0 + c) * NT + t0, [[1, rows_]]), IDF(128),
                    reads=['X', 'CST'], writes=[bk])
            evac(c0 // 4, SCR32.v(so + c0 * 128, [[1, 512]], pn=rows_), PS[b].v(0, [[1, 512]], pn=rows_), [bk], [skey + '_%d' % c0])
        if ti < 16:
            dst_ = dap(dr['y_p'], t0 * 1024, [[1024, 128], [1, 1024]])
        else:
            dst_ = dap(dr['y_s'], 0, [[1024, 64], [1, 1024]])
        P.dma('sp', dst_, SCR32.v(so, [[1, 1024]], pn=rows_), reads=[skey + '_0', skey + '_4'], writes=['OUTy'], out=True)

    P.finish()
    return nc


_CACHE = {}


def _get_prog():
    if 'nc' not in _CACHE:
        _CACHE['nc'] = build_program()
        _CACHE['consts'] = (make_consts(), make_sel())
    return _CACHE['nc'], _CACHE['consts']


def kernel(**inp):
    nc, consts = _get_prog()
    f = lambda a: np.ascontiguousarray(np.asarray(a, dtype=np.float32))
    shared = {}
    for name in ("w_mod", "b_mod", "g_pre_mix", "g_post_mix", "g_pre_ffn", "g_post_ffn", "w_in", "conv_w", "conv_b",
                 "conv_ln_g", "conv_ln_b", "w_conv_out", "ssm_log_dt", "w_ssm_glu", "w_att", "w_out", "w_ffn_in", "w_ffn_out"):
        shared[name] = f(inp[name])
    for name in ("ssm_a_re", "ssm_a_im", "ssm_b_re", "ssm_b_im", "ssm_c_re", "ssm_c_im", "ssm_d"):
        shared[name] = f(inp[name]).reshape(2, -1)
    shared["consts"] = consts[0]
    shared["selc"] = consts[1]
    in_maps = []
    for c in range(8):
        s = slice(c * 16, (c + 1) * 16)
        m = dict(shared)
        m["x_p"] = f(inp["x_prompt"][c])
        m["x_s"] = f(inp["x_sample"][s]).reshape(64, 1024)
        m["c_all"] = f(np.concatenate([np.asarray(inp["c_prompt"])[c:c + 1], np.asarray(inp["c_sample"])[s]], axis=0))
        for g in range(2):
            m["ck%d" % g] = f(np.asarray(inp["cache_k%d" % g])[:, s]).reshape(2, 16, -1, 256)
            m["cv%d" % g] = f(np.asarray(inp["cache_v%d" % g])[:, s]).reshape(2, 16, -1, 256)
        for nm, src in (("ck2", "cache_k2"), ("cv2", "cache_v2")):
            a = np.asarray(inp[src])[:, s].reshape(2, 16, 128, 16, 256)[:, :, :, 0:4]
            m[nm] = f(a.transpose(0, 1, 3, 2, 4)).reshape(2, 16, 512, 256)
        m["st_conv"] = f(np.asarray(inp["state_conv"])[:, s])
        m["st_re"] = f(np.asarray(inp["state_ssm_re"])[:, s]).reshape(2, 16, 2048)
        m["st_im"] = f(np.asarray(inp["state_ssm_im"])[:, s]).reshape(2, 16, 2048)
        in_maps.append(m)
    res = run_bass_kernel_spmd(nc, in_maps, core_ids=list(range(8)))
    R = res.results
    cat = lambda name, ax: np.concatenate([np.asarray(r[name]) for r in R], axis=ax)
    stk = lambda name: np.stack([np.asarray(r[name]) for r in R], axis=1)
    outs = []
    outs.append(np.stack([np.asarray(r["y_p"]) for r in R], axis=0).reshape(8, 2048, 1024))
    outs.append(cat("y_s", 0).reshape(128, 4, 1024))
    for g, keep in enumerate((128, 512, 2048)):
        outs.append(stk("k%d_p" % g).reshape(2, 8, keep, 4, 64))
        outs.append(stk("v%d_p" % g).reshape(2, 8, keep, 4, 64))
    outs.append(stk("conv_p").reshape(2, 8, 30, 512))
    outs.append(stk("ssm_re_p").reshape(2, 8, 32, 64))
    outs.append(stk("ssm_im_p").reshape(2, 8, 32, 64))
    for g in range(3):
        outs.append(cat("k%d_s" % g, 1).reshape(2, 128, 4, 4, 64))
        outs.append(cat("v%d_s" % g, 1).reshape(2, 128, 4, 4, 64))
    outs.append(cat("conv_s", 1).reshape(2, 128, 30, 512))
    outs.append(cat("ssm_re_s", 1).reshape(2, 128, 32, 64))
    outs.append(cat("ssm_im_s", 1).reshape(2, 128, 32, 64))
    return tuple(np.ascontiguousarray(o, dtype=np.float32) for o in outs)
```

```python
import contextlib
import math
import numpy as np
import concourse.bass as bass
import concourse.mybir as mybir
from concourse.bass_utils import run_bass_kernel_spmd

F32 = mybir.dt.float32
BF16 = mybir.dt.bfloat16
AF = mybir.ActivationFunctionType
ALU = mybir.AluOpType
AX = mybir.AxisListType

D = 1024
NT = 2112
TPR = 2048
NSEQ = 16
DEPTH = 2
D_FF = 2816
N_IN = 6912
O1, O2, O3, O4, O5 = 1024, 1536, 2304, 3072, 3840
PATTERNS = ((128, 1), (512, 4), (2048, 16))
RMS_EPS = 1e-6
LN_EPS = 1e-5
NEG = -1e30
TILES = [(0, 512), (512, 512), (1024, 512), (1536, 512), (2048, 64)]

ENGS = ('pe', 'act', 'dve', 'pool', 'sp')
EPOCH = 20000
NDSEM = 12
SAME_ENGINE_SYNC = {'pe': False, 'act': True, 'dve': True, 'pool': True, 'sp': False}


class _Op:
    __slots__ = ('fn', 'waits', 'inc')

    def __init__(self, fn):
        self.fn = fn
        self.waits = []
        self.inc = None


class Prog:
    def __init__(self, nc):
        self.nc = nc
        self.stack = contextlib.ExitStack()
        self.ops = {e: [] for e in ENGS}
        self.nops = {e: 0 for e in ENGS}
        self.last_w = {}
        self.readers = {}
        self.known = {e: {} for e in ENGS}
        self.sems = {}
        self.dma_rr = {e: 0 for e in ENGS}
        self.dma_cnt = {}
        self.out_tokens = []
        self.phase = 'setup'
        self.pe_phase = []

    def sbuf(self, name, shape, dtype):
        return self.stack.enter_context(self.nc.sbuf_tensor(name, shape, dtype))

    def psum(self, name, shape, dtype):
        return self.stack.enter_context(self.nc.psum_tensor(name, shape, dtype))

    def _sem(self, key):
        if key not in self.sems:
            self.sems[key] = self.stack.enter_context(
                self.nc.semaphore("s_" + "_".join(str(k) for k in key)))
        return self.sems[key]

    def _token_of(self, eng, idx):
        return (('p', eng, idx // EPOCH), idx % EPOCH + 1)

    def _add_dep(self, op, eng, tok, src_eng):
        if tok is None:
            return
        semkey, val = tok
        if src_eng == eng and semkey[0] == 'p' and not SAME_ENGINE_SYNC[eng]:
            return
        if self.known[eng].get(semkey, 0) >= val:
            return
        self.known[eng][semkey] = val
        for i, (k, v) in enumerate(op.waits):
            if k == semkey:
                op.waits[i] = (k, max(v, val))
                return
        op.waits.append((semkey, val))

    @staticmethod
    def _flat(keys):
        out = []
        for k in keys:
            if isinstance(k, (list, tuple)):
                out.extend(Prog._flat(k))
            else:
                out.append(k)
        return out

    def _deps(self, op, eng, reads, writes):
        reads, writes = self._flat(reads), self._flat(writes)
        for r in reads:
            lw = self.last_w.get(r)
            if lw is not None:
                self._add_dep(op, eng, lw[1], lw[0])
        for w in writes:
            lw = self.last_w.get(w)
            if lw is not None:
                self._add_dep(op, eng, lw[1], lw[0])
            for (se, tok) in self.readers.get(w, ()):
                self._add_dep(op, eng, tok, se)

    def _commit(self, eng, tok, reads, writes):
        reads, writes = self._flat(reads), self._flat(writes)
        for r in reads:
            self.readers.setdefault(r, []).append((eng, tok))
        for w in writes:
            self.last_w[w] = (eng, tok)
            self.readers[w] = []

    def op(self, eng, fn, reads=(), writes=(), sync_prev=False):
        o = _Op(fn)
        self._deps(o, eng, reads, writes)
        idx = self.nops[eng]
        if sync_prev and idx > 0:
            self._add_dep(o, eng, self._token_of(eng, idx - 1), None)
        self.nops[eng] += 1
        tok = self._token_of(eng, idx)
        o.inc = (tok[0], 1)
        if eng == 'pe':
            self.pe_phase.append(self.phase)
        self.ops[eng].append(o)
        self._commit(eng, tok, reads, writes)
        return o

    def I(self, eng, method, *args, reads=(), writes=(), sync_prev=False, **kw):
        return self.op(eng, lambda e: getattr(e, method)(*args, **kw), reads=reads, writes=writes, sync_prev=sync_prev)

    def dma(self, q, out_ap, in_ap, reads=(), writes=(), out=False, **kw):
        o = _Op(lambda e: e.dma_start(out=out_ap, in_=in_ap, **kw))
        self._deps(o, q, reads, writes)
        k = self.dma_rr[q]
        self.dma_rr[q] = (k + 1) % NDSEM
        gen = 0
        while self.dma_cnt.get(('d', q, k, gen), 0) >= EPOCH // 16:
            gen += 1
        semkey = ('d', q, k, gen)
        cnt = self.dma_cnt.get(semkey, 0)
        if cnt > 0:
            self._add_dep(o, q, (semkey, 16 * cnt), None)
        self.dma_cnt[semkey] = cnt + 1
        tok = (semkey, 16 * (cnt + 1))
        o.inc = (semkey, 16)
        self.ops[q].append(o)
        self._commit(q, tok, reads, writes)
        if out:
            self.out_tokens.append(tok)
        return o

    def barrier(self):
        toks = []
        for e in ENGS:
            if self.nops[e] > 0:
                toks.append((e, self._token_of(e, self.nops[e] - 1)))
        for semkey, cnt in self.dma_cnt.items():
            toks.append((None, (semkey, 16 * cnt)))
        for e in ENGS:
            o = _Op(None)
            for (se, tok) in toks:
                if se == e:
                    continue
                self._add_dep(o, e, tok, se)
            if o.waits:
                self.ops[e].append(o)
        self.last_w = {}
        self.readers = {}

    def finish(self):
        nc = self.nc
        fin = _Op(None)
        best = {}
        for (k, v) in self.out_tokens:
            best[k] = max(best.get(k, 0), v)
        for k, v in best.items():
            fin.waits.append((k, v))
        self.ops['sp'].append(fin)
        for e in ENGS:
            for o in self.ops[e]:
                for (k, v) in o.waits:
                    self._sem(k)
                if o.inc is not None:
                    self._sem(o.inc[0])
        prog = self

        def emit(eng_name, e):
            for o in prog.ops[eng_name]:
                for (k, v) in o.waits:
                    e.wait_ge(prog.sems[k], v)
                if o.fn is None:
                    continue
                ins = o.fn(e)
                if o.inc is not None:
                    ins.then_inc(prog.sems[o.inc[0]], o.inc[1])

        with nc.Block() as block:
            @block.tensor
            def _(e):
                emit('pe', e)

            @block.scalar
            def _(e):
                emit('act', e)

            @block.vector
            def _(e):
                emit('dve', e)

            @block.gpsimd
            def _(e):
                emit('pool', e)

            @block.sync
            def _(e):
                emit('sp', e)
        self.stack.close()


class _StopMix(Exception):
    pass


class T:
    def __init__(self, h, F):
        self.h = h
        self.F = F

    def v(self, off, dims, p0=0, pn=128):
        return bass.AP(self.h, p0 * self.F + off, [[self.F, pn]] + [list(d) for d in dims])


def vps(t, off, dims, p0, pstep, pn):
    return bass.AP(t.h, p0 * t.F + off, [[pstep * t.F, pn]] + [list(d) for d in dims])


def dap(h, off, dims):
    return bass.AP(h, off, [list(d) for d in dims])


C_IDENT = 0
C_DIST = 128
C_MASK = 384
C_MG2 = 640
C_DELTA = 642
C_DS0 = 658
C_MS0 = 662
C_DS1 = 666
C_DN0 = 667
C_MN0 = 731
C_MN1 = 795
C_QM = 859
C_TOT = 863


def make_consts():
    c = np.zeros((128, C_TOT), np.float32)
    c[:, C_IDENT:C_IDENT + 128] = np.eye(128, dtype=np.float32)
    tk = np.arange(128)[:, None]
    tq = np.arange(128)[None, :]
    d0 = tq - tk
    d1 = 128 + tq - tk
    c[:, C_DIST:C_DIST + 128] = np.where(d0 >= 0, d0, 0)
    c[:, C_MASK:C_MASK + 128] = np.where(d0 >= 0, 0.0, NEG)
    c[:, C_DIST + 128:C_DIST + 256] = np.where(d1 <= 128, d1, 0)
    c[:, C_MASK + 128:C_MASK + 256] = np.where(d1 <= 128, 0.0, NEG)
    p = np.arange(128)
    for g2 in range(2):
        c[:, C_MG2 + g2] = ((p % 32) // 16 == g2)
    for cc in range(16):
        c[:, C_DELTA + cc] = (p % 16 == cc)
    for j in range(4):
        kk = 128 + j - p
        c[:, C_DS0 + j] = np.where(kk <= 128, kk, 0)
        c[:, C_MS0 + j] = np.where(kk <= 128, 0.0, NEG)
    c[:, C_DS1] = 128 - p
    ki = np.arange(64)[:, None]
    qi = np.arange(64)[None, :]
    same = (ki // 4) == (qi // 4)
    dn = (qi % 4) - (ki % 4)
    v0 = same & (dn >= 0)
    c[:64, C_DN0:C_DN0 + 64] = np.where(v0, dn, 0)
    c[:64, C_MN0:C_MN0 + 64] = np.where(v0, 0.0, NEG)
    c[:64, C_MN1:C_MN1 + 64] = np.where(ki == qi, 0.0, NEG)
    for q in range(4):
        c[:, C_QM + q] = (p // 32 == q)
    return c


def make_sel():
    c = np.zeros((128, 2048), np.float32)
    for tpar in range(2):
        for gl in range(8):
            blk = np.zeros((128, 128), np.float32)
            for pp in range(128):
                if (pp % 32) // 16 == tpar:
                    blk[pp, gl * 16 + pp % 16] = 1.0
            o = (tpar * 8 + gl) * 128
            c[:, o:o + 128] = blk
    return c


def alibi_slope(h):
    return 2.0 ** (-8.0 * (h + 1) / 12.0)


IN_SPECS = [
    ("x_p", [2048, 1024]), ("x_s", [64, 1024]), ("c_all", [17, 1024]),
    ("ck0", [2, 16, 128, 256]), ("cv0", [2, 16, 128, 256]),
    ("ck1", [2, 16, 512, 256]), ("cv1", [2, 16, 512, 256]),
    ("ck2", [2, 16, 512, 256]), ("cv2", [2, 16, 512, 256]),
    ("st_conv", [2, 16, 30, 512]), ("st_re", [2, 16, 2048]), ("st_im", [2, 16, 2048]),
    ("w_mod", [2, 1024, 6144]), ("b_mod", [2, 6144]),
    ("g_pre_mix", [2, 1024]), ("g_post_mix", [2, 1024]), ("g_pre_ffn", [2, 1024]), ("g_post_ffn", [2, 1024]),
    ("w_in", [2, 1024, 6912]), ("conv_w", [2, 31, 512]), ("conv_b", [2, 512]),
    ("conv_ln_g", [2, 512]), ("conv_ln_b", [2, 512]), ("w_conv_out", [2, 512, 1024]),
    ("ssm_a_re", [2, 2048]), ("ssm_a_im", [2, 2048]), ("ssm_log_dt", [2, 32]),
    ("ssm_b_re", [2, 32 * 64 * 16]), ("ssm_b_im", [2, 32 * 64 * 16]),
    ("ssm_c_re", [2, 32 * 16 * 64]), ("ssm_c_im", [2, 32 * 16 * 64]), ("ssm_d", [2, 512]),
    ("w_ssm_glu", [2, 512, 2048]), ("w_att", [2, 256, 1024]), ("w_out", [2, 1024, 1024]),
    ("w_ffn_in", [2, 1024, 5632]), ("w_ffn_out", [2, 2816, 1024]),
    ("consts", [128, C_TOT]), ("selc", [128, 2048]),
]
OUT_SPECS = [
    ("y_p", [2048, 1024]), ("y_s", [64, 1024]),
    ("k0_p", [2, 128, 256]), ("v0_p", [2, 128, 256]), ("k1_p", [2, 512, 256]), ("v1_p", [2, 512, 256]),
    ("k2_p", [2, 2048, 256]), ("v2_p", [2, 2048, 256]),
    ("conv_p", [2, 30, 512]), ("ssm_re_p", [2, 2048]), ("ssm_im_p", [2, 2048]),
    ("k0_s", [2, 64, 256]), ("v0_s", [2, 64, 256]), ("k1_s", [2, 64, 256]), ("v1_s", [2, 64, 256]),
    ("k2_s", [2, 64, 256]), ("v2_s", [2, 64, 256]),
    ("conv_s", [2, 16, 30, 512]), ("ssm_re_s", [2, 16, 2048]), ("ssm_im_s", [2, 16, 2048]),
]


def build_program(stop_after=None):
    import os
    KSTOP = os.environ.get('KSTOP', '')
    nc = bass.Bass("TRN2", target_bir_lowering=False)
    P = Prog(nc)
    dr = {}
    for name, shp in IN_SPECS:
        dr[name] = nc.dram_tensor(name, shp, F32, kind="ExternalInput")
    for name, shp in OUT_SPECS:
        dr[name] = nc.dram_tensor(name, shp, F32, kind="ExternalOutput")

    def sb(name, F, dt):
        return T(P.sbuf(name, [128, F], dt), F)

    X = sb("X", 8 * NT, F32)
    H = sb("H", 8 * NT, BF16)
    MG = sb("MG", 8 * NT, BF16)
    NWB = 2
    WB = [sb("WB%d" % i, 4096, BF16) for i in range(NWB)]
    CST = sb("CST", C_TOT, F32)
    IDB = sb("IDB", 128, BF16)
    ONB = sb("ONB", 128, BF16)
    ONF = sb("ONF", 128, F32)
    SELB = sb("SELB", 2048, BF16)
    CT = sb("CT", 8 * 17, BF16)
    VEC = sb("VEC", 96, F32)
    MOD = sb("MOD", 48 * 17, F32)
    AMt = sb("AMt", 8 * 17, F32)
    GMt = sb("GMt", 8 * 17, F32)
    AFt = sb("AFt", 8 * 17, F32)
    GFt = sb("GFt", 8 * 17, F32)
    RS = sb("RS", 512, F32)
    TMPA = sb("TMPA", 512, F32)
    SCRB = 41 * 1024
    SCR_h = P.sbuf("SCR", [128, SCRB // 2], BF16)
    SCR16 = T(SCR_h, SCRB // 2)
    SCR32 = T(SCR_h.bitcast(F32), SCRB // 4)
    PS = []
    PSB = []
    for i in range(8):
        h = P.psum("PS%d" % i, [128, 512], F32)
        PS.append(T(h, 512))
        PSB.append(T(h.bitcast(BF16), 1024))

    IDF = lambda pn=128: CST.v(C_IDENT, [[1, pn]], pn=pn)

    st = {'wrr': 0, 'prr': 0, 'tmp': 0}
    DENSE = (0, 1, 2, 3, 6, 7)

    def tmpbuf(w):
        i = st['tmp'] % 2
        st['tmp'] += 1
        return (TMPA, RS)[i].v(0, [[1, w]]), ('TMPA', 'RS')[i]

    def wload(hname, layer_off, row_stride, col0, ncols, kc, row0=0, pieces=None):
        i = st['wrr']
        st['wrr'] = (i + 1) % NWB
        wb = WB[i]
        if pieces is None and kc >= 16:
            k1 = kc // 2
            for pi_, (ka, kn) in enumerate(((0, k1), (k1, kc - k1))):
                src = dap(dr[hname], layer_off + (row0 + ka * 128) * row_stride + col0, [[row_stride, 128], [128 * row_stride, kn], [1, ncols]])
                wkeys = ['WB%d_0' % i] if pi_ == 0 else ['WB%d_1' % i, 'WB%d_2' % i]
                P.dma('pool', wb.v(ka * ncols, [[ncols, kn], [1, ncols]]), src, reads=[], writes=wkeys)
            return wb, ['WB%d_%d' % (i, x) for x in range(3)]
        if pieces is None:
            pieces = [(col0, ncols)]
        tot = sum(n for _, n in pieces)
        o = 0
        keys = []
        for pi_, (c0_, n_) in enumerate(pieces):
            src = dap(dr[hname], layer_off + row0 * row_stride + c0_, [[row_stride, 128], [128 * row_stride, kc], [1, n_]])
            wkeys = ['WB%d_%d' % (i, pi_)] if pi_ < len(pieces) - 1 else ['WB%d_%d' % (i, x) for x in range(pi_, 3)]
            P.dma('pool', wb.v(o, [[tot, kc], [1, n_]]), src, reads=[], writes=wkeys)
            o += n_
        return wb, ['WB%d_%d' % (i, x) for x in range(3)]

    def wload_multi(items):
        i = st['wrr']
        st['wrr'] = (i + 1) % NWB
        wb = WB[i]
        o = 0
        offs = []
        keys = []
        for pi_, (hname, layer_off, row_stride, c0_, n_, kc) in enumerate(items):
            src = dap(dr[hname], layer_off + c0_, [[row_stride, 128], [128 * row_stride, kc], [1, n_]])
            wkeys = ['WB%d_%d' % (i, pi_)] if pi_ < len(items) - 1 else ['WB%d_%d' % (i, x) for x in range(pi_, 3)]
            P.dma('pool', wb.v(o, [[n_, kc], [1, n_]]), src, reads=[], writes=wkeys)
            offs.append(o)
            o += kc * n_
        assert o <= 4096 and len(items) <= 3
        return wb, ['WB%d_%d' % (i, x) for x in range(3)], offs

    def pbank(pool=(0, 1, 2, 3)):
        i = st['prr']
        st['prr'] = i + 1
        b = pool[i % len(pool)]
        return b, 'PS%d' % b

    def evac(i, out_ap, in_ap, reads, writes):
        if i % 2 == 0:
            P.I('dve', 'tensor_copy', out_ap, in_ap, reads=reads, writes=writes)
        else:
            P.I('act', 'copy', out_ap, in_ap, reads=reads, writes=writes)

    def transpose_rows(src_ap_fn, nrows, dst_fn, rkey, ncol_chunks, bankpool=(4, 5)):
        for c0 in range(0, ncol_chunks, 4):
            ncn = min(4, ncol_chunks - c0)
            b, bk = pbank(bankpool)
            for c in range(ncn):
                P.I('pe', 'transpose', PS[b].v(c * 128, [[1, nrows]]), src_ap_fn(c0 + c), IDF(nrows),
                    reads=[rkey, 'CST'], writes=[bk])
            dst_fn(c0, ncn, b, bk)

    P.dma('sp', CST.v(0, [[1, C_TOT]]), dap(dr['consts'], 0, [[C_TOT, 128], [1, C_TOT]]), writes=['CST'])
    P.dma('pool', SELB.v(0, [[1, 2048]]), dap(dr['selc'], 0, [[2048, 128], [1, 2048]]), writes=['SELB'])
    P.I('dve', 'tensor_copy', IDB.v(0, [[1, 128]]), CST.v(C_IDENT, [[1, 128]]), reads=['CST'], writes=['IDB'])
    P.I('pool', 'memset', ONB.v(0, [[1, 128]]), 1.0, writes=['ONB'])
    P.I('pool', 'memset', ONF.v(0, [[1, 128]]), 1.0, writes=['ONF'])

    for ti in range(17):
        rows = 128 if ti < 16 else 64
        boff = (ti % 2) * 1024
        key = 'XIN%d' % (ti % 2)
        if ti < 16:
            src = dap(dr['x_p'], ti * 128 * 1024, [[1024, 128], [1, 1024]])
        else:
            src = dap(dr['x_s'], 0, [[1024, 64], [1, 1024]])
        P.dma('sp', SCR32.v(boff, [[1, 1024]], pn=rows), src, writes=[key])
        t0 = ti * 128

        def dst(c0, ncn, b, bk, t0=t0, rows=rows):
            evac(c0 // 4, X.v(c0 * NT + t0, [[NT, ncn], [1, rows]]), PS[b].v(0, [[128, ncn], [1, rows]]), [bk], ['X'])
        transpose_rows(lambda c, boff=boff, rows=rows: SCR32.v(boff + c * 128, [[1, 128]], pn=rows), rows, dst, key, 8)

    P.dma('sp', SCR32.v(2048, [[1, 1024]], pn=17), dap(dr['c_all'], 0, [[1024, 17], [1, 1024]]), writes=['CIN'])

    def dst_c(c0, ncn, b, bk):
        P.I('act', 'activation', CT.v(c0 * 17, [[17, ncn], [1, 17]]), PS[b].v(0, [[128, ncn], [1, 17]]), AF.Silu,
            reads=[bk], writes=['CT'])
    transpose_rows(lambda c: SCR32.v(2048 + c * 128, [[1, 128]], pn=17), 17, dst_c, 'CIN', 8)
    P.barrier()

    def rstd_from(src_aps, width, scale, eps, key_r):
        nk = len(src_aps)
        for k in range(nk):
            if key_r == 'X' and k % 4 == 3:
                P.I('dve', 'tensor_tensor', SCR16.v(k * 512, [[1, width]]), src_aps[k], src_aps[k], op=ALU.mult, reads=[key_r], writes=['SQ'])
            else:
                P.I('act', 'activation', SCR16.v(k * 512, [[1, width]]), src_aps[k], AF.Square, reads=[key_r], writes=['SQ'])
        b, bk = pbank((4, 5))
        for k in range(nk):
            P.I('pe', 'matmul', PS[b].v(0, [[1, width]]), ONB.v(0, [[1, 128]]), SCR16.v(k * 512, [[1, width]]),
                start=(k == 0), stop=(k == nk - 1), reads=['SQ', 'ONB'], writes=[bk])
        P.I('act', 'activation', RS.v(0, [[1, width]]), PS[b].v(0, [[1, width]]), AF.Sqrt, bias=eps, scale=scale,
            reads=[bk], writes=['RS'])
        P.I('dve', 'reciprocal', RS.v(0, [[1, width]]), RS.v(0, [[1, width]]), reads=['RS'], writes=['RS'])

    def pre_norm(At, Bt):
        for (t0, w) in TILES:
            rstd_from([X.v(k * NT + t0, [[1, w]]) for k in range(8)], w, 1.0 / D, RMS_EPS, 'X')
            for k in range(8):
                if t0 < TPR:
                    tp_ = TMPA.v(0, [[1, w]]) if k % 2 == 0 else SCR32.v(2048, [[1, w]])
                    tpk = 'TMPA' if k % 2 == 0 else 'TMPB'
                    P.I('dve', 'scalar_tensor_tensor', tp_, X.v(k * NT + t0, [[1, w]]), At.v(k * 17, [[1, 1]]),
                        RS.v(0, [[1, w]]), op0=ALU.mult, op1=ALU.mult, reads=['X', 'RS', 'MODS'], writes=[tpk])
                    P.I('act', 'activation', H.v(k * NT + t0, [[1, w]]), tp_, AF.Identity,
                        bias=Bt.v(k * 17, [[1, 1]]), scale=1.0, reads=[tpk, 'MODS', 'MOD'], writes=['H'])
                else:
                    P.I('dve', 'tensor_tensor', TMPA.v(0, [[4, 16], [1, 4]]), X.v(k * NT + TPR, [[4, 16], [1, 4]]),
                        At.v(k * 17 + 1, [[1, 16], [0, 4]]), op=ALU.mult, reads=['X', 'MODS'], writes=['TMPA'])
                    P.I('dve', 'tensor_tensor', TMPA.v(0, [[1, 64]]), TMPA.v(0, [[1, 64]]), RS.v(0, [[1, 64]]), op=ALU.mult,
                        reads=['TMPA', 'RS'], writes=['TMPA'])
                    P.I('dve', 'tensor_tensor', H.v(k * NT + TPR, [[4, 16], [1, 4]]), TMPA.v(0, [[4, 16], [1, 4]]),
                        Bt.v(k * 17 + 1, [[1, 16], [0, 4]]), op=ALU.add, reads=['TMPA', 'MODS', 'MOD'], writes=['H'])

    def post_norm_add(MO, t0, w, Gt, tmp2off):
        rstd_from([MO(k, w) for k in range(8)], w, 1.0 / D, RMS_EPS, 'MO')
        for k in range(8):
            if k % 2 == 0:
                tk_ = 'TMPA'
                tv = lambda dims: TMPA.v(0, dims)
            else:
                tk_ = 'TMPB'
                tv = lambda dims: SCR32.v(tmp2off, dims)
            if t0 < TPR:
                P.I('dve', 'scalar_tensor_tensor', tv([[1, w]]), MO(k, w), Gt.v(k * 17, [[1, 1]]), RS.v(0, [[1, w]]),
                    op0=ALU.mult, op1=ALU.mult, reads=['MO', 'RS', 'MODS'], writes=[tk_])
            else:
                P.I('dve', 'tensor_tensor', tv([[4, 16], [1, 4]]), MO(k, None), Gt.v(k * 17 + 1, [[1, 16], [0, 4]]),
                    op=ALU.mult, reads=['MO', 'MODS'], writes=[tk_])
                P.I('dve', 'tensor_tensor', tv([[1, 64]]), tv([[1, 64]]), RS.v(0, [[1, 64]]), op=ALU.mult,
                    reads=[tk_, 'RS'], writes=[tk_])
            P.I('dve', 'tensor_tensor', X.v(k * NT + t0, [[1, w]]), X.v(k * NT + t0, [[1, w]]), tv([[1, w]]), op=ALU.add,
                reads=[tk_, 'XA%d' % k], writes=['XA%d' % k])

    class _Off:
        def __init__(self, t, off):
            self.t, self.off = t, off

        def v(self, off, dims, p0=0, pn=128):
            return self.t.v(self.off + off, dims, p0, pn)

    for l in range(DEPTH):
        P.phase = 'mod'
        rows = [("g_pre_mix", 8), ("g_post_mix", 8), ("g_pre_ffn", 8), ("g_post_ffn", 8),
                ("conv_b", 4), ("conv_ln_g", 4), ("conv_ln_b", 4), ("b_mod", 48)]
        r0 = 0
        vo = {}
        for name, n in rows:
            P.dma('sp', SCR32.v(0, [[1, 128]], p0=r0, pn=n), dap(dr[name], l * n * 128, [[128, n], [1, 128]]), writes=['STG'])
            vo[name] = r0
            r0 += n
        b, bk = pbank((4, 5))
        P.I('pe', 'transpose', PS[b].v(0, [[1, 92]]), SCR32.v(0, [[1, 128]], pn=92), IDF(92), reads=['STG', 'CST'], writes=[bk])
        P.I('dve', 'tensor_copy', VEC.v(0, [[1, 92]]), PS[b].v(0, [[1, 92]]), reads=[bk], writes=['VEC'])

        for blk in range(12):
            wb, wk = wload('w_mod', l * 1024 * 6144, 6144, blk * 512, 512, 8)
            b, bk = pbank((4, 5))
            for m in range(4):
                for k in range(8):
                    P.I('pe', 'matmul', PS[b].v(m * 17, [[1, 17]]), wb.v(k * 512 + m * 128, [[1, 128]]), CT.v(k * 17, [[1, 17]]),
                        start=(k == 0), stop=(k == 7), reads=[wk, 'CT'], writes=[bk])
            for m in range(4):
                j = blk * 4 + m
                P.I('dve', 'tensor_scalar', MOD.v(j * 17, [[1, 17]]), PS[b].v(m * 17, [[1, 17]]), VEC.v(vo['b_mod'] + j, [[1, 1]]),
                    None, op0=ALU.add, reads=[bk, 'VEC'], writes=['MOD'])

        def mk(dst, modj, vname, plus1):
            if plus1:
                P.I('dve', 'tensor_scalar', dst.v(0, [[1, 136]]), MOD.v(modj * 17, [[1, 136]]), 1.0, None, op0=ALU.add,
                    reads=['MOD'], writes=['MODS'])
                P.I('dve', 'tensor_tensor', dst.v(0, [[17, 8], [1, 17]]), dst.v(0, [[17, 8], [1, 17]]),
                    VEC.v(vo[vname], [[1, 8], [0, 17]]), op=ALU.mult, reads=['MODS', 'VEC'], writes=['MODS'])
            else:
                P.I('dve', 'tensor_tensor', dst.v(0, [[17, 8], [1, 17]]), MOD.v(modj * 17, [[17, 8], [1, 17]]),
                    VEC.v(vo[vname], [[1, 8], [0, 17]]), op=ALU.mult, reads=['MOD', 'VEC'], writes=['MODS'])
        mk(AMt, 8, 'g_pre_mix', True)
        mk(GMt, 16, 'g_post_mix', False)
        mk(AFt, 32, 'g_pre_ffn', True)
        mk(GFt, 40, 'g_post_ffn', False)
        BMv = _Off(MOD, 0)
        BFv = _Off(MOD, 24 * 17)
        P.barrier()

        P.phase = 'prenorm'
        pre_norm(AMt, BMv)
        P.barrier()

        P.phase = 'kv'
        kvi = 0
        for g, (win, dil) in enumerate(PATTERNS):
            keep = min(win, TPR)
            for cbase, oname_p, oname_s in ((O3, "k%d_p" % g, "k%d_s" % g), (O4, "v%d_p" % g, "v%d_s" % g)):
                col0 = cbase + g * 256
                wb, wk = wload('w_in', l * 1024 * N_IN, N_IN, col0, 256, 8)
                tiles = [(t0, 128) for t0 in range(TPR - keep, TPR, 128)] + [(TPR, 64)]
                for (t0, rows_) in tiles:
                    b, bk = pbank(DENSE)
                    for k in range(8):
                        P.I('pe', 'matmul', PS[b].v(0, [[1, 256]], pn=rows_), H.v(k * NT + t0, [[1, rows_]]), wb.v(k * 256, [[1, 256]]),
                            start=(k == 0), stop=(k == 7), reads=[wk, 'H'], writes=[bk])
                    so = 4096 + (kvi % 2) * 256
                    skey = 'KVST%d' % (kvi % 2)
                    kvi += 1
                    evac(kvi, SCR32.v(so, [[1, 256]], pn=rows_), PS[b].v(0, [[1, 256]], pn=rows_), [bk], [skey])
                    if t0 < TPR:
                        dst_ = dap(dr[oname_p], l * keep * 256 + (t0 - (TPR - keep)) * 256, [[256, 128], [1, 256]])
                    else:
                        dst_ = dap(dr[oname_s], l * 64 * 256, [[256, 64], [1, 256]])
                    P.dma('sp', dst_, SCR32.v(so, [[1, 256]], pn=rows_), reads=[skey], writes=['OUT' + oname_p], out=True)
        P.barrier()


        def mixer():
            P.phase = 'conv'
            skipmix = False
            UBo, UBW = 4096, 2142
            USo = 12664
            DGo = 14840
            MEANo = 9404
            UFo = 9916
            CWo = 10292
            P.dma('sp', SCR32.v(0, [[1, 512]], pn=31), dap(dr['conv_w'], l * 31 * 512, [[512, 31], [1, 512]]), writes=['STG'])
            b, bk = pbank((4, 5))
            for c in range(4):
                P.I('pe', 'transpose', PS[b].v(c * 32, [[1, 31]]), SCR32.v(c * 128, [[1, 128]], pn=31), IDF(31),
                    reads=['STG', 'CST'], writes=[bk])
            P.I('dve', 'tensor_copy', SCR32.v(CWo, [[31, 4], [1, 31]]), PS[b].v(0, [[32, 4], [1, 31]]), reads=[bk], writes=['CW'])
            if KSTOP == 'A1':
                raise _StopMix()
            P.dma('sp', dap(dr['conv_s'], l * 16 * 15360, [[15360, 16], [1, 13312]]),
                  dap(dr['st_conv'], l * 16 * 15360 + 4 * 512, [[15360, 16], [1, 13312]]), writes=['OUTconvs'], out=True)
            for rt in range(4):
                P.dma('sp', SCR32.v(0, [[1, 512]], pn=120), dap(dr['st_conv'], l * 16 * 15360 + rt * 4 * 15360, [[512, 120], [1, 512]]),
                      writes=['STG'])
                b, bk = pbank((4, 5))
                for c in range(4):
                    P.I('pe', 'transpose', PS[b].v(c * 128, [[1, 120]]), SCR32.v(c * 128, [[1, 128]], pn=120), IDF(120),
                        reads=['STG', 'CST'], writes=[bk])
                P.I('act', 'copy', SCR16.v(USo + rt * 4 * 34, [[544, 4], [34, 4], [1, 30]]), PS[b].v(0, [[128, 4], [30, 4], [1, 30]]),
                    reads=[bk], writes=['US'])
            if KSTOP == 'A2':
                raise _StopMix()
            P.I('pool', 'memset', SCR16.v(UBo, [[UBW, 4], [1, 30]]), 0.0, writes=['UBu0', 'UBu1', 'UBu2', 'UBu3'])
            for c in range(4):
                wb, wk = wload('w_in', l * 1024 * N_IN, N_IN, 0, 0, 8, pieces=[(c * 128, 128), (512 + c * 128, 128)])
                for (t0, w) in TILES:
                    b1, k1 = pbank(DENSE)
                    b2, k2 = pbank(DENSE)
                    for k in range(8):
                        P.I('pe', 'matmul', PS[b1].v(0, [[1, w]]), wb.v(k * 256, [[1, 128]]), H.v(k * NT + t0, [[1, w]]),
                            start=(k == 0), stop=(k == 7), reads=[wk, 'H'], writes=[k1])
                    for k in range(8):
                        P.I('pe', 'matmul', PS[b2].v(0, [[1, w]]), wb.v(k * 256 + 128, [[1, 128]]), H.v(k * NT + t0, [[1, w]]),
                            start=(k == 0), stop=(k == 7), reads=[wk, 'H'], writes=[k2])
                    TG_, tgk = ((TMPA, 'TMPA'), (RS, 'RS'))[st['tmp'] % 2]
                    st['tmp'] += 1
                    P.I('act', 'activation', TG_.v(0, [[1, w]]), PS[b2].v(0, [[1, w]]), AF.Sigmoid, reads=[k2], writes=[tgk])
                    if t0 < TPR:
                        P.I('dve', 'tensor_tensor', SCR16.v(UBo + c * UBW + 30 + t0, [[1, w]]), TG_.v(0, [[1, w]]), PS[b1].v(0, [[1, w]]),
                            op=ALU.mult, reads=[tgk, k1], writes=['UBu%d' % c])
                        if t0 == 1536:
                            P.I('dve', 'tensor_tensor', SCR32.v(UFo + c * 94, [[1, 30]]), TG_.v(482, [[1, 30]]), PS[b1].v(482, [[1, 30]]),
                                op=ALU.mult, reads=[tgk, k1], writes=['UF'])
                    else:
                        P.I('dve', 'tensor_tensor', SCR16.v(USo + c * 544 + 30, [[34, 16], [1, 4]]), TG_.v(0, [[4, 16], [1, 4]]),
                            PS[b1].v(0, [[4, 16], [1, 4]]), op=ALU.mult, reads=[tgk, k1], writes=['US'])
                        P.I('dve', 'tensor_tensor', SCR32.v(UFo + c * 94 + 30, [[1, 64]]), TG_.v(0, [[1, 64]]), PS[b1].v(0, [[1, 64]]),
                            op=ALU.mult, reads=[tgk, k1], writes=['UF'])
            if KSTOP == 'A3':
                raise _StopMix()
            b, bk = pbank((4, 5))
            for c in range(4):
                P.I('pe', 'transpose', PS[b].v(c * 128, [[1, 128]], pn=30), SCR32.v(UFo + c * 94, [[1, 30]]), IDF(128),
                    reads=['UF', 'CST'], writes=[bk])
            P.I('dve', 'tensor_copy', SCR32.v(0, [[1, 512]], pn=30), PS[b].v(0, [[1, 512]], pn=30), reads=[bk], writes=['STG'])
            P.dma('sp', dap(dr['conv_p'], l * 30 * 512, [[512, 30], [1, 512]]), SCR32.v(0, [[1, 512]], pn=30), reads=['STG'],
                  writes=['OUTconvp'], out=True)
            b, bk = pbank((4, 5))
            for c in range(4):
                P.I('pe', 'transpose', PS[b].v(c * 128, [[1, 128]], pn=64), SCR32.v(UFo + c * 94 + 30, [[1, 64]]), IDF(128),
                    reads=['UF', 'CST'], writes=[bk])
            P.I('act', 'copy', SCR32.v(512, [[1, 512]], pn=64), PS[b].v(0, [[1, 512]], pn=64), reads=[bk], writes=['STG2'])
            for j in range(4):
                P.dma('sp', dap(dr['conv_s'], l * 16 * 15360 + (26 + j) * 512, [[15360, 16], [1, 512]]),
                      vps(SCR32, 512, [[1, 512]], j, 4, 16), reads=['STG2'], writes=['OUTconvs'], out=True)
            if KSTOP == 'A4':
                raise _StopMix()
            for c in range(4):
                for k in range(31):
                    P.I('dve', 'tensor_scalar', SCR16.v(DGo + k * 128, [[1, 128]]), IDB.v(0, [[1, 128]]),
                        SCR32.v(CWo + c * 31 + k, [[1, 1]]), None, op0=ALU.mult, reads=['IDB', 'CW'], writes=['DG%d' % k])
                for (t0, w) in reversed(TILES):
                    b, bk = pbank(DENSE)
                    for k in range(31):
                        if t0 < TPR:
                            rhs = SCR16.v(UBo + c * UBW + t0 + k, [[1, w]])
                            out_ = PS[b].v(0, [[1, w]])
                        else:
                            rhs = SCR16.v(USo + c * 544 + k, [[34, 16], [1, 4]])
                            out_ = PS[b].v(0, [[4, 16], [1, 4]])
                        P.I('pe', 'matmul', out_, SCR16.v(DGo + k * 128, [[1, 128]]), rhs, start=(k == 0), stop=(k == 30),
                            reads=['DG%d' % k, 'UBu%d' % c, 'US'], writes=[bk])
                    P.I('act', 'activation', SCR16.v(UBo + c * UBW + 30 + t0, [[1, w]]), PS[b].v(0, [[1, w]]), AF.Identity,
                        bias=VEC.v(vo['conv_b'] + c, [[1, 1]]), scale=1.0, reads=[bk, 'VEC'], writes=['UBy%d' % c])
            if KSTOP == 'A5':
                raise _StopMix()
            for (t0, w) in TILES:
                ys = [SCR16.v(UBo + c * UBW + 30 + t0, [[1, w]]) for c in range(4)]
                for c in range(4):
                    P.I('act', 'activation', SCR16.v(c * 512, [[1, w]]), ys[c], AF.Square, reads=['UBy%d' % c], writes=['YQ'])
                b1, k1 = pbank((4, 5))
                b2, k2 = pbank((4, 5))
                for c in range(4):
                    P.I('pe', 'matmul', PS[b1].v(0, [[1, w]]), ONB.v(0, [[1, 128]]), ys[c], start=(c == 0), stop=(c == 3),
                        reads=['ONB', 'UBy%d' % c], writes=[k1])
                for c in range(4):
                    P.I('pe', 'matmul', PS[b2].v(0, [[1, w]]), ONB.v(0, [[1, 128]]), SCR16.v(c * 512, [[1, w]]), start=(c == 0), stop=(c == 3),
                        reads=['ONB', 'YQ'], writes=[k2])
                MEAN = SCR32.v(MEANo, [[1, w]])
                TA = TMPA.v(0, [[1, w]])
                RSw = RS.v(0, [[1, w]])
                P.I('act', 'activation', MEAN, PS[b1].v(0, [[1, w]]), AF.Identity, scale=1.0 / 512, reads=[k1], writes=['MEAN'])
                P.I('dve', 'tensor_tensor', TA, MEAN, MEAN, op=ALU.mult, reads=['MEAN'], writes=['TMPA'])
                P.I('dve', 'scalar_tensor_tensor', TA, PS[b2].v(0, [[1, w]]), 1.0 / 512, TA, op0=ALU.mult, op1=ALU.subtract,
                    reads=[k2, 'TMPA'], writes=['TMPA'])
                P.I('act', 'activation', RSw, TA, AF.Sqrt, bias=LN_EPS, scale=1.0, reads=['TMPA'], writes=['RS'])
                P.I('dve', 'reciprocal', RSw, RSw, reads=['RS'], writes=['RS'])
                for c in range(4):
                    P.I('dve', 'tensor_tensor', TA, ys[c], MEAN, op=ALU.subtract, reads=['UBy%d' % c, 'MEAN'], writes=['TMPA'])
                    P.I('dve', 'scalar_tensor_tensor', TA, TA, VEC.v(vo['conv_ln_g'] + c, [[1, 1]]), RSw, op0=ALU.mult, op1=ALU.mult,
                        reads=['TMPA', 'RS', 'VEC'], writes=['TMPA'])
                    P.I('act', 'activation', ys[c], TA, AF.Silu, bias=VEC.v(vo['conv_ln_b'] + c, [[1, 1]]), scale=1.0,
                        reads=['TMPA', 'VEC'], writes=['UBy%d' % c])

            if KSTOP == 'A6':
                raise _StopMix()
            def gated_branch(gate_col0, witem_fn, kcb, rhs_fn, rkeys, first):
                for m in range(8):
                    wb, wk, offs = wload_multi([('w_in', l * 1024 * N_IN, N_IN, gate_col0 + m * 128, 128, 8), witem_fn(m)])
                    for (t0, w) in TILES:
                        bg, kg = pbank(DENSE)
                        bb, kb = pbank(DENSE)
                        for k in range(8):
                            P.I('pe', 'matmul', PS[bg].v(0, [[1, w]]), wb.v(offs[0] + k * 128, [[1, 128]]), H.v(k * NT + t0, [[1, w]]),
                                start=(k == 0), stop=(k == 7), reads=[wk, 'H'], writes=[kg])
                        for k in range(kcb):
                            P.I('pe', 'matmul', PS[bb].v(0, [[1, w]]), wb.v(offs[1] + k * 128, [[1, 128]]), rhs_fn(k, t0, w),
                                start=(k == 0), stop=(k == kcb - 1), reads=[wk] + rkeys, writes=[kb])
                        TA, tak = tmpbuf(w)
                        P.I('act', 'activation', TA, PS[bg].v(0, [[1, w]]), AF.Sigmoid, reads=[kg], writes=[tak])
                        if first:
                            P.I('dve', 'tensor_tensor', MG.v(m * NT + t0, [[1, w]]), TA, PS[bb].v(0, [[1, w]]), op=ALU.mult,
                                reads=[tak, kb], writes=['MG'])
                        else:
                            P.I('dve', 'tensor_tensor', TA, TA, PS[bb].v(0, [[1, w]]), op=ALU.mult, reads=[tak, kb], writes=[tak])
                            P.I('dve', 'tensor_tensor', MG.v(m * NT + t0, [[1, w]]), MG.v(m * NT + t0, [[1, w]]), TA, op=ALU.add,
                                reads=[tak, 'MG'], writes=['MG'])

            gated_branch(O5, lambda m: ('w_conv_out', l * 512 * 1024, 1024, m * 128, 128, 4), 4,
                         lambda k, t0, w: SCR16.v(UBo + k * UBW + 30 + t0, [[1, w]]), ['UBy0', 'UBy1', 'UBy2', 'UBy3'], True)
            P.barrier()


            P.phase = 'attn'
            if KSTOP == 'C0':
                raise _StopMix()
            KTo, QTo0, VTo = 0, 4096, 5120
            NUMo, DENo = 3584, 5632
            ATTo = 15360
            BTo, SSo0, PTo0 = 8736, 9248, 19520
            ANo, ADo = 10016, 10080
            loff = l * 1024 * N_IN
            cnt = {'s': 0, 'n': 0, 'q': 0, 'c': 0}

            def attn_prompt(g, hp):
                win, dil = PATTERNS[g]
                nblk = TPR // (dil * 128)
                wb, wk = wload('w_in', loff, N_IN, O3 + g * 256 + hp * 128, 128, 8)
                for ti in range(4):
                    b, bk = pbank((0, 1))
                    for k in range(8):
                        P.I('pe', 'matmul', PS[b].v(0, [[1, 512]]), wb.v(k * 128, [[1, 128]]), H.v(k * NT + ti * 512, [[1, 512]]),
                            start=(k == 0), stop=(k == 7), reads=[wk, 'H'], writes=[bk])
                    evac(ti, SCR16.v(KTo + ti * 512, [[1, 512]]), PS[b].v(0, [[1, 512]]), [bk], ['KT'])
                wb, wk = wload('w_in', loff, N_IN, O4 + g * 256 + hp * 128, 128, 8)
                for t4 in range(4):
                    b, bk = pbank((0, 1))
                    for tt in range(4):
                        ti = t4 * 4 + tt
                        r, blk = ti // nblk, ti % nblk
                        for k in range(8):
                            P.I('pe', 'matmul', PS[b].v(tt * 128, [[1, 128]]), H.v(k * NT + dil * 128 * blk + r, [[dil, 128]]),
                                wb.v(k * 128, [[1, 128]]), start=(k == 0), stop=(k == 7), reads=[wk, 'H'], writes=[bk])
                    evac(t4, SCR16.v(VTo + t4 * 512, [[1, 512]]), PS[b].v(0, [[1, 512]]), [bk], ['VT'])
                for h in range(2):
                    cc = alibi_slope(g * 4 + hp * 2 + h) * dil
                    P.I('dve', 'scalar_tensor_tensor', SCR32.v(BTo + h * 256, [[1, 256]]), CST.v(C_DIST, [[1, 256]]), -cc,
                        CST.v(C_MASK, [[1, 256]]), op0=ALU.mult, op1=ALU.add, reads=['CST'], writes=['BT'])
                wq, wqk = wload('w_in', loff, N_IN, O2 + g * 256 + hp * 128, 128, 8)
                nq = min(4, nblk)
                units = []
                for r in range(dil):
                    for qc in range(nblk // nq):
                        for bi in range(nq):
                            for h in range(2):
                                units.append((r, qc, bi, h))
                ust = {}

                def stage_a(u):
                    r, qc, bi, h = u
                    p0 = qc * nq * 128
                    width = nq * 128
                    if bi == 0 and h == 0:
                        qi = cnt['q'] % 2
                        cnt['q'] += 1
                        QTo = QTo0 + qi * 512
                        b, bk = pbank((0, 1))
                        for k in range(8):
                            P.I('pe', 'matmul', PS[b].v(0, [[1, width]]), wq.v(k * 128, [[1, 128]]),
                                H.v(k * NT + dil * p0 + r, [[dil, width]]), start=(k == 0), stop=(k == 7), reads=[wqk, 'H'], writes=[bk])
                        evac(qi, SCR16.v(QTo, [[1, width]]), PS[b].v(0, [[1, width]]), [bk], ['QT%d' % qi])
                        ust['q'] = (qi, QTo)
                    qi, QTo = ust['q']
                    if h == 0:
                        ni = cnt['n'] % 2
                        cnt['n'] += 1
                        ust['n'] = ni
                    ni = ust['n']
                    i = qc * nq + bi
                    si = cnt['s'] % 2
                    cnt['s'] += 1
                    stb, stk = 4 + si, 'PS%d' % (4 + si)
                    ncols = 256 if i >= 1 else 128
                    P.I('pe', 'matmul', PS[stb].v(0, [[1, 128]]), SCR16.v(KTo + dil * 128 * i + r, [[dil, 128]], p0=h * 64, pn=64),
                        SCR16.v(QTo + bi * 128, [[1, 128]], p0=h * 64, pn=64), start=True, stop=True,
                        reads=['KT', 'QT%d' % qi], writes=[stk])
                    if i >= 1:
                        P.I('pe', 'matmul', PS[stb].v(128, [[1, 128]]),
                            SCR16.v(KTo + dil * 128 * (i - 1) + r, [[dil, 128]], p0=h * 64, pn=64),
                            SCR16.v(QTo + bi * 128, [[1, 128]], p0=h * 64, pn=64), start=True, stop=True,
                            reads=['KT', 'QT%d' % qi], writes=[stk])
                    SSo = SSo0 + si * 256
                    PTo = PTo0 + si * 256
                    P.I('dve', 'scalar_tensor_tensor', SCR32.v(SSo, [[1, ncols]]), PS[stb].v(0, [[1, ncols]]), 0.125,
                        SCR32.v(BTo + h * 256, [[1, ncols]]), op0=ALU.mult, op1=ALU.add, reads=[stk, 'BT'], writes=['SS%d' % si])
                    P.I('act', 'activation', SCR16.v(PTo, [[1, ncols]]), SCR32.v(SSo, [[1, ncols]]), AF.Exp,
                        reads=['SS%d' % si], writes=['PT%d' % si])
                    return (r, i, h, si, ni, PTo)

                def stage_b(r, i, h, si, ni, PTo):
                    tcur = r * nblk + i
                    for (pb, pkey, lfn) in ((2 + ni, 'PS%d' % (2 + ni), lambda t: SCR16.v(VTo + t * 128 + h * 64, [[1, 64]])),
                                            (6 + ni, 'PS%d' % (6 + ni), lambda t: ONB.v(0, [[1, 64]]))):
                        P.I('pe', 'matmul', PS[pb].v(0, [[1, 128]], p0=h * 64, pn=64), lfn(tcur), SCR16.v(PTo, [[1, 128]]),
                            start=True, stop=(i == 0), reads=['VT', 'ONB', 'PT%d' % si], writes=[pkey])
                        if i >= 1:
                            P.I('pe', 'matmul', PS[pb].v(0, [[1, 128]], p0=h * 64, pn=64), lfn(tcur - 1),
                                SCR16.v(PTo + 128, [[1, 128]]), start=False, stop=True,
                                reads=['VT', 'ONB', 'PT%d' % si], writes=[pkey])
                    if h == 1:
                        tok0 = dil * 128 * i + r
                        an = SCR32.v(NUMo + tok0, [[dil, 128]])
                        ad = SCR32.v(DENo + tok0, [[dil, 128]])
                        if g == 0:
                            P.I('dve', 'tensor_copy', an, PS[2 + ni].v(0, [[1, 128]]), reads=['PS%d' % (2 + ni)], writes=['ACC'])
                            P.I('act', 'copy', ad, PS[6 + ni].v(0, [[1, 128]]), reads=['PS%d' % (6 + ni)], writes=['ACCD'])
                        else:
                            P.I('dve', 'tensor_tensor', an, an, PS[2 + ni].v(0, [[1, 128]]), op=ALU.add,
                                reads=['PS%d' % (2 + ni), 'ACC'], writes=['ACC'])
                            P.I('dve', 'tensor_tensor', ad, ad, PS[6 + ni].v(0, [[1, 128]]), op=ALU.add,
                                reads=['PS%d' % (6 + ni), 'ACCD'], writes=['ACCD'])

                cur = stage_a(units[0])
                for ui_ in range(len(units)):
                    nxt = stage_a(units[ui_ + 1]) if ui_ + 1 < len(units) else None
                    stage_b(*cur)
                    cur = nxt

            def attn_sample(g, hp):
                win, dil = PATTERNS[g]
                nt = 1 if g == 0 else 4
                wb, wk, offs = wload_multi([('w_in', loff, N_IN, O2 + g * 256 + hp * 128, 128, 8),
                                            ('w_in', loff, N_IN, O3 + g * 256 + hp * 128, 128, 8),
                                            ('w_in', loff, N_IN, O4 + g * 256 + hp * 128, 128, 8)])
                b, bk = pbank((0, 1, 2, 3))
                for k in range(8):
                    P.I('pe', 'matmul', PS[b].v(0, [[1, 64]]), wb.v(offs[0] + k * 128, [[1, 128]]), H.v(k * NT + TPR, [[1, 64]]),
                        start=(k == 0), stop=(k == 7), reads=[wk, 'H'], writes=[bk])
                for k in range(8):
                    P.I('pe', 'matmul', PS[b].v(64, [[1, 64]]), wb.v(offs[1] + k * 128, [[1, 128]]), H.v(k * NT + TPR, [[1, 64]]),
                        start=(k == 0), stop=(k == 7), reads=[wk, 'H'], writes=[bk])
                for k in range(8):
                    P.I('pe', 'matmul', PS[b].v(128, [[1, 128]], pn=64), H.v(k * NT + TPR, [[1, 64]]), wb.v(offs[2] + k * 128, [[1, 128]]),
                        start=(k == 0), stop=(k == 7), reads=[wk, 'H'], writes=[bk])
                P.I('dve', 'tensor_copy', SCR16.v(0, [[1, 128]]), PS[b].v(0, [[1, 128]]), reads=[bk], writes=['QKS'])
                P.I('dve', 'tensor_copy', SCR16.v(128, [[1, 128]], pn=64), PS[b].v(128, [[1, 128]], pn=64), reads=[bk], writes=['VS'])
                BTS0o, BCo, BTNo, SSNo, SSSo = 3232, 3240, 3264, 3392, 3488
                PTNo, PTSo = 6912, 6400
                for h in range(2):
                    cc = alibi_slope(g * 4 + hp * 2 + h) * dil
                    if g == 0:
                        P.I('dve', 'scalar_tensor_tensor', SCR32.v(BTNo + h * 64, [[1, 64]], pn=64), CST.v(C_DN0, [[1, 64]], pn=64), -cc,
                            CST.v(C_MN0, [[1, 64]], pn=64), op0=ALU.mult, op1=ALU.add, reads=['CST'], writes=['BTN'])
                        P.I('dve', 'scalar_tensor_tensor', SCR32.v(BTS0o + h * 4, [[1, 4]]), CST.v(C_DS0, [[1, 4]]), -cc,
                            CST.v(C_MS0, [[1, 4]]), op0=ALU.mult, op1=ALU.add, reads=['CST'], writes=['BTS'])
                    else:
                        P.I('dve', 'tensor_scalar', SCR32.v(BCo + h, [[1, 1]]), CST.v(C_DS1, [[1, 1]]), -cc, None, op0=ALU.mult,
                            reads=['CST'], writes=['BTS'])
                for h in range(2):
                    P.I('pe', 'matmul', PS[4].v(0, [[1, 64]], pn=64), SCR16.v(64, [[1, 64]], p0=h * 64, pn=64),
                        SCR16.v(0, [[1, 64]], p0=h * 64, pn=64), start=True, stop=True, reads=['QKS'], writes=['PS4'])
                    btn = SCR32.v(BTNo + h * 64, [[1, 64]], pn=64) if g == 0 else CST.v(C_MN1, [[1, 64]], pn=64)
                    P.I('dve', 'scalar_tensor_tensor', SCR32.v(SSNo, [[1, 64]], pn=64), PS[4].v(0, [[1, 64]], pn=64), 0.125, btn,
                        op0=ALU.mult, op1=ALU.add, reads=['PS4', 'BTN', 'CST'], writes=['SSN'])
                    P.I('act', 'activation', SCR16.v(PTNo, [[1, 64]], pn=64), SCR32.v(SSNo, [[1, 64]], pn=64), AF.Exp,
                        reads=['SSN'], writes=['PTN'])
                    P.I('pe', 'matmul', PS[6].v(256, [[1, 64]], p0=h * 64, pn=64), SCR16.v(128 + h * 64, [[1, 64]], pn=64),
                        SCR16.v(PTNo, [[1, 64]], pn=64), start=True, stop=True, reads=['VS', 'PTN'], writes=['PS6'])
                    P.I('pe', 'matmul', PS[7].v(256, [[1, 64]], p0=h * 64, pn=64), ONB.v(0, [[1, 64]], pn=64),
                        SCR16.v(PTNo, [[1, 64]], pn=64), start=True, stop=True, reads=['ONB', 'PTN'], writes=['PS7'])
                L = (128, 512, 512)[g]

                def stage_T(seq):
                    ci = cnt['c'] % 2
                    cnt['c'] += 1
                    Ksto, Vsto = 128 + ci * 512, 1152 + ci * 512
                    KTco, Vco = 4352 + ci * 512, 5376 + ci * 512
                    base = ((l * 16 + seq) * L) * 256 + hp * 128
                    for (nm, sto, key) in (('ck%d' % g, Ksto, 'KST%d' % ci), ('cv%d' % g, Vsto, 'VST%d' % ci)):
                        if g == 0:
                            P.dma('sp', SCR32.v(sto, [[1, 128]]), dap(dr[nm], base, [[256, 128], [1, 128]]), writes=[key])
                        elif g == 1:
                            P.dma('sp', SCR32.v(sto, [[128, 4], [1, 128]]), dap(dr[nm], base, [[4 * 256, 128], [256, 4], [1, 128]]), writes=[key])
                        else:
                            P.dma('sp', SCR32.v(sto, [[128, 4], [1, 128]]), dap(dr[nm], base, [[256, 128], [128 * 256, 4], [1, 128]]), writes=[key])
                    b, bk = pbank((0, 1, 2, 3))
                    for t in range(nt):
                        P.I('pe', 'transpose', PS[b].v(t * 128, [[1, 128]]), SCR32.v(Ksto + t * 128, [[1, 128]]), IDF(128),
                            reads=['KST%d' % ci, 'CST'], writes=[bk])
                    P.I('dve', 'tensor_copy', SCR16.v(KTco, [[1, nt * 128]]), PS[b].v(0, [[1, nt * 128]]), reads=[bk], writes=['KTC%d' % ci])
                    P.I('act', 'copy', SCR16.v(Vco, [[1, nt * 128]]), SCR32.v(Vsto, [[1, nt * 128]]), reads=['VST%d' % ci],
                        writes=['VC%d' % ci])
                    return ci, KTco, Vco

                def stage_Q(seq, h, ci, KTco, Vco):
                    si = cnt['s'] % 2
                    cnt['s'] += 1
                    stb, stk = 4 + si, 'PS%d' % (4 + si)
                    pts = SCR16.v(PTSo + si * 4, [[1, 4]])
                    if g == 0:
                        P.I('pe', 'matmul', PS[stb].v(0, [[1, 4]]), SCR16.v(KTco, [[1, 128]], p0=h * 64, pn=64),
                            SCR16.v(seq * 4, [[1, 4]], p0=h * 64, pn=64), start=True, stop=True, reads=['KTC%d' % ci, 'QKS'], writes=[stk])
                        P.I('dve', 'scalar_tensor_tensor', SCR32.v(SSSo + si * 4, [[1, 4]]), PS[stb].v(0, [[1, 4]]), 0.125,
                            SCR32.v(BTS0o + h * 4, [[1, 4]]), op0=ALU.mult, op1=ALU.add, reads=[stk, 'BTS'], writes=['SSS%d' % si])
                        P.I('act', 'activation', pts, SCR32.v(SSSo + si * 4, [[1, 4]]), AF.Exp, reads=['SSS%d' % si], writes=['PTS%d' % si])
                    else:
                        for j in range(4):
                            P.I('pe', 'matmul', PS[stb].v(j, [[1, 1]]), SCR16.v(KTco + j * 128, [[1, 128]], p0=h * 64, pn=64),
                                SCR16.v(seq * 4 + j, [[1, 1]], p0=h * 64, pn=64), start=True, stop=True,
                                reads=['KTC%d' % ci, 'QKS'], writes=[stk])
                        P.I('act', 'activation', pts, PS[stb].v(0, [[1, 4]]), AF.Exp, bias=SCR32.v(BCo + h, [[1, 1]]), scale=0.125,
                            reads=[stk, 'BTS'], writes=['PTS%d' % si])
                    return si

                def stage_V(seq, h, ci, Vco, si):
                    pts = SCR16.v(PTSo + si * 4, [[1, 4]])
                    if g == 0:
                        P.I('pe', 'matmul', PS[6].v(320 + seq * 4, [[1, 4]], p0=h * 64, pn=64), SCR16.v(Vco + h * 64, [[1, 64]]), pts,
                            start=True, stop=True, reads=['VC%d' % ci, 'PTS%d' % si], writes=['PS6'])
                        P.I('pe', 'matmul', PS[7].v(320 + seq * 4, [[1, 4]], p0=h * 64, pn=64), ONB.v(0, [[1, 64]]), pts,
                            start=True, stop=True, reads=['ONB', 'PTS%d' % si], writes=['PS7'])
                    else:
                        for j in range(4):
                            P.I('pe', 'matmul', PS[6].v(320 + seq * 4 + j, [[1, 1]], p0=h * 64, pn=64),
                                SCR16.v(Vco + j * 128 + h * 64, [[1, 64]]), SCR16.v(PTSo + si * 4 + j, [[1, 1]]),
                                start=True, stop=True, reads=['VC%d' % ci, 'PTS%d' % si], writes=['PS6'])
                            P.I('pe', 'matmul', PS[7].v(320 + seq * 4 + j, [[1, 1]], p0=h * 64, pn=64), ONB.v(0, [[1, 64]]),
                                SCR16.v(PTSo + si * 4 + j, [[1, 1]]), start=True, stop=True, reads=['ONB', 'PTS%d' % si], writes=['PS7'])

                tcur = stage_T(0)
                for seq in range(NSEQ):
                    tnext = stage_T(seq + 1) if seq + 1 < NSEQ else None
                    ci, KTco, Vco = tcur
                    s0 = stage_Q(seq, 0, ci, KTco, Vco)
                    s1 = stage_Q(seq, 1, ci, KTco, Vco)
                    stage_V(seq, 0, ci, Vco, s0)
                    stage_V(seq, 1, ci, Vco, s1)
                    tcur = tnext
                an, ad = SCR32.v(ANo, [[1, 64]]), SCR32.v(ADo, [[1, 64]])
                if g == 0:
                    P.I('dve', 'tensor_copy', an, PS[6].v(256, [[1, 64]]), reads=['PS6'], writes=['ACCS'])
                    P.I('dve', 'tensor_tensor', an, an, PS[6].v(320, [[1, 64]]), op=ALU.add, reads=['PS6', 'ACCS'], writes=['ACCS'])
                    P.I('dve', 'tensor_copy', ad, PS[7].v(256, [[1, 64]]), reads=['PS7'], writes=['ACCSD'])
                    P.I('dve', 'tensor_tensor', ad, ad, PS[7].v(320, [[1, 64]]), op=ALU.add, reads=['PS7', 'ACCSD'], writes=['ACCSD'])
                else:
                    for (a_, pb, k_, ka) in ((an, 6, 'PS6', 'ACCS'), (ad, 7, 'PS7', 'ACCSD')):
                        P.I('dve', 'tensor_tensor', a_, a_, PS[pb].v(256, [[1, 64]]), op=ALU.add, reads=[k_, ka], writes=[ka])
                        P.I('dve', 'tensor_tensor', a_, a_, PS[pb].v(320, [[1, 64]]), op=ALU.add, reads=[k_, ka], writes=[ka])

            for hp in range(2):
                P.phase = 'attn_p'
                for g in range(3):
                    attn_prompt(g, hp)
                P.barrier()
                P.phase = 'attn_s'
                if KSTOP != 'NOS':
                    for g in range(3):
                        attn_sample(g, hp)
                else:
                    P.I('pool', 'memset', SCR32.v(ANo, [[1, 64]]), 0.0, writes=['ACCS'])
                    P.I('pool', 'memset', SCR32.v(ADo, [[1, 64]]), 1.0, writes=['ACCSD'])
                P.I('dve', 'reciprocal', SCR32.v(DENo, [[1, 2048]]), SCR32.v(DENo, [[1, 2048]]), reads=['ACCD'], writes=['ACCD'])
                P.I('dve', 'tensor_tensor', SCR16.v(ATTo, [[1, 2048]]), SCR32.v(NUMo, [[1, 2048]]), SCR32.v(DENo, [[1, 2048]]),
                    op=ALU.mult, reads=['ACC', 'ACCD'], writes=['ATT'])
                P.I('dve', 'reciprocal', SCR32.v(ADo, [[1, 64]]), SCR32.v(ADo, [[1, 64]]), reads=['ACCSD'], writes=['ACCSD'])
                P.I('dve', 'tensor_tensor', SCR16.v(ATTo + 2048, [[1, 64]]), SCR32.v(ANo, [[1, 64]]), SCR32.v(ADo, [[1, 64]]),
                    op=ALU.mult, reads=['ACCS', 'ACCSD'], writes=['ATT'])
                P.phase = 'attn_out'
                gated_branch(O5 + 2048, lambda m, hp=hp: ('w_att', l * 256 * 1024 + hp * 128 * 1024, 1024, m * 128, 128, 1), 1,
                             lambda k, t0, w: SCR16.v(ATTo + t0, [[1, w]]), ['ATT'], False)
                P.barrier()

            P.phase = 'ssm'
            if KSTOP == 'B0':
                raise _StopMix()
            YTo, ESTo, TMSo, CAo = 0, 8448, 10496, 12544
            SHo = 6784
            YGo = 17984
            PPo = 9264
            loff = l * 1024 * N_IN
            for ch in range(4):
                wb, wk = wload('w_in', loff, N_IN, O1 + ch * 128, 128, 8)
                for (t0, w) in TILES:
                    b, bk = pbank(DENSE)
                    for k in range(8):
                        P.I('pe', 'matmul', PS[b].v(0, [[1, w]]), wb.v(k * 128, [[1, 128]]), H.v(k * NT + t0, [[1, w]]),
                            start=(k == 0), stop=(k == 7), reads=[wk, 'H'], writes=[bk])
                    if t0 < TPR:
                        evac(b, SCR16.v(YTo + ch * NT + t0 // 8, [[256, 8], [1, 64]]), PS[b].v(0, [[1, 8], [8, 64]]), [bk], ['YT%d' % ch])
                    else:
                        evac(b, SCR16.v(YTo + ch * NT + t0, [[1, w]]), PS[b].v(0, [[1, w]]), [bk], ['YT%d' % ch])
            P.barrier()

            def slot(k, n=4):
                return SCR32.v(PPo + 4 * k, [[1, n]])
            S_AR, S_AI, S_DT, S_MAG, S_C, S_S, S_ABR, S_ABI, S_FR, S_FI, S_T1, S_T2, S_T3, S_T4 = range(14)
            S_P = 14
            S_W = 32
            S_SQ = 48
            PWo = PPo + 4 * 56
            BSTo = PWo + 128
            CREo = BSTo + 256
            KBo = CREo + 128
            DMo = KBo + 128
            H0o = DMo + 4
            S4o = H0o + 128
            assert S4o + 128 <= SCRB // 4
            dv = lambda *a, **k: P.I('dve', *a, **k)
            KK = ['PP']

            def tt(o, a, b_, op):
                dv('tensor_tensor', o, a, b_, op=op, reads=KK, writes=KK)

            def cmul(or_, oi_, ar_, ai_, br_, bi_, t1, t2):
                tt(t1, ar_, br_, ALU.mult)
                tt(t2, ai_, bi_, ALU.mult)
                tt(or_, t1, t2, ALU.subtract)
                tt(t1, ar_, bi_, ALU.mult)
                tt(t2, ai_, br_, ALU.mult)
                tt(oi_, t1, t2, ALU.add)

            for ch in range(4):
                ukey = 'YT%d' % ch
                P.phase = 'ssm_ld'
                P.dma('sp', RS.v(0, [[1, 128]], pn=4), dap(dr['ssm_a_re'], l * 2048 + ch * 512, [[128, 4], [1, 128]]), writes=['STA'])
                P.dma('sp', RS.v(128, [[1, 128]], pn=4), dap(dr['ssm_a_im'], l * 2048 + ch * 512, [[128, 4], [1, 128]]), writes=['STA2'])
                for ri, nm in enumerate(('ssm_c_re', 'ssm_c_im')):
                    for q in range(4):
                        P.dma('sp', RS.v(256 + ri * 128, [[64, 2], [1, 64]], p0=q * 16, pn=16),
                              dap(dr[nm], l * 32768 + (8 * ch + 2 * q) * 1024, [[64, 16], [1024, 2], [1, 64]]), writes=['STC%d%d' % (ri, q)])
                for ri, nm in enumerate(('st_re', 'st_im')):
                    for q in range(4):
                        P.dma('sp', TMPA.v(ri * 128, [[1, 128]], p0=q * 16, pn=16),
                              dap(dr[nm], l * 16 * 2048 + (4 * ch + q) * 128, [[2048, 16], [1, 128]]), writes=['STH%d%d' % (ri, q)])
                for g2 in range(2):
                    P.dma('sp', SCR32.v(PPo + 4 * S_DT, [[1, 4]], p0=g2 * 64, pn=64),
                          dap(dr['ssm_log_dt'], l * 32 + 8 * ch + g2, [[0, 64], [2, 4]]), writes=['PPDT%d' % g2], allow_slow_non_contiguous=True)
                P.I('pool', 'memset', SCR32.v(BSTo, [[1, 256]]), 0.0, writes=['BST00', 'BST01', 'BST10', 'BST11'])
                for g2 in range(2):
                    for ri, nm in enumerate(('ssm_b_re', 'ssm_b_im')):
                        P.dma('sp', SCR32.v(BSTo + ri * 128 + g2 * 16, [[32, 4], [1, 16]], p0=g2 * 64, pn=64),
                              dap(dr[nm], l * 32768 + (8 * ch + g2) * 1024, [[16, 64], [2048, 4], [1, 16]]), writes=['BST%d%d' % (g2, ri)],
                              allow_slow_non_contiguous=True)
                P.dma('sp', SCR32.v(DMo, [[1, 1]]), dap(dr['ssm_d'], l * 512 + ch * 128, [[1, 128], [1, 1]]), writes=['PPD'],
                      allow_slow_non_contiguous=True)
                tb_, tbk = pbank((5, 6))
                P.I('pe', 'transpose', PS[tb_].v(0, [[1, 4]]), RS.v(0, [[1, 128]], pn=4), IDF(4), reads=['STA', 'CST'], writes=[tbk])
                P.I('pe', 'transpose', PS[tb_].v(4, [[1, 4]]), RS.v(128, [[1, 128]], pn=4), IDF(4), reads=['STA2', 'CST'], writes=[tbk])
                for ri in range(2):
                    P.I('pe', 'transpose', PS[tb_].v(64 + ri * 64, [[1, 64]]), RS.v(256 + ri * 128, [[1, 128]], pn=64), IDF(64),
                        reads=['STC%d%d' % (ri, q) for q in range(4)] + ['CST'], writes=[tbk])
                    P.I('pe', 'transpose', PS[tb_].v(192 + ri * 64, [[1, 64]]), TMPA.v(ri * 128, [[1, 128]], pn=64), IDF(64),
                        reads=['STH%d%d' % (ri, q) for q in range(4)] + ['CST'], writes=[tbk])
                LK = [tbk, 'PPDT0', 'PPDT1', 'PPD', 'BST00', 'BST01', 'BST10', 'BST11']
                dv('tensor_copy', SCR32.v(PPo, [[1, 8]]), PS[tb_].v(0, [[1, 8]]), reads=LK + KK, writes=KK)
                dv('tensor_copy', SCR32.v(CREo, [[1, 64]]), PS[tb_].v(64, [[1, 64]]), reads=LK + KK, writes=KK)
                dv('tensor_scalar', SCR32.v(CREo + 64, [[1, 64]]), PS[tb_].v(128, [[1, 64]]), -1.0, None, op0=ALU.mult, reads=LK + KK, writes=KK)
                for ri in range(2):
                    dv('tensor_copy', SCR32.v(H0o + ri * 16, [[32, 4], [1, 16]]), PS[tb_].v(192 + ri * 64, [[16, 4], [1, 16]]),
                       reads=LK + KK, writes=KK)
                AR, AI, DT, MAG, CC, SS_, ABR, ABI, FR, FI, T1, T2, T3, T4 = [slot(k) for k in range(14)]
                P.I('act', 'activation', DT, DT, AF.Exp, reads=KK, writes=KK)
                tt(T1, DT, AR, ALU.mult)
                P.I('act', 'activation', MAG, T1, AF.Exp, reads=KK, writes=KK)
                tt(T2, DT, AI, ALU.mult)
                P.I('act', 'activation', CC, T2, AF.Sin, bias=math.pi / 2, scale=1.0 / 32, reads=KK, writes=KK)
                P.I('act', 'activation', SS_, T2, AF.Sin, scale=1.0 / 32, reads=KK, writes=KK)
                for _ in range(5):
                    tt(T1, CC, CC, ALU.mult)
                    tt(T2, SS_, SS_, ALU.mult)
                    tt(T3, CC, SS_, ALU.mult)
                    tt(CC, T1, T2, ALU.subtract)
                    dv('tensor_scalar', SS_, T3, 2.0, None, op0=ALU.mult, reads=KK, writes=KK)
                tt(ABR, MAG, CC, ALU.mult)
                tt(ABI, MAG, SS_, ALU.mult)
                tt(T1, AR, AR, ALU.mult)
                tt(T2, AI, AI, ALU.mult)
                tt(T1, T1, T2, ALU.add)
                dv('reciprocal', T1, T1, reads=KK, writes=KK)
                dv('tensor_scalar', T2, ABR, -1.0, None, op0=ALU.add, reads=KK, writes=KK)
                tt(T3, T2, AR, ALU.mult)
                tt(T4, ABI, AI, ALU.mult)
                tt(T3, T3, T4, ALU.add)
                tt(FR, T3, T1, ALU.mult)
                tt(T3, ABI, AR, ALU.mult)
                tt(T4, T2, AI, ALU.mult)
                tt(T3, T3, T4, ALU.subtract)
                tt(FI, T3, T1, ALU.mult)
                Pr = lambda j: slot(S_P + 2 * j)
                Pi = lambda j: slot(S_P + 2 * j + 1)
                Wr = lambda j: slot(S_W + 2 * j)
                Wi = lambda j: slot(S_W + 2 * j + 1)
                P.I('pool', 'memset', Pr(0), 1.0, reads=KK, writes=KK)
                P.I('pool', 'memset', Pi(0), 0.0, reads=KK, writes=KK)
                dv('tensor_copy', Pr(1), ABR, reads=KK, writes=KK)
                dv('tensor_copy', Pi(1), ABI, reads=KK, writes=KK)
                def pv(base_slot, j0, n):
                    return SCR32.v(PPo + 4 * (base_slot + 2 * j0), [[8, n], [1, 4]])

                def bcn(sl_, n):
                    return SCR32.v(PPo + 4 * sl_, [[0, n], [1, 4]])
                cmul(Pr(2), Pi(2), Pr(1), Pi(1), ABR, ABI, T1, T2)
                for (j0, n_, src0) in ((3, 2, 1), (5, 4, 1)):
                    mj = j0 - src0
                    cmul(pv(S_P, j0, n_), pv(S_P + 1, j0, n_), pv(S_P, src0, n_), pv(S_P + 1, src0, n_),
                         bcn(S_P + 2 * mj, n_), bcn(S_P + 2 * mj + 1, n_), TMPA.v(0, [[4, n_], [1, 4]]), TMPA.v(64, [[4, n_], [1, 4]]))
                cmul(pv(S_W, 0, 8), pv(S_W + 1, 0, 8), pv(S_P, 0, 8), pv(S_P + 1, 0, 8), bcn(S_FR, 8), bcn(S_FI, 8),
                     TMPA.v(0, [[4, 8], [1, 4]]), TMPA.v(64, [[4, 8], [1, 4]]))
                SQr = lambda k: slot(S_SQ + 2 * k)
                SQi = lambda k: slot(S_SQ + 2 * k + 1)
                prev_r, prev_i = Pr(8), Pi(8)
                for k in range(7):
                    cmul(SQr(k), SQi(k), prev_r, prev_i, prev_r, prev_i, T1, T2)
                    prev_r, prev_i = SQr(k), SQi(k)
                bc = lambda sl_, n: SCR32.v(PPo + 4 * sl_, [[1, 4], [0, n]])
                S_PW3 = S_SQ + 14
                dv('tensor_copy', slot(S_PW3), Pr(8), reads=KK, writes=KK)
                dv('tensor_copy', slot(S_PW3 + 1), Pi(8), reads=KK, writes=KK)
                dv('tensor_copy', slot(S_PW3 + 2), SQr(0), reads=KK, writes=KK)
                dv('tensor_copy', slot(S_PW3 + 3), SQi(0), reads=KK, writes=KK)
                cmul(slot(S_PW3 + 4), slot(S_PW3 + 5), SQr(0), SQi(0), Pr(8), Pi(8), T1, T2)
                P.phase = 'ssm_E'
                BRE = SCR32.v(BSTo, [[32, 4], [1, 32]])
                BIM = SCR32.v(BSTo + 128, [[32, 4], [1, 32]])
                ER = TMPA.v(0, [[32, 4], [1, 32]])
                EI = TMPA.v(128, [[32, 4], [1, 32]])
                ET = TMPA.v(256, [[32, 4], [1, 32]])
                P.I('dve', 'memset', PS[7].v(0, [[1, 512]]), 0.0, reads=['PS7'], writes=['PS7'])
                for j in range(8):
                    wr_, wi_ = bc(S_W + 2 * j, 32), bc(S_W + 2 * j + 1, 32)
                    dv('tensor_tensor', ER, BRE, wr_, op=ALU.mult, reads=KK + ['TMPA'], writes=['TMPA'])
                    dv('tensor_tensor', ET, BIM, wi_, op=ALU.mult, reads=KK + ['TMPA'], writes=['TMPA'])
                    dv('tensor_tensor', ER, ER, ET, op=ALU.subtract, reads=['TMPA'], writes=['TMPA'])
                    dv('tensor_tensor', EI, BIM, wr_, op=ALU.mult, reads=KK + ['TMPA'], writes=['TMPA'])
                    dv('tensor_tensor', ET, BRE, wi_, op=ALU.mult, reads=KK + ['TMPA'], writes=['TMPA'])
                    dv('tensor_tensor', EI, EI, ET, op=ALU.add, reads=['TMPA'], writes=['TMPA'])
                    b, bk = pbank((5, 6))
                    P.I('pe', 'transpose', PS[b].v(0, [[1, 128]]), TMPA.v(0, [[1, 128]]), IDF(128), reads=['TMPA', 'CST'], writes=[bk])
                    P.I('pe', 'transpose', PS[b].v(128, [[1, 128]]), TMPA.v(128, [[1, 128]]), IDF(128), reads=['TMPA', 'CST'], writes=[bk])
                    P.I('act', 'copy', SCR16.v(ESTo + j * 256, [[1, 256]]), PS[b].v(0, [[1, 256]]), reads=[bk], writes=['EST'])
                    P.I('pe', 'matmul', PS[7].v(j * 64, [[1, 64]]), TMPA.v(0, [[1, 128]]), SCR32.v(CREo, [[1, 64]]), start=True, stop=False,
                        reads=['TMPA'] + KK, writes=['PS7'])
                    P.I('pe', 'matmul', PS[7].v(j * 64, [[1, 64]]), TMPA.v(128, [[1, 128]]), SCR32.v(CREo + 64, [[1, 64]]), start=False, stop=True,
                        reads=['TMPA'] + KK, writes=['PS7'])
                dv('tensor_tensor', TMPA.v(0, [[64, 8], [16, 4], [1, 16]]), PS[7].v(0, [[64, 8], [16, 4], [1, 16]]),
                   CST.v(C_QM, [[0, 8], [1, 4], [0, 16]]), op=ALU.mult, reads=['PS7', 'CST', 'TMPA'], writes=['TMPA'])
                dv('tensor_reduce', SCR32.v(KBo, [[16, 8], [1, 16]]), TMPA.v(0, [[64, 8], [1, 16], [16, 4]]), axis=AX.X, op=ALU.add,
                   reads=['TMPA'] + KK, writes=KK)
                dv('scalar_tensor_tensor', SCR32.v(KBo, [[1, 16]]), CST.v(C_DELTA, [[1, 16]]), SCR32.v(DMo, [[1, 1]]), SCR32.v(KBo, [[1, 16]]),
                   op0=ALU.mult, op1=ALU.add, reads=KK + ['CST'], writes=KK)
                P.I('pool', 'memset', SCR16.v(TMSo, [[1, 2048]]), 0.0, reads=['TMS'], writes=['TMS'])
                for g2s in range(2):
                    for s_ in range(8):
                        n_ = (8 - s_) * 16
                        dv('tensor_scalar', SCR16.v(TMSo + (g2s * 8 + s_) * 128 + s_ * 16, [[1, n_]]), SCR32.v(KBo, [[1, n_]]),
                           CST.v(C_MG2 + g2s, [[1, 1]]), None, op0=ALU.mult, reads=KK + ['CST', 'TMS'], writes=['TMS'])
                CRE4 = SCR32.v(CREo, [[16, 4], [0, 8], [1, 16]])
                NCI4 = SCR32.v(CREo + 64, [[16, 4], [0, 8], [1, 16]])
                PR4 = SCR32.v(PPo + 4 * (S_P + 2), [[1, 4], [8, 8], [0, 16]])
                PI4 = SCR32.v(PPo + 4 * (S_P + 3), [[1, 4], [8, 8], [0, 16]])
                A1 = TMPA.v(0, [[128, 4], [16, 8], [1, 16]])
                A2 = RS.v(0, [[128, 4], [16, 8], [1, 16]])
                CK = KK + ['TMPA', 'RS']
                dv('tensor_tensor', A1, CRE4, PR4, op=ALU.mult, reads=CK, writes=['TMPA'])
                dv('tensor_tensor', A2, NCI4, PI4, op=ALU.mult, reads=CK, writes=['RS'])
                dv('tensor_tensor', SCR16.v(CAo, [[256, 4], [16, 8], [1, 16]]), A1, A2, op=ALU.add, reads=['TMPA', 'RS', 'CA'], writes=['CA'])
                dv('tensor_tensor', A1, NCI4, PR4, op=ALU.mult, reads=CK, writes=['TMPA'])
                dv('tensor_tensor', A2, CRE4, PI4, op=ALU.mult, reads=CK, writes=['RS'])
                dv('tensor_tensor', SCR16.v(CAo + 128, [[256, 4], [16, 8], [1, 16]]), A1, A2, op=ALU.subtract, reads=['TMPA', 'RS', 'CA'], writes=['CA'])
                P.phase = 'ssm_inj'
                P.I('pool', 'memset', SCR32.v(SHo, [[276, 8], [1, 1]]), 0.0, reads=['SH'], writes=['SH'])
                for q in range(4):
                    for ri in range(2):
                        b, bk = pbank((5, 6))
                        for s_ in range(8):
                            P.I('pe', 'matmul', PS[b].v(0, [[1, 256]]), SCR16.v(ESTo + ((7 - s_) * 2 + ri) * 128, [[1, 128]], p0=32 * q, pn=32),
                                SCR16.v(YTo + ch * NT + s_ * 256, [[1, 256]], p0=32 * q, pn=32), start=(s_ == 0), stop=(s_ == 7),
                                tile_position=(32 * q, 0), reads=['EST', ukey], writes=[bk])
                        for s_ in range(4):
                            P.I('pe', 'matmul', PS[b].v(256, [[1, 16]]), SCR16.v(ESTo + ((3 - s_) * 2 + ri) * 128, [[1, 128]], p0=32 * q, pn=32),
                                SCR16.v(YTo + ch * NT + TPR + s_, [[4, 16]], p0=32 * q, pn=32), start=(s_ == 0), stop=(s_ == 3),
                                tile_position=(32 * q, 0), reads=['EST', ukey], writes=[bk])
                        P.I('act', 'copy', SCR32.v(SHo + (q * 2 + ri) * 276 + 1, [[1, 256]]), PS[b].v(0, [[1, 256]]), reads=[bk], writes=['SH'])
                        P.I('act', 'copy', SCR32.v(S4o + (q * 2 + ri) * 16, [[1, 16]]), PS[b].v(256, [[1, 16]]), reads=[bk], writes=KK)
                KS = ['SH']

                def hv(ri, col0, dims):
                    return SCR32.v(SHo + ri * 276 + col0, [[552, 4]] + dims)

                def st2(o, a, b_, op):
                    dv('tensor_tensor', o, a, b_, op=op, reads=KS + KK + ['TMPA', 'RS'], writes=KS)

                def tmp2(o, a, b_, key):
                    dv('tensor_tensor', o, a, b_, op=ALU.mult, reads=KS + KK + [key], writes=[key])
                a8r, a8i = bc(S_P + 16, 64), bc(S_P + 17, 64)
                U1 = TMPA.v(0, [[64, 4], [1, 64]])
                U2 = RS.v(0, [[64, 4], [1, 64]])
                for jj in range(1, 4):
                    hr, hi = hv(0, 1 + jj, [[4, 64]]), hv(1, 1 + jj, [[4, 64]])
                    pr_, pi_ = hv(0, jj, [[4, 64]]), hv(1, jj, [[4, 64]])
                    tmp2(U1, a8r, pr_, 'TMPA')
                    tmp2(U2, a8i, pi_, 'RS')
                    st2(hr, hr, U1, ALU.add)
                    st2(hr, hr, U2, ALU.subtract)
                    tmp2(U1, a8r, pi_, 'TMPA')
                    tmp2(U2, a8i, pr_, 'RS')
                    st2(hi, hi, U1, ALU.add)
                    st2(hi, hi, U2, ALU.add)
                for di, d_ in enumerate((1, 2, 4, 8, 16, 32)):
                    n_ = 64 - d_
                    adr, adi = bc(S_SQ + 2 * (di + 1), n_), bc(S_SQ + 2 * (di + 1) + 1, n_)
                    sr, si_ = hv(0, 4, [[4, n_]]), hv(1, 4, [[4, n_]])
                    dr_, di2 = hv(0, 4 * d_ + 4, [[4, n_]]), hv(1, 4 * d_ + 4, [[4, n_]])
                    q1, q2 = TMPA.v(0, [[64, 4], [1, n_]]), TMPA.v(256, [[64, 4], [1, n_]])
                    q3, q4 = RS.v(0, [[64, 4], [1, n_]]), RS.v(256, [[64, 4], [1, n_]])
                    tmp2(q1, adr, sr, 'TMPA')
                    tmp2(q2, adi, si_, 'TMPA')
                    tmp2(q3, adr, si_, 'RS')
                    tmp2(q4, adi, sr, 'RS')
                    st2(dr_, dr_, q1, ALU.add)
                    st2(dr_, dr_, q2, ALU.subtract)
                    st2(di2, di2, q3, ALU.add)
                    st2(di2, di2, q4, ALU.add)
                for q in range(4):
                    def hq(ri, col0, dims):
                        return SCR32.v(SHo + (q * 2 + ri) * 276 + col0, dims)
                    er = hq(0, 4, [[4, 63], [0, 3]])
                    ei = hq(1, 4, [[4, 63], [0, 3]])
                    hr = hq(0, 5, [[4, 63], [1, 3]])
                    hi = hq(1, 5, [[4, 63], [1, 3]])
                    pwr = SCR32.v(PPo + 4 * S_PW3 + q, [[0, 63], [8, 3]])
                    pwi = SCR32.v(PPo + 4 * (S_PW3 + 1) + q, [[0, 63], [8, 3]])
                    F1 = TMPA.v(0, [[3, 63], [1, 3]])
                    F2 = RS.v(0, [[3, 63], [1, 3]])
                    tmp2(F1, pwr, er, 'TMPA')
                    tmp2(F2, pwi, ei, 'RS')
                    st2(hr, hr, F1, ALU.add)
                    st2(hr, hr, F2, ALU.subtract)
                    tmp2(F1, pwr, ei, 'TMPA')
                    tmp2(F2, pwi, er, 'RS')
                    st2(hi, hi, F1, ALU.add)
                    st2(hi, hi, F2, ALU.add)
                for ri in range(2):
                    dv('tensor_copy', SCR32.v(SHo + ri * 276 + 257, [[552, 4], [1, 16]]), SCR32.v(H0o + ri * 16, [[32, 4], [1, 16]]),
                       reads=KK + KS, writes=KS)
                h0r, h0i = SCR32.v(H0o, [[32, 4], [1, 16]]), SCR32.v(H0o + 16, [[32, 4], [1, 16]])
                s4r, s4i = SCR32.v(S4o, [[32, 4], [1, 16]]), SCR32.v(S4o + 16, [[32, 4], [1, 16]])
                p4r, p4i = bc(S_P + 8, 16), bc(S_P + 9, 16)
                W1 = TMPA.v(0, [[16, 4], [1, 16]])
                for (o_, x1, y1, x2, y2, op2) in ((s4r, p4r, h0r, p4i, h0i, ALU.subtract), (s4i, p4r, h0i, p4i, h0r, ALU.add)):
                    dv('tensor_tensor', W1, x1, y1, op=ALU.mult, reads=KK + ['TMPA'], writes=['TMPA'])
                    dv('tensor_tensor', o_, o_, W1, op=ALU.add, reads=KK + ['TMPA'], writes=KK)
                    dv('tensor_tensor', W1, x2, y2, op=ALU.mult, reads=KK + ['TMPA'], writes=['TMPA'])
                    dv('tensor_tensor', o_, o_, W1, op=op2, reads=KK + ['TMPA'], writes=KK)
                P.phase = 'ssm_so'
                ob, obk = pbank((5, 6))
                for ri in range(2):
                    P.I('pe', 'transpose', PS[ob].v(ri * 128, [[1, 128]], pn=4), SCR32.v(SHo + ri * 276 + 256, [[552, 4]]), IDF(128),
                        reads=KS + ['CST'], writes=[obk])
                    dv('tensor_copy', RS.v(256 + ri * 64, [[16, 4], [1, 16]]), SCR32.v(S4o + ri * 16, [[32, 4], [1, 16]]),
                       reads=KK, writes=['STC%d%d' % (r_, q_) for r_ in range(2) for q_ in range(4)])
                    P.I('pe', 'transpose', PS[ob].v(256 + ri * 128, [[1, 128]], pn=64), RS.v(256 + ri * 64, [[1, 64]]), IDF(128),
                        reads=['STC00', 'CST'], writes=[obk])
                dv('tensor_copy', RS.v(0, [[1, 256]], pn=4), PS[ob].v(0, [[1, 256]], pn=4), reads=[obk, 'RS', 'STA', 'STA2'], writes=['STO1'])
                dv('tensor_copy', TMPA.v(0, [[1, 256]], pn=64), PS[ob].v(256, [[1, 256]], pn=64), reads=[obk, 'TMPA'], writes=['STO2', 'TMPA'])
                for ri, (np_, ns_) in enumerate((('ssm_re_p', 'ssm_re_s'), ('ssm_im_p', 'ssm_im_s'))):
                    P.dma('sp', dap(dr[np_], l * 2048 + ch * 512, [[128, 4], [1, 128]]), RS.v(ri * 128, [[1, 128]], pn=4),
                          reads=['STO1', 'RS'], writes=['OUT' + np_], out=True)
                    for q in range(4):
                        P.dma('sp', dap(dr[ns_], l * 16 * 2048 + (4 * ch + q) * 128, [[2048, 16], [1, 128]]),
                              TMPA.v(ri * 128, [[1, 128]], p0=q * 16, pn=16), reads=['STO2', 'TMPA'], writes=['OUT' + ns_], out=True)
                dv('tensor_copy', SCR16.v(2 * SHo, [[552, 8], [1, 273]]), SCR32.v(SHo, [[276, 8], [1, 273]]), reads=KS, writes=['SHB', 'SH'])
                P.phase = 'ssm_Y'
                for b_ in range(5):
                    P.I('dve', 'memset', PS[b_].v(0, [[1, 512]]), 0.0, reads=['PS%d' % b_], writes=['PS%d' % b_])
                def emit_Y(gl):
                    q, g2 = gl // 2, gl % 2
                    yb, ybk = pbank((5, 6))
                    for s_ in range(8):
                        P.I('pe', 'matmul', PS[yb].v(0, [[1, 256]]), SCR16.v(TMSo + (g2 * 8 + s_) * 128, [[1, 128]], p0=32 * q, pn=32),
                            SCR16.v(YTo + ch * NT + s_ * 256, [[1, 256]], p0=32 * q, pn=32), start=(s_ == 0), stop=False,
                            tile_position=(32 * q, 0), reads=['TMS', ukey], writes=[ybk])
                    for ri in range(2):
                        P.I('pe', 'matmul', PS[yb].v(0, [[1, 256]]), SCR16.v(CAo + (q * 2 + ri) * 128, [[1, 128]], p0=64 * g2, pn=64),
                            SCR16.v(2 * SHo + (q * 2 + ri) * 552, [[1, 256]], p0=64 * g2, pn=64), start=False, stop=(ri == 1),
                            sync_prev=(ri == 0), reads=['CA', 'SHB'], writes=[ybk])
                    for s_ in range(4):
                        P.I('pe', 'matmul', PS[yb].v(256, [[1, 16]]), SCR16.v(TMSo + (g2 * 8 + s_) * 128, [[1, 128]], p0=32 * q, pn=32),
                            SCR16.v(YTo + ch * NT + TPR + s_, [[4, 16]], p0=32 * q, pn=32), start=(s_ == 0), stop=False,
                            tile_position=(32 * q, 0), sync_prev=(s_ == 0), reads=['TMS', ukey], writes=[ybk])
                    for ri in range(2):
                        P.I('pe', 'matmul', PS[yb].v(256, [[1, 16]]), SCR16.v(CAo + (q * 2 + ri) * 128, [[1, 128]], p0=64 * g2, pn=64),
                            SCR16.v(2 * SHo + (q * 2 + ri) * 552 + 257, [[1, 16]], p0=64 * g2, pn=64), start=False, stop=(ri == 1),
                            sync_prev=(ri == 0), reads=['CA', 'SHB'], writes=[ybk])
                    return yb, ybk

                def emit_post(gl, yb, ybk):
                    yg = SCR16.v(YGo + (gl % 2) * 272, [[1, 272]])
                    P.I('act', 'activation', yg, PS[yb].v(0, [[1, 272]]), AF.Gelu_apprx_tanh, reads=[ybk], writes=['YG%d' % (gl % 2)])
                    for t in range(8):
                        sel = SELB.v(((t % 2) * 8 + gl) * 128, [[1, 128]], p0=32 * (t // 2), pn=32)
                        for tb in range(4):
                            P.I('pe', 'matmul', PS[tb].v(t, [[8, 64]]), sel, SCR16.v(YGo + (gl % 2) * 272 + 64 * tb, [[1, 64]], p0=32 * (t // 2), pn=32),
                                start=False, stop=False, skip_group_check=True, tile_position=(32 * (t // 2), 0), sync_prev=(tb == 0 and t % 2 == 0),
                                reads=['SELB', 'YG%d' % (gl % 2)], writes=['PS%d' % tb])
                        if t < 4:
                            P.I('pe', 'matmul', PS[4].v(t, [[4, 16]]), sel, SCR16.v(YGo + (gl % 2) * 272 + 256, [[1, 16]], p0=32 * (t // 2), pn=32),
                                start=False, stop=False, skip_group_check=True, tile_position=(32 * (t // 2), 0),
                                reads=['SELB', 'YG%d' % (gl % 2)], writes=['PS4'])

                ycur = emit_Y(0)
                for gl in range(8):
                    ynext = emit_Y(gl + 1) if gl < 7 else None
                    emit_post(gl, *ycur)
                    ycur = ynext
                for tb in range(4):
                    evac(tb, SCR16.v(YTo + ch * NT + tb * 512, [[1, 512]]), PS[tb].v(0, [[1, 512]]), ['PS%d' % tb], [ukey])
                evac(0, SCR16.v(YTo + ch * NT + TPR, [[1, 64]]), PS[4].v(0, [[1, 64]]), ['PS4'], [ukey])
                P.barrier()
                if KSTOP == 'B1':
                    raise _StopMix()

            P.phase = 'ssm_glu'
            for m in range(8):
                wb, wk, offs = wload_multi([('w_in', loff, N_IN, O5 + 1024 + m * 128, 128, 8),
                                            ('w_ssm_glu', l * 512 * 2048, 2048, m * 128, 128, 4),
                                            ('w_ssm_glu', l * 512 * 2048, 2048, 1024 + m * 128, 128, 4)])
                for (t0, w) in TILES:
                    bg, kg = pbank(DENSE)
                    b1, k1 = pbank(DENSE)
                    b2, k2 = pbank(DENSE)
                    for k in range(8):
                        P.I('pe', 'matmul', PS[bg].v(0, [[1, w]]), wb.v(offs[0] + k * 128, [[1, 128]]), H.v(k * NT + t0, [[1, w]]),
                            start=(k == 0), stop=(k == 7), reads=[wk, 'H'], writes=[kg])
                    for (bb_, kb_, oo) in ((b1, k1, offs[1]), (b2, k2, offs[2])):
                        for k in range(4):
                            P.I('pe', 'matmul', PS[bb_].v(0, [[1, w]]), wb.v(oo + k * 128, [[1, 128]]), SCR16.v(YTo + k * NT + t0, [[1, w]]),
                                start=(k == 0), stop=(k == 3), reads=[wk, 'YT0', 'YT1', 'YT2', 'YT3'], writes=[kb_])
                    TA = TMPA.v(0, [[1, w]])
                    TB = RS.v(0, [[1, w]])
                    P.I('act', 'activation', TA, PS[bg].v(0, [[1, w]]), AF.Sigmoid, reads=[kg], writes=['TMPA'])
                    P.I('act', 'activation', TB, PS[b2].v(0, [[1, w]]), AF.Sigmoid, reads=[k2], writes=['RS'])
                    dv('tensor_tensor', TB, TB, PS[b1].v(0, [[1, w]]), op=ALU.mult, reads=['RS', k1], writes=['RS'])
                    dv('tensor_tensor', TA, TA, TB, op=ALU.mult, reads=['TMPA', 'RS'], writes=['TMPA'])
                    P.I('dve', 'tensor_tensor', MG.v(m * NT + t0, [[1, w]]), MG.v(m * NT + t0, [[1, w]]), TA, op=ALU.add,
                        reads=['TMPA', 'MG'], writes=['MG'])
            P.barrier()
            if KSTOP == 'A7':
                raise _StopMix()
            P.phase = 'wout'
            MOoD = 2048

            def MOapD(k, w):
                if w is None:
                    return SCR32.v(MOoD + k * 512, [[4, 16], [1, 4]])
                return SCR32.v(MOoD + k * 512, [[1, w]])
            wo_ = [wload('w_out', l * 1024 * 1024, 1024, hh * 512, 512, 8) for hh in range(2)]
            for (t0, w) in TILES:
                for m in range(8):
                    wb, wk = wo_[m // 4]
                    b, bk = pbank(DENSE)
                    for k in range(8):
                        P.I('pe', 'matmul', PS[b].v(0, [[1, w]]), wb.v(k * 512 + (m % 4) * 128, [[1, 128]]), MG.v(k * NT + t0, [[1, w]]),
                            start=(k == 0), stop=(k == 7), reads=[wk, 'MG'], writes=[bk])
                    evac(m, SCR32.v(MOoD + m * 512, [[1, w]]), PS[b].v(0, [[1, w]]), [bk], ['MO'])
                if KSTOP != 'D1':
                    post_norm_add(MOapD, t0, w, GMt, 6144)
            P.barrier()
        try:
            mixer()
        except _StopMix:
            P.barrier()

        P.phase = 'ffn'
        pre_norm(AFt, BFv)
        P.barrier()
        FW = 1088
        MOoff = 5856

        def Fap(j, c, w):
            if j < 15:
                return MG.v(j * FW + c, [[1, w]])
            return SCR16.v(4096 + (j - 15) * FW + c, [[1, w]])

        def MOap(k, w):
            if w is None:
                return SCR32.v(MOoff + k * 512, [[4, 16], [1, 4]])
            return SCR32.v(MOoff + k * 512, [[1, w]])

        for half in range(2):
            htiles = TILES[0:2] if half == 0 else TILES[2:5]
            hc0 = htiles[0][0]
            for j in range(22):
                wb, wk = wload('w_ffn_in', l * 1024 * 5632, 5632, 0, 0, 8, pieces=[(j * 128, 128), (D_FF + j * 128, 128)])
                for (t0, w) in htiles:
                    bg, bgk = pbank(DENSE)
                    bu, buk = pbank(DENSE)
                    for k in range(8):
                        P.I('pe', 'matmul', PS[bg].v(0, [[1, w]]), wb.v(k * 256, [[1, 128]]), H.v(k * NT + t0, [[1, w]]),
                            start=(k == 0), stop=(k == 7), reads=[wk, 'H'], writes=[bgk])
                    for k in range(8):
                        P.I('pe', 'matmul', PS[bu].v(0, [[1, w]]), wb.v(k * 256 + 128, [[1, 128]]), H.v(k * NT + t0, [[1, w]]),
                            start=(k == 0), stop=(k == 7), reads=[wk, 'H'], writes=[buk])
                    tq, tqk = tmpbuf(w)
                    P.I('act', 'activation', tq, PS[bg].v(0, [[1, w]]), AF.Silu, reads=[bgk], writes=[tqk])
                    P.I('dve', 'tensor_tensor', Fap(j, t0 - hc0, w), tq, PS[bu].v(0, [[1, w]]), op=ALU.mult,
                        reads=[tqk, buk], writes=['F'])
            for (t0, w) in htiles:
                for m in range(8):
                    wb, wk = wload('w_ffn_out', l * D_FF * 1024, 1024, m * 128, 128, 22)
                    b, bk = pbank(DENSE)
                    for j in range(22):
                        P.I('pe', 'matmul', PS[b].v(0, [[1, w]]), wb.v(j * 128, [[1, 128]]), Fap(j, t0 - hc0, w),
                            start=(j == 0), stop=(j == 21), reads=[wk, 'F'], writes=[bk])
                    evac(m, SCR32.v(MOoff + m * 512, [[1, w]]), PS[b].v(0, [[1, w]]), [bk], ['MO'])
                post_norm_add(MOap, t0, w, GFt, 9952)
            P.barrier()

    P.barrier()
    for ti in range(17):
        rows_ = 128 if ti < 16 else 64
        t0 = ti * 128
        so = (ti % 2) * 1024
        skey = 'YST%d' % (ti % 2)
        for c0 in (0, 4):
            b, bk = pbank((4, 5))
            for c in range(4):
                P.I('pe', 'transpose', PS[b].v(c * 128, [[1, 128]], pn=rows_), X.v((c0 + c) * NT + t0, [[1, rows_]]), IDF(128),
                    reads=['X', 'CST'], writes=[bk])
            evac(c0 // 4, SCR32.v(so + c0 * 128, [[1, 512]], pn=rows_), PS[b].v(0, [[1, 512]], pn=rows_), [bk], [skey + '_%d' % c0])
        if ti < 16:
            dst_ = dap(dr['y_p'], t0 * 1024, [[1024, 128], [1, 1024]])
        else:
            dst_ = dap(dr['y_s'], 0, [[1024, 64], [1, 1024]])
        P.dma('sp', dst_, SCR32.v(so, [[1, 1024]], pn=rows_), reads=[skey + '_0', skey + '_4'], writes=['OUTy'], out=True)

    P.finish()
    return nc


_CACHE = {}


def _get_prog():
    if 'nc' not in _CACHE:
        _CACHE['nc'] = build_program()
        _CACHE['consts'] = (make_consts(), make_sel())
    return _CACHE['nc'], _CACHE['consts']


def kernel(**inp):
    nc, consts = _get_prog()
    f = lambda a: np.ascontiguousarray(np.asarray(a, dtype=np.float32))
    shared = {}
    for name in ("w_mod", "b_mod", "g_pre_mix", "g_post_mix", "g_pre_ffn", "g_post_ffn", "w_in", "conv_w", "conv_b",
                 "conv_ln_g", "conv_ln_b", "w_conv_out", "ssm_log_dt", "w_ssm_glu", "w_att", "w_out", "w_ffn_in", "w_ffn_out"):
        shared[name] = f(inp[name])
    for name in ("ssm_a_re", "ssm_a_im", "ssm_b_re", "ssm_b_im", "ssm_c_re", "ssm_c_im", "ssm_d"):
        shared[name] = f(inp[name]).reshape(2, -1)
    shared["consts"] = consts[0]
    shared["selc"] = consts[1]
    in_maps = []
    for c in range(8):
        s = slice(c * 16, (c + 1) * 16)
        m = dict(shared)
        m["x_p"] = f(inp["x_prompt"][c])
        m["x_s"] = f(inp["x_sample"][s]).reshape(64, 1024)
        m["c_all"] = f(np.concatenate([np.asarray(inp["c_prompt"])[c:c + 1], np.asarray(inp["c_sample"])[s]], axis=0))
        for g in range(2):
            m["ck%d" % g] = f(np.asarray(inp["cache_k%d" % g])[:, s]).reshape(2, 16, -1, 256)
            m["cv%d" % g] = f(np.asarray(inp["cache_v%d" % g])[:, s]).reshape(2, 16, -1, 256)
        for nm, src in (("ck2", "cache_k2"), ("cv2", "cache_v2")):
            a = np.asarray(inp[src])[:, s].reshape(2, 16, 128, 16, 256)[:, :, :, 0:4]
            m[nm] = f(a.transpose(0, 1, 3, 2, 4)).reshape(2, 16, 512, 256)
        m["st_conv"] = f(np.asarray(inp["state_conv"])[:, s])
        m["st_re"] = f(np.asarray(inp["state_ssm_re"])[:, s]).reshape(2, 16, 2048)
        m["st_im"] = f(np.asarray(inp["state_ssm_im"])[:, s]).reshape(2, 16, 2048)
        in_maps.append(m)
    res = run_bass_kernel_spmd(nc, in_maps, core_ids=list(range(8)))
    R = res.results
    cat = lambda name, ax: np.concatenate([np.asarray(r[name]) for r in R], axis=ax)
    stk = lambda name: np.stack([np.asarray(r[name]) for r in R], axis=1)
    outs = []
    outs.append(np.stack([np.asarray(r["y_p"]) for r in R], axis=0).reshape(8, 2048, 1024))
    outs.append(cat("y_s", 0).reshape(128, 4, 1024))
    for g, keep in enumerate((128, 512, 2048)):
        outs.append(stk("k%d_p" % g).reshape(2, 8, keep, 4, 64))
        outs.append(stk("v%d_p" % g).reshape(2, 8, keep, 4, 64))
    outs.append(stk("conv_p").reshape(2, 8, 30, 512))
    outs.append(stk("ssm_re_p").reshape(2, 8, 32, 64))
    outs.append(stk("ssm_im_p").reshape(2, 8, 32, 64))
    for g in range(3):
        outs.append(cat("k%d_s" % g, 1).reshape(2, 128, 4, 4, 64))
        outs.append(cat("v%d_s" % g, 1).reshape(2, 128, 4, 4, 64))
    outs.append(cat("conv_s", 1).reshape(2, 128, 30, 512))
    outs.append(cat("ssm_re_s", 1).reshape(2, 128, 32, 64))
    outs.append(cat("ssm_im_s", 1).reshape(2, 128, 32, 64))
    return tuple(np.ascontiguousarray(o, dtype=np.float32) for o in outs)
```
